# Optimizing a Trainium2 kernel written in Bass

```python
import math
import jax, jax.numpy as jnp
from jax import lax
import numpy as np

D_MODEL = 1024
BATCH = 2
SEQ = 8192
DEPTH = 1

PLE_DIM = 256
HEAD_DIM = 64
DIFF_HEADS = 4
DIFF_QK = 2 * HEAD_DIM
DIFF_V = 2 * HEAD_DIM
DIFF_WIDTH = DIFF_HEADS * DIFF_V
DIL_HEADS = 8
DIL_WIDTH = DIL_HEADS * HEAD_DIM
DIL_PATTERNS = ((128, 1), (512, 4), (2048, 16))
MIX_WIDTH = DIFF_WIDTH + DIL_WIDTH
Q_BLOCK = 128
NORM_EPS = 1e-6
MASK_VALUE = -1e30

IN_SPLIT_SIZES = (
    DIFF_HEADS * DIFF_QK,
    DIFF_HEADS * DIFF_QK,
    DIFF_WIDTH,
    DIL_WIDTH,
    DIL_WIDTH,
    DIL_WIDTH,
    MIX_WIDTH,
)
IN_WIDTH = sum(IN_SPLIT_SIZES)
IN_SPLIT_IDX = tuple(int(v) for v in np.cumsum(IN_SPLIT_SIZES)[:-1])

kernel_name = "hymba_diff_dilated_alibi_ple_encoder"


def rms_norm(t, g):
    tf = t.astype(jnp.float32)
    y = tf * lax.rsqrt(jnp.mean(tf * tf, axis=-1, keepdims=True) + NORM_EPS)
    return (y * g.astype(jnp.float32)).astype(t.dtype)


def alibi_slopes(n_heads):
    return jnp.exp2(-8.0 * jnp.arange(1, n_heads + 1, dtype=jnp.float32) / n_heads)


def diff_attention(q, k, v, lam, slopes):
    B, S, H, _, Dh = q.shape
    scale = Dh ** -0.5
    nqb = S // Q_BLOCK
    pos = jnp.arange(S)
    qb = q.reshape(B, nqb, Q_BLOCK, H, 2, Dh).transpose(1, 0, 2, 3, 4, 5)
    qpos = pos.reshape(nqb, Q_BLOCK)

    def block(args):
        qblk, qp = args
        s = jnp.einsum('bqhce,bkhce->bchqk', qblk, k).astype(jnp.float32) * scale
        dist = jnp.abs(qp[:, None] - pos[None, :]).astype(jnp.float32)
        s = s - slopes[:, None, None] * dist
        a = jax.nn.softmax(s, axis=-1)
        w = a[:, 0] - lam * a[:, 1]
        return jnp.einsum('bhqk,bkhe->bqhe', w.astype(v.dtype), v)

    out = lax.map(block, (qb, qpos))
    return out.transpose(1, 0, 2, 3, 4).reshape(B, S, H, 2 * Dh)


def dilated_branch(q, k, v, slopes, window, dilation):
    B, S, H, Dh = q.shape
    r = window // (2 * dilation)
    blk = r
    unit = dilation * blk
    Lp = -(-S // unit) * unit
    n = Lp // dilation
    nb = n // blk
    scale = Dh ** -0.5

    def split(t):
        t = jnp.pad(t, ((0, 0), (0, Lp - S), (0, 0), (0, 0)))
        return t.reshape(B, nb, blk, dilation, H, Dh)

    def window3(t):
        tp = jnp.pad(t, ((0, 0), (1, 1), (0, 0), (0, 0), (0, 0), (0, 0)))
        return jnp.concatenate([tp[:, :-2], tp[:, 1:-1], tp[:, 2:]], axis=2)

    qs = split(q)
    kw = window3(split(k))
    vw = window3(split(v))

    qi = jnp.arange(nb)[:, None] * blk + jnp.arange(blk)[None, :]
    kj = (jnp.arange(nb)[:, None] - 1) * blk + jnp.arange(3 * blk)[None, :]
    rel = qi[:, :, None] - kj[:, None, :]
    kpos = kj[:, :, None] * dilation + jnp.arange(dilation)[None, None, :]
    key_ok = ((kj >= 0)[:, :, None] & (kpos < S)).transpose(0, 2, 1)
    mask = (jnp.abs(rel) <= r)[:, None, None] & key_ok[:, :, None, None, :]
    dist = (dilation * jnp.abs(rel)).astype(jnp.float32)[:, None, None]
    bias = -slopes[:, None, None] * dist

    s = jnp.einsum('bnqche,bnkche->bnchqk', qs, kw).astype(jnp.float32) * scale + bias
    s = jnp.where(mask, s, MASK_VALUE)
    m = jnp.max(s, axis=-1, keepdims=True)
    e = jnp.exp(s - m)
    l = jnp.sum(e, axis=-1, keepdims=True)
    o = jnp.einsum('bnchqk,bnkche->bnqche', e.astype(v.dtype), vw)
    inv_l = (1.0 / l[..., 0]).transpose(0, 1, 4, 2, 3)[..., None]
    o = (o.astype(jnp.float32) * inv_l).reshape(B, Lp, H, Dh)[:, :S]
    lse = (m[..., 0] + jnp.log(l[..., 0])).transpose(0, 1, 4, 2, 3).reshape(B, Lp, H)[:, :S]
    return o, lse


def dilated_attention(q, k, v, slopes):
    outs, lses = [], []
    for window, dilation in DIL_PATTERNS:
        o, lse = dilated_branch(q, k, v, slopes, window, dilation)
        outs.append(o)
        lses.append(lse)
    w = jax.nn.softmax(jnp.stack(lses, axis=0), axis=0)
    out = jnp.sum(w[..., None] * jnp.stack(outs, axis=0), axis=0)
    return out.astype(q.dtype)


def setup_inputs(seed: int = 0) -> dict:
    key = jax.random.key(seed)
    ks = jax.random.split(key, 20)
    f32 = jnp.float32

    def gain(k, n):
        return 1.0 + 0.02 * jax.random.normal(k, (DEPTH, n), f32)

    return {
        "x": jax.random.normal(ks[0], (BATCH, SEQ, D_MODEL), f32),
        "p": jax.random.normal(ks[1], (DEPTH, BATCH, SEQ, PLE_DIM), f32),
        "mix_norm_g": gain(ks[2], D_MODEL),
        "w_in": jax.random.normal(ks[3], (DEPTH, D_MODEL, IN_WIDTH), f32) * D_MODEL ** -0.5,
        "diff_q_norm_g": gain(ks[4], HEAD_DIM),
        "diff_k_norm_g": gain(ks[5], HEAD_DIM),
        "lambda_q1": 0.1 * jax.random.normal(ks[6], (DEPTH, HEAD_DIM), f32),
        "lambda_k1": 0.1 * jax.random.normal(ks[7], (DEPTH, HEAD_DIM), f32),
        "lambda_q2": 0.1 * jax.random.normal(ks[8], (DEPTH, HEAD_DIM), f32),
        "lambda_k2": 0.1 * jax.random.normal(ks[9], (DEPTH, HEAD_DIM), f32),
        "diff_sub_norm_g": gain(ks[10], DIFF_V),
        "dil_q_norm_g": gain(ks[11], HEAD_DIM),
        "dil_k_norm_g": gain(ks[12], HEAD_DIM),
        "w_out": jax.random.normal(ks[13], (DEPTH, MIX_WIDTH, D_MODEL), f32) * MIX_WIDTH ** -0.5,
        "ple_norm_g": gain(ks[14], D_MODEL),
        "w_ple_gate": jax.random.normal(ks[15], (DEPTH, D_MODEL, D_MODEL), f32) * D_MODEL ** -0.5,
        "w_ple_proj": jax.random.normal(ks[16], (DEPTH, PLE_DIM, D_MODEL), f32) * PLE_DIM ** -0.5,
    }


def reference(x, p, mix_norm_g, w_in, diff_q_norm_g, diff_k_norm_g,
              lambda_q1, lambda_k1, lambda_q2, lambda_k2, diff_sub_norm_g,
              dil_q_norm_g, dil_k_norm_g, w_out, ple_norm_g, w_ple_gate, w_ple_proj):
    B, S, _ = x.shape
    diff_slopes = alibi_slopes(DIFF_HEADS)
    dil_slopes = alibi_slopes(DIL_HEADS)
    for i in range(DEPTH):
        lam_init = 0.8 - 0.6 * math.exp(-0.3 * i)
        h = rms_norm(x, mix_norm_g[i])
        u = h @ w_in[i]
        dq, dk, dv, bq, bk, bv, z = jnp.split(u, IN_SPLIT_IDX, axis=-1)

        dq = rms_norm(dq.reshape(B, S, DIFF_HEADS, 2, HEAD_DIM), diff_q_norm_g[i])
        dk = rms_norm(dk.reshape(B, S, DIFF_HEADS, 2, HEAD_DIM), diff_k_norm_g[i])
        dv = dv.reshape(B, S, DIFF_HEADS, DIFF_V)
        lam = (jnp.exp(jnp.sum(lambda_q1[i].astype(jnp.float32) * lambda_k1[i].astype(jnp.float32)))
               - jnp.exp(jnp.sum(lambda_q2[i].astype(jnp.float32) * lambda_k2[i].astype(jnp.float32)))
               + lam_init)
        a = diff_attention(dq, dk, dv, lam, diff_slopes)
        a = rms_norm(a, diff_sub_norm_g[i]) * (1.0 - lam_init)
        a = a.reshape(B, S, DIFF_WIDTH)

        bq = rms_norm(bq.reshape(B, S, DIL_HEADS, HEAD_DIM), dil_q_norm_g[i])
        bk = rms_norm(bk.reshape(B, S, DIL_HEADS, HEAD_DIM), dil_k_norm_g[i])
        bv = bv.reshape(B, S, DIL_HEADS, HEAD_DIM)
        b = dilated_attention(bq, bk, bv, dil_slopes).reshape(B, S, DIL_WIDTH)

        y = jnp.concatenate([a, b], axis=-1) * jax.nn.silu(z)
        x = x + y @ w_out[i]

        gate = jax.nn.sigmoid(rms_norm(x, ple_norm_g[i]) @ w_ple_gate[i])
        x = x + gate * (p[i] @ w_ple_proj[i])
    return x
```

```python
import numpy as np
import ml_dtypes
from contextlib import ExitStack
import concourse.bass as bass
import concourse.mybir as mybir
from concourse.bass_utils import run_bass_kernel_spmd

F32 = mybir.dt.float32
BF16 = mybir.dt.bfloat16
I32 = mybir.dt.int32
AF = mybir.ActivationFunctionType
ALU = mybir.AluOpType
AX = mybir.AxisListType

S_LEN = 8192
DM = 1024
NTILE = 64
NQT = 16
EPS = 1e-6
LAM_INIT = 0.8 - 0.6 * 1.0
ENGS = ("pe", "act", "dve", "pool", "sp")
SB_BASE = 16384

_DEBUG = {}


class Prog:
    def __init__(self, nc, stack):
        self.nc = nc
        self.stack = stack
        self.q = {e: [] for e in ENGS}
        self.esem = {e: stack.enter_context(nc.semaphore(f"es_{e}")) for e in ENGS}
        self.ecnt = {e: 0 for e in ENGS}
        self.waited = {e: {} for e in ENGS}
        self.nsem = 0
        self.ntens = 0

    def dsem(self, name=None):
        self.nsem += 1
        s = self.stack.enter_context(self.nc.semaphore(name or f"ds{self.nsem}"))
        return [s, 0]

    def sb(self, shape, dtype, off):
        self.ntens += 1
        h = self.nc.alloc_sbuf_tensor_at(f"sb{self.ntens}", list(shape), dtype, offset=SB_BASE + off)
        return h.ap()

    def _flat(self, deps):
        out = []
        for d in deps:
            if d is None:
                continue
            if isinstance(d, tuple) and len(d) == 2 and not isinstance(d[0], (tuple, list)):
                out.append(d)
            else:
                out.extend(self._flat(d))
        return out

    def _wait(self, eng, tok):
        sem, val = tok
        key = id(sem)
        if self.waited[eng].get(key, 0) >= val:
            return
        self.waited[eng][key] = val
        self.q[eng].append(lambda E, sem=sem, val=val: E.wait_ge(sem, val))

    def op(self, eng, fn, deps=(), pre=None):
        if pre is not None:
            self.q[eng].append(pre)
        for d in self._flat(deps):
            self._wait(eng, d)
        self.ecnt[eng] += 1
        n = self.ecnt[eng]
        sem = self.esem[eng]
        self.q[eng].append(lambda E, fn=fn, sem=sem: fn(E).then_inc(sem, 1))
        return (sem, n)

    def dma(self, eng, fn, ds, deps=()):
        for d in self._flat(deps):
            self._wait(eng, d)
        ds[1] += 16
        self.q[eng].append(lambda E, fn=fn, s=ds[0]: fn(E).then_inc(s, 16))
        return (ds[0], ds[1])

    def wait(self, eng, deps):
        for d in self._flat(deps):
            self._wait(eng, d)

    def replay(self, block):
        q = self.q

        @block.tensor
        def _(E):
            for f in q["pe"]:
                f(E)

        @block.scalar
        def _(E):
            for f in q["act"]:
                f(E)

        @block.vector
        def _(E):
            for f in q["dve"]:
                f(E)

        @block.gpsimd
        def _(E):
            for f in q["pool"]:
                f(E)

        @block.sync
        def _(E):
            for f in q["sp"]:
                f(E)


def DMA(out, in_):
    return lambda E: E.dma_start(out=out, in_=in_)


def build_program(dbg=None):
    dbg = dbg or {}
    stop = dbg.get("stop")
    nqt_dbg = dbg.get("nqt", NQT)
    nc = bass.Bass("TRN2", target_bir_lowering=False)

    def din(name, shape, dt=F32):
        return nc.dram_tensor(name, list(shape), dt, kind="ExternalInput").ap()

    def dout(name, shape, dt=F32):
        return nc.dram_tensor(name, list(shape), dt, kind="ExternalOutput").ap()

    xb = din("xb", [S_LEN, DM])
    xo_d = din("xo", [2048, DM])
    pT_d = din("pT", [256, 2048])
    w_in = din("w_in", [DM, 1024])
    w_out = din("w_out", [1024, DM])
    w_pg = din("w_pg", [DM, DM])
    w_pp = din("w_pp", [256, DM])
    gmix_d = din("gmix", [128, 8])
    gple_d = din("gple", [128, 8])
    gqk_d = din("gqk", [128, 512])
    lamv_d = din("lamv", [128, 256])
    gsub_d = din("gsub", [128, 1])
    nsl_d = din("nsl", [128, 3])
    tabDd_d = din("tabDd", [128, 1152])
    tabDl_d = din("tabDl", [128, 2944])
    tabMl_d = din("tabMl", [128, 2944])
    mcol_d = din("mcol", [128, 64])
    ident_d = din("ident", [128, 128], BF16)
    idx_d = din("idx", [128, 8], I32)
    out_d = dout("out", [2048, DM])
    ybin = nc.dram_tensor("ybin", [4, 256, 2048], BF16).ap()
    ygat = nc.dram_tensor("ygat", [4096, 2048], BF16).ap()
    if stop == "p1":
        dbg_qk = dout("dbg_qk", [128, 4 * S_LEN], BF16)
        dbg_vd = dout("dbg_vd", [128, 64 * 128], BF16)
        dbg_vl = dout("dbg_vl", [128, 64 * 128], BF16)
        dbg_zt = dout("dbg_zt", [128, 2 * S_LEN], BF16)
    if stop in ("dil", "diff"):
        dbg_y = dout("dbg_y", [4 * 256, 2048], BF16)
    if dbg.get("dump_x1"):
        dbg_x1 = dout("dbg_x1", [2048, DM])
        dbg_h2t = dout("dbg_h2t", [16 * 128, DM], BF16)

    with ExitStack() as st:
        P = Prog(nc, st)
        QK = P.sb([128, 4, S_LEN], BF16, 0)
        Vd = P.sb([128, 64, 128], BF16, 65536)
        Vl = P.sb([128, 64, 128], BF16, 81920)
        zT = P.sb([128, 2, S_LEN], BF16, 98304)
        C0 = 131072
        identb = P.sb([128, 128], BF16, C0)
        ones32 = P.sb([128, 128], F32, C0 + 256)
        onesb = P.sb([128, 128], BF16, C0 + 768)
        cb = P.sb([128, 64], F32, C0 + 1024)
        gmix32 = P.sb([128, 8], F32, C0 + 1280)
        gple32 = P.sb([128, 8], F32, C0 + 1408)
        nsl = P.sb([128, 3], F32, C0 + 1536)
        nlam = P.sb([128, 1], F32, C0 + 1664)
        gsubs = P.sb([128, 1], F32, C0 + 1792)
        idxs = P.sb([128, 8], I32, C0 + 1920)
        sc = P.sb([128, 1536], F32, 204800)

        def SC(slot, w=1):
            return sc[:, 32 * slot:32 * slot + w]
        gqk8 = P.sb([128, 512], F32, C0 + 2048)
        epsc = P.sb([128, 8], F32, 211968)
        sel = P.sb([64, 2, 128], F32, 210944)
        W0 = C0 + 4096
        Wi = P.sb([128, 8, 1024], BF16, W0)
        xs = [P.sb([128, 1024], F32, W0 + 16384 + 4096 * i) for i in range(4)]
        junk = P.sb([128, 1024], BF16, W0 + 32768)
        hb = [P.sb([128, 1024], BF16, W0 + 34816 + 2048 * i) for i in range(2)]
        hT = [P.sb([128, 8, 512], BF16, W0 + 38912 + 8192 * i) for i in range(2)]
        usb = [P.sb([128, 512], F32, W0 + 55296 + 2048 * i) for i in range(3)]
        sqb = P.sb([128, 512], F32, W0 + 61440)
        tmpb = P.sb([128, 512], F32, W0 + 63488)
        qn = [P.sb([128, 512], BF16, W0 + 65536 + 1024 * i) for i in range(2)]
        lamtmp = P.sb([128, 128], F32, W0 + 67584)
        Etab = P.sb([128, 1152], F32, W0)
        Gtab = [P.sb([128, 2944], F32, W0 + 4608 + 11776 * i) for i in range(2)]
        Mtmp = P.sb([128, 2944], F32, W0 + 28160)
        P32p = [P.sb([128, 2, 512], BF16, W0 + 44032 + 2048 * i) for i in range(2)]
        P32 = [[P32p[i][:, s, :] for s in range(2)] for i in range(2)]
        EtabB = P.sb([128, 1152], BF16, W0 + 28160)
        GtabB = [P.sb([128, 2944], BF16, W0 + 30464 + 5888 * i) for i in range(2)]
        Pb = [[P.sb([128, 512], BF16, W0 + 48128 + 1024 * (2 * i + s)) for s in range(2)] for i in range(3)]
        ET = [P.sb([128, 512], F32, W0 + 54272 + 2048 * i) for i in range(6)]
        yo = [P.sb([128, 512], BF16, W0 + 66560 + 1024 * i) for i in range(2)]
        Wo = P.sb([128, 8, 1024], BF16, 32768)
        Wg = P.sb([128, 8, 1024], BF16, 49152)
        Wp = P.sb([128, 2, 1024], BF16, 81920)
        wst = [P.sb([128, 1024], F32, 81920 + 4096 + 4096 * i) for i in range(2)]
        yTa = P.sb([128, 8, 2048], BF16, 0)
        pTb = P.sb([128, 2, 2048], BF16, 65536)
        xo = [P.sb([128, 1024], F32, 98304 + 4096 * i) for i in range(2)]
        x1 = [P.sb([128, 1024], F32, 98304 + 8192 + 4096 * i) for i in range(3)]
        osb = [P.sb([128, 1024], F32, 98304 + 20480 + 4096 * i) for i in range(2)]
        gate = P.sb([128, 1024], F32, 98304 + 28672)
        h2 = [P.sb([128, 1024], BF16, W0 + 2048 * i) for i in range(2)]
        h2T = [P.sb([128, 8, 128], BF16, W0 + 4096 + 2048 * i) for i in range(2)]
        junk3 = P.sb([128, 1024], BF16, W0 + 8192)
        pst = [P.sb([128, 1024], F32, W0 + 10240 + 4096 * i) for i in range(2)]

        bpair = [nc.alloc_psum_tensor(f"bpair{i}", [128, 1024], F32).ap() for i in range(4)]
        bpair_bf = [b.bitcast(BF16) for b in bpair]
        banks = [bpair[i // 2][:, (i % 2) * 512:(i % 2 + 1) * 512] for i in range(8)]
        banks_bf = [bpair_bf[i // 2][:, (i % 2) * 1024:(i % 2 + 1) * 1024] for i in range(8)]

        ld = P.dsem("ld_const")
        tc = {}
        for name, dst, src in (("ident", identb, ident_d), ("gmix", gmix32, gmix_d), ("gple", gple32, gple_d),
                               ("gqk", gqk8, gqk_d), ("lamv", lamtmp[:, 0:128], lamv_d[:, 0:128]),
                               ("lamv2", tmpb[:, 0:128], lamv_d[:, 128:256]),
                               ("gsub", gsubs, gsub_d), ("nsl", nsl, nsl_d), ("mcol", cb, mcol_d), ("idx", idxs, idx_d)):
            tc[name] = P.dma("sp", DMA(dst, src), ld)
        for name in list(tc):
            tc[name] = (ld[0], ld[1])
        t_ones32 = P.op("pool", lambda E: E.memset(ones32[:, :], 1.0))
        t_epsc = P.op("pool", lambda E: E.memset(epsc[:, :], float(128 * EPS)))
        P.op("pool", lambda E: E.memset(sel[:, :, :], 0.0))
        P.op("pool", lambda E: E.memset(sel[0:1, 0, :], 1.0))
        t_sel = P.op("pool", lambda E: E.memset(sel[32:33, 1, :], 1.0))
        t_onesb = P.op("pool", lambda E: E.memset(onesb[:, :], 1.0))
        t_gmix = P.op("dve", lambda E: E.tensor_scalar(out=gmix32[:, :], in0=gmix32[:, :], scalar1=32.0, scalar2=None, op0=ALU.mult), [tc["gmix"]])
        t_gple = P.op("dve", lambda E: E.tensor_scalar(out=gple32[:, :], in0=gple32[:, :], scalar1=32.0, scalar2=None, op0=ALU.mult), [tc["gple"]])
        t_gqk = P.op("dve", lambda E: E.tensor_scalar(out=gqk8[:, :], in0=gqk8[:, :], scalar1=8.0, scalar2=None, op0=ALU.mult), [tc["gqk"]])
        t_cb = P.op("dve", lambda E: E.tensor_scalar(out=cb[:, :], in0=cb[:, :], scalar1=nsl[:, 0:1], scalar2=None, op0=ALU.mult), [tc["mcol"], tc["nsl"]])
        t_gs = P.op("dve", lambda E: E.tensor_scalar(out=gsubs[:, :], in0=gsubs[:, :], scalar1=float((1.0 - LAM_INIT) * np.sqrt(128.0)), scalar2=None, op0=ALU.mult), [tc["gsub"]])
        t_l1 = P.op("dve", lambda E: E.tensor_tensor(out=lamtmp[:, 0:64], in0=lamtmp[:, 0:64], in1=lamtmp[:, 64:128], op=ALU.mult), [tc["lamv"]])
        t_l2 = P.op("dve", lambda E: E.tensor_tensor(out=tmpb[:, 0:64], in0=tmpb[:, 0:64], in1=tmpb[:, 64:128], op=ALU.mult), [tc["lamv2"]])
        t_l3 = P.op("dve", lambda E: E.tensor_reduce(out=SC(40), in_=lamtmp[:, 0:64], axis=AX.X, op=ALU.add), [t_l1])
        t_l4 = P.op("dve", lambda E: E.tensor_reduce(out=SC(41), in_=tmpb[:, 0:64], axis=AX.X, op=ALU.add), [t_l2])
        t_l5a = P.op("act", lambda E: E.activation(out=SC(42), in_=SC(40), func=AF.Exp), [t_l3])
        t_l5 = P.op("act", lambda E: E.activation(out=SC(43), in_=SC(41), func=AF.Exp), [t_l4])
        t_l6 = P.op("dve", lambda E: E.tensor_tensor(out=SC(44), in0=SC(43), in1=SC(42), op=ALU.subtract), [t_l5, t_l5a])
        t_nlam = P.op("dve", lambda E: E.tensor_scalar(out=nlam[:, :], in0=SC(44), scalar1=float(-LAM_INIT), scalar2=None, op0=ALU.add), [t_l6])

        wsem = [P.dsem("wst0"), P.dsem("wst1")]
        wi_tok = []
        cast_tok = [None, None]
        for k in range(8):
            sl = k % 2
            t_ld = P.dma("sp", DMA(xs[sl][:, :], w_in[k * 128:(k + 1) * 128, :]), wsem[sl], [cast_tok[sl]])
            cast_tok[sl] = P.op("act", lambda E, k=k, sl=sl: E.activation(out=Wi[:, k, :], in_=xs[sl][:, :], func=AF.Copy, scale=gmix32[:, k:k + 1]), [t_ld, t_gmix])
            wi_tok.append(cast_tok[sl])

        xsem = [P.dsem(f"xs{i}") for i in range(4)]
        hs_tok, tra_tok, hTe_tok, u_tok, ue_tok, vl_tok, tmp_tok, qn_tok, trq_tok, qke_tok = ({} for _ in range(10))
        rcp_tok, a8_tok = {}, {}
        z_tok = {}
        silu_tok = {0: {}, 1: {}}
        vd_tok = {}
        nt1 = dbg.get("ntile", NTILE)

        xl1_tok, sq1_tok, add1_tok, sqt1_tok, s8_tok = {}, {}, {}, {}, {}

        def f_load(t):
            sl4 = t % 4
            d0 = [hs_tok.get(t - 4)]
            if t < 2:
                d0.append(cast_tok[t % 2])
            xl1_tok[t] = P.dma("sp", DMA(xs[sl4][:, :], xb[t * 128:(t + 1) * 128, :]), xsem[sl4], d0)

        def f_a1(t):
            sl4 = t % 4
            c = 4 * sl4
            sq1_tok[t] = P.op("act", lambda E: E.activation(out=junk[:, :], in_=xs[sl4][:, :], func=AF.Square, accum_out=SC(c)), [xl1_tok[t]])

        def f_a1_add(t):
            c = 4 * (t % 4)
            add1_tok[t] = P.op("dve", lambda E: E.tensor_scalar(out=SC(c + 1), in0=SC(c), scalar1=float(DM * EPS), scalar2=None, op0=ALU.add), [sq1_tok[t]])

        def f_a1_sqrt(t):
            c = 4 * (t % 4)
            sqt1_tok[t] = P.op("act", lambda E: E.activation(out=SC(c + 2), in_=SC(c + 1), func=AF.Sqrt), [add1_tok[t]])

        def f_a1_rcp(t):
            c = 4 * (t % 4)
            rcp_tok[t] = P.op("dve", lambda E: E.reciprocal(out=SC(c + 3), in_=SC(c + 2)), [sqt1_tok[t]])

        def f_hs(t):
            sl, sl4 = t % 2, t % 4
            c = 4 * sl4
            hs_tok[t] = P.op("act", lambda E: E.activation(out=hb[sl][:, :], in_=xs[sl4][:, :], func=AF.Copy, scale=SC(c + 3)), [rcp_tok[t], tra_tok.get(t - 2)])

        def f_T(t):
            sl = t % 2
            tb = banks_bf[sl]
            for k in range(8):
                tk = P.op("pe", lambda E, k=k: E.transpose(out=tb[:, k * 128:(k + 1) * 128], in_=hb[sl][:, k * 128:(k + 1) * 128], identity=identb[:, :]),
                          [hs_tok[t], hTe_tok.get(t - 2), tc["ident"]])
            tra_tok[t] = tk

        def f_hTe(t):
            G, sub = divmod(t, 4)
            sl = t % 2
            tb = banks_bf[sl]
            dfree = []
            if G >= 2 and sub == 0:
                dfree = [u_tok[4 * (G - 2) + 3], z_tok[G - 2]]
            hTe_tok[t] = P.op("dve", lambda E: E.tensor_copy(out=hT[G % 2][:, :, sub * 128:(sub + 1) * 128],
                                                             in_=tb[:, :].rearrange("p (k t) -> p k t", k=8)), [tra_tok[t], dfree])

        def f_U(t):
            G, sub = divmod(t, 4)
            sl = t % 2
            U0 = banks[2 + sl]
            U1 = banks[4 + sl]
            for k in range(8):
                tk = P.op("pe", lambda E, k=k: E.matmul(U0[:, :], lhsT=hT[G % 2][:, k, sub * 128:(sub + 1) * 128], rhs=Wi[:, k, 0:512], start=(k == 0), stop=(k == 7)),
                          [hTe_tok[t], wi_tok, ue_tok.get(t - 2)])
            for k in range(8):
                tk = P.op("pe", lambda E, k=k: E.matmul(U1[:, 0:256], lhsT=hT[G % 2][:, k, sub * 128:(sub + 1) * 128], rhs=Wi[:, k, 512:768], start=(k == 0), stop=(k == 7)),
                          [vl_tok.get(t - 2)])
            u_tok[t] = tk
            if sub == 3:
                f_z(t, 0)

        def f_z(t, zi):
            G = t // 4
            Z = banks[6]
            for k in range(8):
                tk = P.op("pe", lambda E, k=k: E.matmul(Z[:, :], lhsT=Wi[:, k, 768 + zi * 128:896 + zi * 128], rhs=hT[G % 2][:, k, :], start=(k == 0), stop=(k == 7)),
                          [silu_tok[1].get(G - 1) if zi == 0 else silu_tok[0][G], hTe_tok[t]])
            z_tok[(G, zi)] = tk
            if zi == 1:
                z_tok[G] = tk

        def f_silu(t, zi):
            G = t // 4
            Z = banks[6]
            silu_tok[zi][G] = P.op("act", lambda E: E.activation(out=zT[:, zi, G * 512:(G + 1) * 512], in_=Z[:, :], func=AF.Silu), [z_tok[(G, zi)]])

        def f_z1(t):
            if t % 4 == 3:
                f_z(t, 1)
                f_silu(t, 1)

        def f_ue(t):
            G, sub = divmod(t, 4)
            sl, s3 = t % 2, t % 3
            U0 = banks[2 + sl]
            U1 = banks[4 + sl]
            ue_tok[t] = P.op("act", lambda E: E.activation(out=usb[s3][:, :], in_=U0[:, :], func=AF.Copy), [u_tok[t], tmp_tok.get(t - 3)])
            vd_tok[t] = P.op("act", lambda E: E.activation(out=Vd[:, t, :], in_=U1[:, 0:128], func=AF.Copy), [u_tok[t]])
            vl_tok[t] = P.op("act", lambda E: E.activation(out=Vl[:, t, :], in_=U1[:, 128:256], func=AF.Copy), [u_tok[t]])
            if sub == 3:
                f_silu(t, 0)

        def f_sq(t):
            s3 = t % 3
            c = 16 + 4 * (t % 4)
            t1 = P.op("dve", lambda E: E.tensor_tensor(out=sqb[:, :], in0=usb[s3][:, :], in1=usb[s3][:, :], op=ALU.mult), [ue_tok[t]])
            t2 = P.op("dve", lambda E: E.tensor_reduce(out=SC(c, 8), in_=sqb[:, :].rearrange("p (g d) -> p g d", g=8), axis=AX.X, op=ALU.add), [t1])
            a8_tok[t] = P.op("dve", lambda E: E.tensor_scalar(out=SC(c + 1, 8), in0=SC(c, 8), scalar1=float(64 * EPS), scalar2=None, op0=ALU.add), [t2])

        def f_sqrt8(t):
            c = 16 + 4 * (t % 4)
            s8_tok[t] = P.op("act", lambda E: E.activation(out=SC(c + 2, 8), in_=SC(c + 1, 8), func=AF.Sqrt), [a8_tok[t]])

        def f_b2(t):
            sl, s3 = t % 2, t % 3
            c = 16 + 4 * (t % 4)
            t5 = P.op("dve", lambda E: E.reciprocal(out=SC(c + 3, 8), in_=SC(c + 2, 8)), [s8_tok[t]])
            tmp_tok[t] = P.op("dve", lambda E: E.tensor_tensor(out=tmpb[:, :].rearrange("p (g d) -> p g d", g=8), in0=usb[s3][:, :].rearrange("p (g d) -> p g d", g=8),
                                                               in1=SC(c + 3, 8).unsqueeze(2).broadcast_to([128, 8, 64]), op=ALU.mult), [t5])
            qn_tok[t] = P.op("dve", lambda E: E.tensor_tensor(out=qn[sl][:, :], in0=tmpb[:, :], in1=gqk8[:, :], op=ALU.mult), [tmp_tok[t], trq_tok.get(t - 2), t_gqk])

        def f_Tq(t):
            sl = t % 2
            qb = banks_bf[7]
            for cidx in range(4):
                tk = P.op("pe", lambda E, cidx=cidx: E.transpose(out=qb[:, cidx * 128:(cidx + 1) * 128], in_=qn[sl][:, cidx * 128:(cidx + 1) * 128], identity=identb[:, :]),
                          [qn_tok[t], qke_tok.get(t - 1)])
            trq_tok[t] = tk

        def f_qke(t):
            qb = banks_bf[7]
            qke_tok[t] = P.op("dve", lambda E: E.tensor_copy(out=QK[:, :, t * 128:(t + 1) * 128], in_=qb[:, 0:512].rearrange("p (c t) -> p c t", c=4)), [trq_tok[t]])

        def emit_step(s):
            ok = lambda t: 0 <= t < nt1
            for f, t in ((f_U, s - 3), (f_Tq, s - 6), (f_a1, s), (f_hs, s - 1), (f_T, s - 1), (f_sqrt8, s - 5), (f_hTe, s - 2), (f_sq, s - 4),
                         (f_a1_add, s), (f_a1_sqrt, s), (f_b2, s - 5), (f_a1_rcp, s), (f_ue, s - 3), (f_z1, s - 3), (f_qke, s - 6), (f_load, s + 1)):
                if ok(t):
                    f(t)

        f_load(0)
        for step in range(nt1 + 7):
            emit_step(step)

        p1_done = [qke_tok[nt1 - 1], qke_tok[nt1 - 2], vd_tok[nt1 - 1], vl_tok[nt1 - 1], silu_tok[0][(nt1 - 1) // 4], silu_tok[1][(nt1 - 1) // 4], z_tok[(nt1 - 1) // 4], u_tok[nt1 - 1], trq_tok[nt1 - 1]]

        fin = []
        stq = P.dsem("st_out")
        if stop == "p1":
            P.wait("sp", p1_done)
            for a in range(4):
                fin.append(P.dma("sp", DMA(dbg_qk[:, a * S_LEN:(a + 1) * S_LEN], QK[:, a, :]), stq, p1_done))
            for a in range(4):
                fin.append(P.dma("sp", DMA(dbg_vd[:, a * 2048:(a + 1) * 2048], Vd[:, a * 16:(a + 1) * 16, :].rearrange("p a t -> p (a t)")), stq))
                fin.append(P.dma("sp", DMA(dbg_vl[:, a * 2048:(a + 1) * 2048], Vl[:, a * 16:(a + 1) * 16, :].rearrange("p a t -> p (a t)")), stq))
            for a in range(2):
                fin.append(P.dma("sp", DMA(dbg_zt[:, a * S_LEN:(a + 1) * S_LEN], zT[:, a, :]), stq))

        ysem = [[P.dsem(f"ybin{j}_{p}") for p in range(2)] for j in range(4)]
        ccsem = P.dsem("cc")
        if stop != "p1":
            tsem = P.dsem("tabs")
            t_e = P.dma("sp", DMA(Etab[:, :], tabDd_d[:, :]), tsem, p1_done)
            t_g0 = P.dma("sp", DMA(Gtab[0][:, :], tabDl_d[:, :]), tsem)
            t_g1 = P.dma("sp", DMA(Gtab[1][:, :], tabDl_d[:, :]), tsem)
            t_m = P.dma("sp", DMA(Mtmp[:, :], tabMl_d[:, :]), tsem)
            tabs_ld = [t_e, t_g0, t_g1, t_m]
            tG = []
            for i in range(2):
                ta = P.op("act", lambda E, i=i: E.activation(out=Gtab[i][:, :], in_=Gtab[i][:, :], func=AF.Exp, scale=nsl[:, 1 + i:2 + i]), [tabs_ld, tc["nsl"]])
                tG.append(P.op("dve", lambda E, i=i: E.tensor_tensor(out=Gtab[i][:, :], in0=Gtab[i][:, :], in1=Mtmp[:, :], op=ALU.mult), [ta]))
            tE = P.op("act", lambda E: E.activation(out=Etab[:, :], in_=Etab[:, :], func=AF.Exp, scale=nsl[:, 0:1]), [tabs_ld])
            tGb = [P.op("dve", lambda E, i=i: E.tensor_copy(out=GtabB[i][:, :], in_=Gtab[i][:, :]), [tG]) for i in range(2)]
            tEb = P.op("dve", lambda E: E.tensor_copy(out=EtabB[:, :], in_=Etab[:, :]), [tE, tG])
            tabs_ready = [tGb, tEb, t_cb]

            SB = [[banks[0], banks[1]], [banks[2], banks[3]]]

            def attention(kind, extra=None):
                diff = kind == "diff"
                steps = []
                for q in range(nqt_dbg):
                    kb0 = q * 4
                    kbs = list(range(64)) if diff else [kb for kb in range(kb0 - 8, kb0 + 12) if 0 <= kb < 64]
                    for i, kb in enumerate(kbs):
                        steps.append((q, kb, i == 0, i == len(kbs) - 1))
                N = len(steps)
                exp_tok = {}
                mul_tok = {}
                av_tok = {}
                st8 = {"epi_free": None, "ss_free": None, "bk7_free": None}
                BK6, BK7 = banks[6], banks[7]
                if diff:
                    OB = [banks[4], banks[5]]
                    LB = [banks[6], banks[6]]
                else:
                    OB = [banks[4], banks[4]]
                    LB = [banks[5], banks[5]]
                qi = 0 if diff else 2
                ki = 1 if diff else 3

                def front(n):
                    q, kb, first, last = steps[n]
                    i0 = q * 512
                    j0 = kb * 128
                    par = n % 2
                    if diff:
                        if j0 + 128 <= i0:
                            off = 640
                            bias = cb[:, (i0 - j0 - 128) // 128:(i0 - j0 - 128) // 128 + 1]
                        elif j0 >= i0 + 512:
                            off = 0
                            bias = cb[:, (j0 - i0 - 512) // 128:(j0 - i0 - 512) // 128 + 1]
                        else:
                            off = 512 - (j0 - i0)
                            bias = cb[:, 0:1]
                    else:
                        off = 1408 - (j0 - i0)
                        bias = None
                    exp_tok[n] = []
                    mul_tok[n] = []
                    tss = []
                    for s in range(2):
                        rows = slice(64 * s, 64 * s + 64)
                        Sb = SB[par][s]
                        prev = exp_tok[n - 2][s] if n >= 2 else None
                        tss.append(P.op("pe", lambda E, Sb=Sb, rows=rows: E.matmul(Sb[:, :], lhsT=QK[rows, ki, j0:j0 + 128], rhs=QK[rows, qi, i0:i0 + 512], start=True, stop=True),
                                        [prev, p1_done if n < 2 else None]))
                    pms = mul_tok[n - 2] if n >= 2 else None
                    if diff:
                        for s in range(2):
                            Sb, p32 = SB[par][s], P32[par][s]
                            te = P.op("act", lambda E, Sb=Sb, p32=p32: E.activation(out=p32[:, :], in_=Sb[:, :], func=AF.Exp, bias=bias, scale=0.125),
                                      [tss[s], pms[s] if pms else None, tabs_ready if n < 2 else None])
                            exp_tok[n].append(te)
                    else:
                        s_in = bpair[par][:, :].rearrange("p (s q) -> p s q", s=2)
                        p32p = P32p[par]
                        te = P.op("act", lambda E: E.activation(out=p32p[:, :, :], in_=s_in, func=AF.Exp, scale=0.125), [tss, pms, tabs_ready if n < 2 else None])
                        exp_tok[n] = [te, te]
                    for s in range(2):
                        p32 = P32[par][s]
                        tab = EtabB if diff else GtabB[s]
                        pb = Pb[n % 3][s]
                        tm = P.op("dve", lambda E, p32=p32, pb=pb, tab=tab: E.tensor_tensor(out=pb[:, :], in0=p32[:, :], in1=tab[:, off:off + 512], op=ALU.mult),
                                  [exp_tok[n][s], av_tok.get(n - 3), tabs_ready if n < 3 else None])
                        mul_tok[n].append(tm)

                pending = []

                def run_pending():
                    for stages in list(pending):
                        stages.pop(0)()
                        if not stages:
                            pending.remove(stages)

                def back(n):
                    q, kb, first, last = steps[n]
                    i0 = q * 512
                    run_pending()
                    deps0 = [st8["epi_free"]] if first else []
                    for s in range(2):
                        pb = Pb[n % 3][s]
                        if diff:
                            P.op("pe", lambda E, pb=pb, s=s: E.matmul(OB[s][:, :], lhsT=Vd[:, kb, :], rhs=pb[:, :], start=first, stop=last), [mul_tok[n][s], deps0],
                                 pre=(lambda E: E.ldweights(Vd[:, kb, :])) if s == 0 else None)
                        else:
                            rows = slice(64 * s, 64 * s + 64)
                            P.op("pe", lambda E, pb=pb, s=s, rows=rows: E.matmul(OB[s][rows, :], lhsT=Vl[:, kb, 64 * s:64 * s + 64], rhs=pb[:, :], start=first, stop=last, tile_position=(0, 64 * s)),
                                 [mul_tok[n][s], deps0])
                    for s in range(2):
                        pb = Pb[n % 3][s]
                        if diff:
                            tk = P.op("pe", lambda E, pb=pb, s=s: E.matmul(BK6[32 * s:32 * s + 32, :], lhsT=onesb[:, 0:32], rhs=pb[:, :], start=first, stop=last, tile_position=(0, 32 * s)),
                                      [st8["ss_free"] if first else None, t_onesb])
                        else:
                            rows = slice(64 * s, 64 * s + 64)
                            tk = P.op("pe", lambda E, pb=pb, s=s, rows=rows: E.matmul(LB[s][rows, :], lhsT=onesb[:, 0:64], rhs=pb[:, :], start=first, stop=last, tile_position=(0, 64 * s)), [t_onesb])
                    av_tok[n] = tk
                    if not last:
                        return
                    j, qq = divmod(q, 4)
                    y = yo[q % 2]

                    def finish_store(ey, rows):
                        st8[("ydma", q % 2)] = P.dma("sp", DMA(ybin[j, rows, qq * 512:(qq + 1) * 512], y[:, :]), ysem[j][q % 2], [ey])
                        if diff and qq == 3:
                            P.wait("pool", [(ysem[j][0][0], ysem[j][0][1]), (ysem[j][1][0], ysem[j][1][1])])
                            ccsem[1] += 1
                            P.q["pool"].append(lambda E, j=j: E.collective_compute(
                                "AllGather", ALU.bypass, replica_groups=[[0, 1, 2, 3], [4, 5, 6, 7]],
                                ins=[ybin[j].opt()], outs=[ygat[j * 1024:(j + 1) * 1024, :].opt()]).then_inc(ccsem[0], 1))

                    if diff:
                        Rs, T0, R0s, R1s, T1, A = ET
                        ycur = y
                        e1p = []

                        def rec1(c):
                            e1p.append(P.op("dve", lambda E: E.reciprocal(out=Rs[0:64, c * 128:(c + 1) * 128], in_=BK6[0:64, c * 128:(c + 1) * 128]), [tk, st8.get("epi_done")]))

                        e1 = P.op("dve", lambda E: E.tensor_copy(out=A[0:64, :], in_=BK6[0:64, :]), [tk, st8.get("epi_done")])
                        st8["ss_free"] = e1
                        o0 = P.op("act", lambda E: E.activation(out=T0[:, :], in_=OB[0][:, :], func=AF.Copy), [tk, st8.get("epi_done")])
                        o1 = P.op("act", lambda E: E.activation(out=T1[:, :], in_=OB[1][:, :], func=AF.Copy), [tk])
                        st8["epi_free"] = [o0, o1]
                        ctx = {}

                        def r1(c):
                            def f():
                                e1p.append(P.op("dve", lambda E: E.reciprocal(out=Rs[0:64, c * 128:(c + 1) * 128], in_=A[0:64, c * 128:(c + 1) * 128]), [e1]))
                            return f

                        def s1():
                            ctx["b0"] = P.op("pe", lambda E: E.matmul(BK7[:, :], lhsT=sel[0:64, 0, :], rhs=Rs[0:64, :], start=True, stop=True), [e1p, st8["bk7_free"], t_sel])

                        def s2():
                            ctx["c0"] = P.op("act", lambda E: E.activation(out=R0s[:, :], in_=BK7[:, :], func=AF.Copy), [ctx["b0"]])

                        def s3():
                            ctx["b1"] = P.op("pe", lambda E: E.matmul(BK7[:, :], lhsT=sel[0:64, 1, :], rhs=Rs[0:64, :], start=True, stop=True), [ctx["c0"]])

                        def s4():
                            ctx["c1"] = P.op("act", lambda E: E.activation(out=R1s[:, :], in_=BK7[:, :], func=AF.Copy), [ctx["b1"]])

                        def s5():
                            e2 = P.op("dve", lambda E: E.tensor_tensor(out=T0[:, :], in0=T0[:, :], in1=R0s[:, :], op=ALU.mult), [ctx["c0"], o0])
                            e4 = P.op("dve", lambda E: E.tensor_tensor(out=T1[:, :], in0=T1[:, :], in1=R1s[:, :], op=ALU.mult), [ctx["c1"], o1])
                            ctx["e5"] = P.op("dve", lambda E: E.scalar_tensor_tensor(out=A[:, :], in0=T1[:, :], scalar=nlam[:, 0:1], in1=T0[:, :], op0=ALU.mult, op1=ALU.add), [e2, e4, t_nlam])

                        def s6():
                            ctx["e6"] = P.op("act", lambda E: E.activation(out=R0s[:, :], in_=A[:, :], func=AF.Square), [ctx["e5"]])

                        def s7():
                            ctx["e7"] = P.op("pe", lambda E: E.matmul(BK7[:, :], lhsT=ones32[:, :], rhs=R0s[:, :], start=True, stop=True), [ctx["e6"], ctx["c1"], t_ones32])

                        def s8():
                            ctx["e8"] = P.op("act", lambda E: E.activation(out=Rs[:, :], in_=BK7[:, :], func=AF.Ln, bias=epsc[:, 0:1], scale=1.0), [ctx["e7"], t_epsc])
                            st8["bk7_free"] = ctx["e8"]

                        def s9():
                            ctx["e9"] = P.op("act", lambda E: E.activation(out=T0[:, :], in_=Rs[:, :], func=AF.Exp, scale=-0.5), [ctx["e8"]])

                        def s10():
                            e11 = P.op("dve", lambda E: E.tensor_tensor(out=T1[:, :], in0=A[:, :], in1=T0[:, :], op=ALU.mult), [ctx["e9"]])
                            ey = P.op("dve", lambda E: E.scalar_tensor_tensor(out=ycur[:, :], in0=T1[:, :], scalar=gsubs[:, 0:1], in1=zT[:, 0, i0:i0 + 512], op0=ALU.mult, op1=ALU.mult),
                                      [e11, t_gs, st8.get(("ydma", q % 2))])
                            st8["epi_done"] = ey
                            finish_store(ey, slice(0, 128))

                        nop = lambda: None
                        pending.append([r1(0), r1(1), r1(2), r1(3), nop, nop, s1, nop, s2, nop, s3, nop, s4, nop, s5, nop, s6, nop, s7, nop, s8, nop, s9, nop, s10])
                        return
                        rows = slice(0, 128)
                    else:
                        Rs, T0, A = ET[0], ET[1], ET[5]
                        ycur = y
                        o0 = P.op("dve", lambda E: E.tensor_copy(out=T0[:, :], in_=OB[0][:, :]), [tk, st8.get("epi_done")])
                        o1 = P.op("dve", lambda E: E.tensor_copy(out=A[:, :], in_=LB[0][:, :]), [tk])
                        st8["epi_free"] = [o0, o1]
                        rp = []

                        def rr(c):
                            def f():
                                rp.append(P.op("dve", lambda E: E.reciprocal(out=Rs[:, c * 128:(c + 1) * 128], in_=A[:, c * 128:(c + 1) * 128]), [o1]))
                            return f

                        ctx = {}

                        def d1():
                            ctx["e2"] = P.op("dve", lambda E: E.tensor_tensor(out=T0[:, :], in0=T0[:, :], in1=Rs[:, :], op=ALU.mult), [rp, o0])

                        def d2():
                            ey = P.op("dve", lambda E: E.tensor_tensor(out=ycur[:, :], in0=T0[:, :], in1=zT[:, 1, i0:i0 + 512], op=ALU.mult), [ctx["e2"], st8.get(("ydma", q % 2))])
                            st8["epi_done"] = ey
                            finish_store(ey, slice(128, 256))

                        nop = lambda: None
                        pending.append([rr(0), rr(1), rr(2), rr(3), nop, d1, nop, d2])
                        return
                    finish_store(ey, rows)

                for n in range(N + 2):
                    if n < N:
                        front(n)
                        if extra is not None:
                            extra(n)
                    if n >= 2:
                        back(n - 2)
                while pending:
                    run_pending()
                return [av_tok[N - 1], mul_tok[N - 1], exp_tok[N - 1], st8["epi_free"], st8.get(("ydma", 0)), st8.get(("ydma", 1)), st8["ss_free"], st8["bk7_free"], st8.get("epi_done")]

            dil_done = attention("dil")
            if stop == "dil":
                P.wait("sp", dil_done)
            else:
                wsem3 = [P.dsem("w3a"), P.dsem("w3b")]
                w3_cast = [None, None]
                w3_tok = []
                jobs = [(Wo, k, w_out, None) for k in range(8)] + [(Wg, k, w_pg, k) for k in range(8)] + [(Wp, k, w_pp, None) for k in range(2)]

                def w3_job(ji):
                    dst, k, src, gk = jobs[ji]
                    sl = ji % 2
                    t_ld = P.dma("sp", DMA(wst[sl][:, :], src[k * 128:(k + 1) * 128, :]), wsem3[sl], [w3_cast[sl], dil_done if ji < 2 else None])
                    if gk is None:
                        w3_cast[sl] = P.op("act", lambda E: E.activation(out=dst[:, k, :], in_=wst[sl][:, :], func=AF.Copy), [t_ld])
                    else:
                        w3_cast[sl] = P.op("act", lambda E: E.activation(out=dst[:, k, :], in_=wst[sl][:, :], func=AF.Copy, scale=gple32[:, k:k + 1]), [t_ld, t_gple])
                    w3_tok.append(w3_cast[sl])

                def diff_extra(n):
                    if n % 6 == 3 and n // 6 < len(jobs):
                        w3_job(n // 6)

                diff_done = attention("diff", diff_extra)

            if stop in ("dil", "diff"):
                last = dil_done if stop == "dil" else diff_done
                P.wait("sp", last)
                for j in range(4):
                    P.wait("sp", [(ysem[j][0][0], ysem[j][0][1]), (ysem[j][1][0], ysem[j][1][1])])
                for j in range(4):
                    for r in range(2):
                        t1 = P.dma("sp", DMA(ET[0][:, :].bitcast(BF16)[:, 0:1024], ybin[j, r * 128:(r + 1) * 128, 0:1024]), stq, [fin[-1]] if fin else [])
                        P.wait("sp", [t1])
                        fin.append(P.dma("sp", DMA(dbg_y[j * 256 + r * 128:j * 256 + (r + 1) * 128, 0:1024], ET[0][:, :].bitcast(BF16)[:, 0:1024]), stq))
                        P.wait("sp", [fin[-1]])
                        t1 = P.dma("sp", DMA(ET[0][:, :].bitcast(BF16)[:, 0:1024], ybin[j, r * 128:(r + 1) * 128, 1024:2048]), stq)
                        P.wait("sp", [t1])
                        fin.append(P.dma("sp", DMA(dbg_y[j * 256 + r * 128:j * 256 + (r + 1) * 128, 1024:2048], ET[0][:, :].bitcast(BF16)[:, 0:1024]), stq))
                        P.wait("sp", [fin[-1]])

        if stop is None:
            P.wait("pool", [(ccsem[0], ccsem[1])])
            gsem = P.dsem("gath")
            gtok = []
            for c in range(8):
                gtok.append(P.dma("pool", lambda E, c=c: E.indirect_dma_start(
                    out=yTa[:, c, :], out_offset=None, in_=ygat[:, :],
                    in_offset=bass.IndirectOffsetOnAxis(ap=idxs[:, c:c + 1], axis=0)), gsem, [diff_done, tc["idx"]]))
            psem = [P.dsem("p3a"), P.dsem("p3b")]
            pc_tok = [None, None]
            p_tok = []
            for ji in range(4):
                c2, hf = divmod(ji, 2)
                sl = ji % 2
                t_ld = P.dma("sp", DMA(pst[sl][:, :], pT_d[c2 * 128:(c2 + 1) * 128, hf * 1024:(hf + 1) * 1024]), psem[sl], [pc_tok[sl], diff_done if ji < 2 else None])
                pc_tok[sl] = P.op("act", lambda E, c2=c2, hf=hf, sl=sl: E.activation(out=pTb[:, c2, hf * 1024:(hf + 1) * 1024], in_=pst[sl][:, :], func=AF.Copy), [t_ld])
                p_tok.append(pc_tok[sl])
            xosem = [P.dsem("xo0"), P.dsem("xo1")]
            osem = [P.dsem("o0"), P.dsem("o1")]
            x1e_tok, hs3_tok, tr3_tok, h2e_tok, gm_tok, pm_tok, sg_tok, fo_tok, od_tok, am_tok = ({} for _ in range(10))
            A0, A1, TB, G0, G1, PP0, PP1 = banks[0], banks[1], banks_bf[2], banks[3], banks[4], banks[5], banks[6]
            xl_tok = {}
            dx_tok = {}
            dxsem = P.dsem("dx")

            def load3(tt):
                if tt >= 16:
                    return
                sl = tt % 2
                xl_tok[tt] = P.dma("sp", DMA(xo[sl][:, :], xo_d[tt * 128:(tt + 1) * 128, :]), xosem[sl], [x1e_tok.get(tt - 2), diff_done if tt < 2 else None])

            rc3_tok = {}

            def stX(tt):
                sl, s3 = tt % 2, tt % 3
                c = 32 + 4 * (tt % 4)
                tsl = slice(tt * 128, (tt + 1) * 128)
                t_x = xl_tok[tt]
                for hf, AB in ((0, A0), (1, A1)):
                    for cc in range(8):
                        tk = P.op("pe", lambda E, hf=hf, AB=AB, cc=cc: E.matmul(AB[:, :], lhsT=yTa[:, cc, tsl], rhs=Wo[:, cc, hf * 512:(hf + 1) * 512], start=(cc == 0), stop=(cc == 7)),
                                  [gtok, w3_tok, x1e_tok.get(tt - 1)])
                am_tok[tt] = tk
                ta = P.op("dve", lambda E: E.tensor_tensor(out=x1[s3][:, 0:512], in0=A0[:, :], in1=xo[sl][:, 0:512], op=ALU.add), [tk, t_x, fo_tok.get(tt - 3), dx_tok.get(tt - 3)])
                x1e_tok[tt] = P.op("dve", lambda E: E.tensor_tensor(out=x1[s3][:, 512:1024], in0=A1[:, :], in1=xo[sl][:, 512:1024], op=ALU.add), [tk])
                load3(tt + 2)
                if dbg.get("dump_x1"):
                    dx_tok[tt] = P.dma("sp", DMA(dbg_x1[tsl, :], x1[s3][:, :]), dxsem, [ta, x1e_tok[tt]])
                    fin.append(dx_tok[tt])
                hs3_tok[tt] = P.op("act", lambda E: E.activation(out=h2[sl][:, :], in_=x1[s3][:, :], func=AF.Copy), [ta, x1e_tok[tt], tr3_tok.get(tt - 2)])
                t_sq = P.op("act", lambda E: E.activation(out=junk3[:, :], in_=x1[s3][:, :], func=AF.Square, accum_out=SC(c)), [ta, x1e_tok[tt]])
                t_a = P.op("dve", lambda E: E.tensor_scalar(out=SC(c + 1), in0=SC(c), scalar1=float(DM * EPS), scalar2=None, op0=ALU.add), [t_sq])
                t_b = P.op("act", lambda E: E.activation(out=SC(c + 2), in_=SC(c + 1), func=AF.Sqrt), [t_a])
                rc3_tok[tt] = P.op("dve", lambda E: E.reciprocal(out=SC(c + 3), in_=SC(c + 2)), [t_b])

            def stY1(tt):
                sl, s3 = tt % 2, tt % 3
                c = 32 + 4 * (tt % 4)
                for k in range(8):
                    tk = P.op("pe", lambda E, k=k: E.transpose(out=TB[:, k * 128:(k + 1) * 128], in_=h2[sl][:, k * 128:(k + 1) * 128], identity=identb[:, :]), [hs3_tok[tt], h2e_tok.get(tt - 1)])
                tr3_tok[tt] = tk
                h2e_tok[tt] = P.op("dve", lambda E: E.tensor_copy(out=h2T[sl][:, :, :], in_=TB[:, :].rearrange("p (k t) -> p k t", k=8)), [tk, gm_tok.get(tt - 2)])

            def stY2(tt):
                sl, s3 = tt % 2, tt % 3
                tsl = slice(tt * 128, (tt + 1) * 128)
                for hf, GB in ((0, G0), (1, G1)):
                    for k in range(8):
                        tk = P.op("pe", lambda E, hf=hf, GB=GB, k=k: E.matmul(GB[:, :], lhsT=h2T[sl][:, k, :], rhs=Wg[:, k, hf * 512:(hf + 1) * 512], start=(k == 0), stop=(k == 7)),
                                  [h2e_tok[tt], sg_tok.get(tt - 1)])
                gm_tok[tt] = tk
                for hf, PB in ((0, PP0), (1, PP1)):
                    for c2 in range(2):
                        tk = P.op("pe", lambda E, hf=hf, PB=PB, c2=c2: E.matmul(PB[:, :], lhsT=pTb[:, c2, tsl], rhs=Wp[:, c2, hf * 512:(hf + 1) * 512], start=(c2 == 0), stop=(c2 == 1)),
                                  [p_tok, fo_tok.get(tt - 1)])
                pm_tok[tt] = tk
                c = 32 + 4 * (tt % 4)
                s0 = P.op("act", lambda E: E.activation(out=gate[:, 0:512], in_=G0[:, :], func=AF.Sigmoid, scale=SC(c + 3)), [gm_tok[tt], fo_tok.get(tt - 1), rc3_tok[tt]])
                sg_tok[tt] = P.op("act", lambda E: E.activation(out=gate[:, 512:1024], in_=G1[:, :], func=AF.Sigmoid, scale=SC(c + 3)), [gm_tok[tt]])
                f0 = P.op("dve", lambda E: E.tensor_tensor(out=gate[:, 0:512], in0=gate[:, 0:512], in1=PP0[:, :], op=ALU.mult), [s0, pm_tok[tt]])
                f1 = P.op("dve", lambda E: E.tensor_tensor(out=gate[:, 512:1024], in0=gate[:, 512:1024], in1=PP1[:, :], op=ALU.mult), [sg_tok[tt], pm_tok[tt]])
                fo_tok[tt] = P.op("dve", lambda E: E.tensor_tensor(out=osb[sl][:, :], in0=gate[:, :], in1=x1[s3][:, :], op=ALU.add), [f0, f1, od_tok.get(tt - 2), hs3_tok[tt]])
                od_tok[tt] = P.dma("sp", DMA(out_d[tsl, :], osb[sl][:, :]), osem[sl], [fo_tok[tt]])
                fin.append(od_tok[tt])

            load3(0)
            load3(1)
            for step in range(16 + 2):
                if step < 16:
                    stX(step)
                if 1 <= step <= 16:
                    stY1(step - 1)
                if step >= 2:
                    stY2(step - 2)

        P.wait("sp", fin)
        with nc.Block() as block:
            P.replay(block)
    return nc


def _alibi(n):
    return np.exp2(-8.0 * np.arange(1, n + 1, dtype=np.float64) / n).astype(np.float32)


def _const_tables():
    p = np.arange(128)[:, None]
    n = np.arange(1152)[None, :]
    tabDd = np.abs(n - p - 512).astype(np.float32)
    n = np.arange(2944)[None, :]
    dl = n - p - 1408
    ad = np.abs(dl)
    tabDl = ad.astype(np.float32)
    mult = (ad <= 64).astype(np.float32) + ((dl % 4 == 0) & (ad <= 256)).astype(np.float32) + ((dl % 16 == 0) & (ad <= 1024)).astype(np.float32)
    mcol = np.broadcast_to((128.0 * np.arange(64, dtype=np.float32))[None, :], (128, 64)).copy()
    ident = np.eye(128, dtype=np.float32).astype(ml_dtypes.bfloat16)
    return tabDd, tabDl, mult.astype(np.float32), mcol, ident


def make_in_maps(x, p, mix_norm_g, w_in, diff_q_norm_g, diff_k_norm_g, lambda_q1, lambda_k1, lambda_q2, lambda_k2,
                 diff_sub_norm_g, dil_q_norm_g, dil_k_norm_g, w_out, ple_norm_g, w_ple_gate, w_ple_proj):
    f = lambda a: np.ascontiguousarray(np.asarray(a, dtype=np.float32))
    x, p = f(x), f(p)
    w_in0, w_out0, w_pg0, w_pp0 = f(w_in)[0], f(w_out)[0], f(w_ple_gate)[0], f(w_ple_proj)[0]
    tabDd, tabDl, tabMl, mcol, ident = _const_tables()
    sl_diff, sl_dil = _alibi(4), _alibi(8)
    bc = lambda v, n=128: np.ascontiguousarray(np.broadcast_to(np.asarray(v, np.float32)[None, :], (n, len(v))))
    gmix = np.ascontiguousarray(f(mix_norm_g)[0].reshape(8, 128).T)
    gple = np.ascontiguousarray(f(ple_norm_g)[0].reshape(8, 128).T)
    dq, dk, bq, bk = f(diff_q_norm_g)[0], f(diff_k_norm_g)[0], f(dil_q_norm_g)[0], f(dil_k_norm_g)[0]
    gqk = bc(np.concatenate([dq, dq, dk, dk, bq, bq, bk, bk]))
    lamv = bc(np.concatenate([f(lambda_q1)[0], f(lambda_k1)[0], f(lambda_q2)[0], f(lambda_k2)[0]]))
    gsub = np.ascontiguousarray(f(diff_sub_norm_g)[0][:, None])
    perm = np.concatenate([np.arange(pt * 512 + r * 128, pt * 512 + (r + 1) * 128) for r in range(4) for pt in range(2)])
    w_out_p = np.ascontiguousarray(w_out0[perm])
    maps = []
    for c in range(8):
        b, g = divmod(c, 4)
        cols = np.concatenate([np.arange(o + g * 128, o + (g + 1) * 128) for o in (0, 512, 1536, 2048, 1024, 2560, 3072, 3584)])
        nsl = bc(np.array([-sl_diff[g], -sl_dil[2 * g], -sl_dil[2 * g + 1]], np.float32))
        idx = (g * 1024 + np.arange(8)[None, :] * 128 + np.arange(128)[:, None]).astype(np.int32)
        maps.append({
            "xb": x[b], "xo": np.ascontiguousarray(x[b, g * 2048:(g + 1) * 2048]),
            "pT": np.ascontiguousarray(p[0, b, g * 2048:(g + 1) * 2048].T),
            "w_in": np.ascontiguousarray(w_in0[:, cols]), "w_out": w_out_p, "w_pg": w_pg0, "w_pp": w_pp0,
            "gmix": gmix, "gple": gple, "gqk": gqk, "lamv": lamv, "gsub": gsub, "nsl": nsl,
            "tabDd": tabDd, "tabDl": tabDl, "tabMl": tabMl, "mcol": mcol, "ident": ident, "idx": idx,
        })
    return maps


def kernel(**inputs):
    maps = make_in_maps(**inputs)
    nc = build_program(_DEBUG)
    res = run_bass_kernel_spmd(nc, maps, core_ids=list(range(8)))
    if _DEBUG.get("stop") or _DEBUG.get("dump_x1"):
        return res
    out = np.empty((2, S_LEN, DM), np.float32)
    for c in range(8):
        b, g = divmod(c, 4)
        out[b, g * 2048:(g + 1) * 2048] = np.asarray(res.results[c]["out"], dtype=np.float32)
    return out
```

```python
import numpy as np
import ml_dtypes
from contextlib import ExitStack
import concourse.bass as bass
import concourse.mybir as mybir
from concourse.bass_utils import run_bass_kernel_spmd

F32 = mybir.dt.float32
BF16 = mybir.dt.bfloat16
I32 = mybir.dt.int32
AF = mybir.ActivationFunctionType
ALU = mybir.AluOpType
AX = mybir.AxisListType

S_LEN = 8192
DM = 1024
NTILE = 64
NQT = 16
EPS = 1e-6
LAM_INIT = 0.8 - 0.6 * 1.0
ENGS = ("pe", "act", "dve", "pool", "sp")
SB_BASE = 16384

_DEBUG = {}


class Prog:
    def __init__(self, nc, stack):
        self.nc = nc
        self.stack = stack
        self.q = {e: [] for e in ENGS}
        self.esem = {e: stack.enter_context(nc.semaphore(f"es_{e}")) for e in ENGS}
        self.ecnt = {e: 0 for e in ENGS}
        self.waited = {e: {} for e in ENGS}
        self.nsem = 0
        self.ntens = 0

    def dsem(self, name=None):
        self.nsem += 1
        s = self.stack.enter_context(self.nc.semaphore(name or f"ds{self.nsem}"))
        return [s, 0]

    def sb(self, shape, dtype, off):
        self.ntens += 1
        h = self.nc.alloc_sbuf_tensor_at(f"sb{self.ntens}", list(shape), dtype, offset=SB_BASE + off)
        return h.ap()

    def _flat(self, deps):
        out = []
        for d in deps:
            if d is None:
                continue
            if isinstance(d, tuple) and len(d) == 2 and not isinstance(d[0], (tuple, list)):
                out.append(d)
            else:
                out.extend(self._flat(d))
        return out

    def _wait(self, eng, tok):
        sem, val = tok
        key = id(sem)
        if self.waited[eng].get(key, 0) >= val:
            return
        self.waited[eng][key] = val
        self.q[eng].append(lambda E, sem=sem, val=val: E.wait_ge(sem, val))

    def op(self, eng, fn, deps=(), pre=None):
        if pre is not None:
            self.q[eng].append(pre)
        for d in self._flat(deps):
            self._wait(eng, d)
        self.ecnt[eng] += 1
        n = self.ecnt[eng]
        sem = self.esem[eng]
        self.q[eng].append(lambda E, fn=fn, sem=sem: fn(E).then_inc(sem, 1))
        return (sem, n)

    def dma(self, eng, fn, ds, deps=()):
        for d in self._flat(deps):
            self._wait(eng, d)
        ds[1] += 16
        self.q[eng].append(lambda E, fn=fn, s=ds[0]: fn(E).then_inc(s, 16))
        return (ds[0], ds[1])

    def wait(self, eng, deps):
        for d in self._flat(deps):
            self._wait(eng, d)

    def replay(self, block):
        q = self.q

        @block.tensor
        def _(E):
            for f in q["pe"]:
                f(E)

        @block.scalar
        def _(E):
            for f in q["act"]:
                f(E)

        @block.vector
        def _(E):
            for f in q["dve"]:
                f(E)

        @block.gpsimd
        def _(E):
            for f in q["pool"]:
                f(E)

        @block.sync
        def _(E):
            for f in q["sp"]:
                f(E)


def DMA(out, in_):
    return lambda E: E.dma_start(out=out, in_=in_)


def build_program(dbg=None):
    dbg = dbg or {}
    stop = dbg.get("stop")
    nqt_dbg = dbg.get("nqt", NQT)
    nc = bass.Bass("TRN2", target_bir_lowering=False)

    def din(name, shape, dt=F32):
        return nc.dram_tensor(name, list(shape), dt, kind="ExternalInput").ap()

    def dout(name, shape, dt=F32):
        return nc.dram_tensor(name, list(shape), dt, kind="ExternalOutput").ap()

    xb = din("xb", [S_LEN, DM])
    xo_d = din("xo", [2048, DM])
    pT_d = din("pT", [256, 2048])
    w_in = din("w_in", [DM, 1024])
    w_out = din("w_out", [1024, DM])
    w_pg = din("w_pg", [DM, DM])
    w_pp = din("w_pp", [256, DM])
    gmix_d = din("gmix", [128, 8])
    gple_d = din("gple", [128, 8])
    gqk_d = din("gqk", [128, 512])
    lamv_d = din("lamv", [128, 256])
    gsub_d = din("gsub", [128, 1])
    nsl_d = din("nsl", [128, 3])
    tabDd_d = din("tabDd", [128, 1152])
    tabDl_d = din("tabDl", [128, 2944])
    tabMl_d = din("tabMl", [128, 2944])
    mcol_d = din("mcol", [128, 64])
    ident_d = din("ident", [128, 128], BF16)
    idx_d = din("idx", [128, 8], I32)
    out_d = dout("out", [2048, DM])
    ybin = nc.dram_tensor("ybin", [4, 256, 2048], BF16).ap()
    ygat = nc.dram_tensor("ygat", [4096, 2048], BF16).ap()
    if stop == "p1":
        dbg_qk = dout("dbg_qk", [128, 4 * S_LEN], BF16)
        dbg_vd = dout("dbg_vd", [128, 64 * 128], BF16)
        dbg_vl = dout("dbg_vl", [128, 64 * 128], BF16)
        dbg_zt = dout("dbg_zt", [128, 2 * S_LEN], BF16)
    if stop in ("dil", "diff"):
        dbg_y = dout("dbg_y", [4 * 256, 2048], BF16)
    if dbg.get("dump_x1"):
        dbg_x1 = dout("dbg_x1", [2048, DM])
        dbg_h2t = dout("dbg_h2t", [16 * 128, DM], BF16)

    with ExitStack() as st:
        P = Prog(nc, st)
        QK = P.sb([128, 4, S_LEN], BF16, 0)
        Vd = P.sb([128, 64, 128], BF16, 65536)
        Vl = P.sb([128, 64, 128], BF16, 81920)
        zT = P.sb([128, 2, S_LEN], BF16, 98304)
        C0 = 131072
        identb = P.sb([128, 128], BF16, C0)
        ones32 = P.sb([128, 128], F32, C0 + 256)
        onesb = P.sb([128, 128], BF16, C0 + 768)
        cb = P.sb([128, 64], F32, C0 + 1024)
        gmix32 = P.sb([128, 8], F32, C0 + 1280)
        gple32 = P.sb([128, 8], F32, C0 + 1408)
        nsl = P.sb([128, 3], F32, C0 + 1536)
        nlam = P.sb([128, 1], F32, C0 + 1664)
        gsubs = P.sb([128, 1], F32, C0 + 1792)
        idxs = P.sb([128, 8], I32, C0 + 1920)
        sc = P.sb([128, 1536], F32, 204800)

        def SC(slot, w=1):
            return sc[:, 32 * slot:32 * slot + w]
        gqk8 = P.sb([128, 512], F32, C0 + 2048)
        epsc = P.sb([128, 8], F32, 211968)
        sel = P.sb([64, 2, 128], F32, 210944)
        W0 = C0 + 4096
        Wi = P.sb([128, 8, 1024], BF16, W0)
        xs = [P.sb([128, 1024], F32, W0 + 16384 + 4096 * i) for i in range(4)]
        junk = P.sb([128, 1024], BF16, W0 + 32768)
        hb = [P.sb([128, 1024], BF16, W0 + 34816 + 2048 * i) for i in range(2)]
        hT = [P.sb([128, 8, 512], BF16, W0 + 38912 + 8192 * i) for i in range(2)]
        usb = [P.sb([128, 512], F32, W0 + 55296 + 2048 * i) for i in range(3)]
        sqb = P.sb([128, 512], F32, W0 + 61440)
        tmpb = P.sb([128, 512], F32, W0 + 63488)
        qn = [P.sb([128, 512], BF16, W0 + 65536 + 1024 * i) for i in range(2)]
        lamtmp = P.sb([128, 128], F32, W0 + 67584)
        wstg = [P.sb([128, 1024], F32, W0 + 38912 + 4096 * i) for i in range(4)]
        Etab = P.sb([128, 1152], F32, W0)
        Gtab = [P.sb([128, 2944], F32, W0 + 4608 + 11776 * i) for i in range(2)]
        Mtmp = P.sb([128, 2944], F32, W0 + 28160)
        P32 = [[P.sb([128, 512], BF16, W0 + 44032 + 1024 * (2 * i + s)) for s in range(2)] for i in range(2)]
        EtabB = P.sb([128, 1152], BF16, W0 + 28160)
        GtabB = [P.sb([128, 2944], BF16, W0 + 30464 + 5888 * i) for i in range(2)]
        Pb = [[P.sb([128, 512], BF16, W0 + 48128 + 1024 * (2 * i + s)) for s in range(2)] for i in range(3)]
        ET = [P.sb([128, 512], F32, W0 + 54272 + 2048 * i) for i in range(6)]
        yo = [P.sb([128, 512], BF16, W0 + 66560 + 1024 * i) for i in range(2)]
        Wo = P.sb([128, 8, 1024], BF16, 32768)
        Wg = P.sb([128, 8, 1024], BF16, 49152)
        Wp = P.sb([128, 2, 1024], BF16, 81920)
        wst = [P.sb([128, 1024], F32, 81920 + 4096 + 4096 * i) for i in range(2)]
        yTa = P.sb([128, 8, 2048], BF16, 0)
        pTb = P.sb([128, 2, 2048], BF16, 65536)
        xo = [P.sb([128, 1024], F32, 98304 + 4096 * i) for i in range(2)]
        x1 = [P.sb([128, 1024], F32, 98304 + 8192 + 4096 * i) for i in range(3)]
        osb = [P.sb([128, 1024], F32, 98304 + 20480 + 4096 * i) for i in range(2)]
        gate = P.sb([128, 1024], F32, 98304 + 28672)
        h2 = [P.sb([128, 1024], BF16, W0 + 2048 * i) for i in range(2)]
        h2T = [P.sb([128, 8, 128], BF16, W0 + 4096 + 2048 * i) for i in range(2)]
        junk3 = P.sb([128, 1024], BF16, W0 + 8192)
        pst = [P.sb([128, 1024], F32, W0 + 10240 + 4096 * i) for i in range(2)]

        banks = [nc.alloc_psum_tensor(f"bank{i}", [128, 512], F32).ap() for i in range(8)]
        banks_bf = [b.bitcast(BF16) for b in banks]

        ld = P.dsem("ld_const")
        tc = {}
        for name, dst, src in (("ident", identb, ident_d), ("gmix", gmix32, gmix_d), ("gple", gple32, gple_d),
                               ("gqk", gqk8, gqk_d), ("lamv", lamtmp[:, 0:128], lamv_d[:, 0:128]),
                               ("lamv2", tmpb[:, 0:128], lamv_d[:, 128:256]),
                               ("gsub", gsubs, gsub_d), ("nsl", nsl, nsl_d), ("mcol", cb, mcol_d), ("idx", idxs, idx_d)):
            tc[name] = P.dma("sp", DMA(dst, src), ld)
        for name in list(tc):
            tc[name] = (ld[0], ld[1])
        t_ones32 = P.op("pool", lambda E: E.memset(ones32[:, :], 1.0))
        t_epsc = P.op("pool", lambda E: E.memset(epsc[:, :], float(128 * EPS)))
        P.op("pool", lambda E: E.memset(sel[:, :, :], 0.0))
        P.op("pool", lambda E: E.memset(sel[0:1, 0, :], 1.0))
        t_sel = P.op("pool", lambda E: E.memset(sel[32:33, 1, :], 1.0))
        t_onesb = P.op("pool", lambda E: E.memset(onesb[:, :], 1.0))
        t_gmix = P.op("dve", lambda E: E.tensor_scalar(out=gmix32[:, :], in0=gmix32[:, :], scalar1=32.0, scalar2=None, op0=ALU.mult), [tc["gmix"]])
        t_gple = P.op("dve", lambda E: E.tensor_scalar(out=gple32[:, :], in0=gple32[:, :], scalar1=32.0, scalar2=None, op0=ALU.mult), [tc["gple"]])
        t_gqk = P.op("dve", lambda E: E.tensor_scalar(out=gqk8[:, :], in0=gqk8[:, :], scalar1=8.0, scalar2=None, op0=ALU.mult), [tc["gqk"]])
        t_cb = P.op("dve", lambda E: E.tensor_scalar(out=cb[:, :], in0=cb[:, :], scalar1=nsl[:, 0:1], scalar2=None, op0=ALU.mult), [tc["mcol"], tc["nsl"]])
        t_gs = P.op("dve", lambda E: E.tensor_scalar(out=gsubs[:, :], in0=gsubs[:, :], scalar1=float((1.0 - LAM_INIT) * np.sqrt(128.0)), scalar2=None, op0=ALU.mult), [tc["gsub"]])
        t_l1 = P.op("dve", lambda E: E.tensor_tensor(out=lamtmp[:, 0:64], in0=lamtmp[:, 0:64], in1=lamtmp[:, 64:128], op=ALU.mult), [tc["lamv"]])
        t_l2 = P.op("dve", lambda E: E.tensor_tensor(out=tmpb[:, 0:64], in0=tmpb[:, 0:64], in1=tmpb[:, 64:128], op=ALU.mult), [tc["lamv2"]])
        t_l3 = P.op("dve", lambda E: E.tensor_reduce(out=SC(40), in_=lamtmp[:, 0:64], axis=AX.X, op=ALU.add), [t_l1])
        t_l4 = P.op("dve", lambda E: E.tensor_reduce(out=SC(41), in_=tmpb[:, 0:64], axis=AX.X, op=ALU.add), [t_l2])
        t_l5a = P.op("act", lambda E: E.activation(out=SC(42), in_=SC(40), func=AF.Exp), [t_l3])
        t_l5 = P.op("act", lambda E: E.activation(out=SC(43), in_=SC(41), func=AF.Exp), [t_l4])
        t_l6 = P.op("dve", lambda E: E.tensor_tensor(out=SC(44), in0=SC(43), in1=SC(42), op=ALU.subtract), [t_l5, t_l5a])
        t_nlam = P.op("dve", lambda E: E.tensor_scalar(out=nlam[:, :], in0=SC(44), scalar1=float(-LAM_INIT), scalar2=None, op0=ALU.add), [t_l6])

        wsem = [P.dsem(f"wst{i}") for i in range(4)]
        wi_tok = []
        cast_tok = [None] * 4
        for k in range(8):
            sl = k % 4
            t_ld = P.dma("sp", DMA(wstg[sl][:, :], w_in[k * 128:(k + 1) * 128, :]), wsem[sl], [cast_tok[sl]])
            cast_tok[sl] = P.op("act", lambda E, k=k, sl=sl: E.activation(out=Wi[:, k, :], in_=wstg[sl][:, :], func=AF.Copy, scale=gmix32[:, k:k + 1]), [t_ld, t_gmix])
            wi_tok.append(cast_tok[sl])

        xsem = [P.dsem(f"xs{i}") for i in range(4)]
        hs_tok, tra_tok, hTe_tok, u_tok, ue_tok, vl_tok, tmp_tok, qn_tok, trq_tok, qke_tok = ({} for _ in range(10))
        rcp_tok, a8_tok = {}, {}
        z_tok = {}
        silu_tok = {0: {}, 1: {}}
        vd_tok = {}
        nt1 = dbg.get("ntile", NTILE)

        xl1_tok, sq1_tok, add1_tok, sqt1_tok, s8_tok = {}, {}, {}, {}, {}

        def f_load(t):
            sl4 = t % 4
            d0 = [hs_tok.get(t - 4)]
            xl1_tok[t] = P.dma("sp", DMA(xs[sl4][:, :], xb[t * 128:(t + 1) * 128, :]), xsem[sl4], d0)

        def f_a1(t):
            sl4 = t % 4
            c = 4 * sl4
            sq1_tok[t] = P.op("act", lambda E: E.activation(out=junk[:, :], in_=xs[sl4][:, :], func=AF.Square, accum_out=SC(c)), [xl1_tok[t]])

        def f_a1_add(t):
            c = 4 * (t % 4)
            add1_tok[t] = P.op("dve", lambda E: E.tensor_scalar(out=SC(c + 1), in0=SC(c), scalar1=float(DM * EPS), scalar2=None, op0=ALU.add), [sq1_tok[t]])

        def f_a1_sqrt(t):
            c = 4 * (t % 4)
            sqt1_tok[t] = P.op("act", lambda E: E.activation(out=SC(c + 2), in_=SC(c + 1), func=AF.Sqrt), [add1_tok[t]])

        def f_a1_rcp(t):
            c = 4 * (t % 4)
            rcp_tok[t] = P.op("dve", lambda E: E.reciprocal(out=SC(c + 3), in_=SC(c + 2)), [sqt1_tok[t]])

        def f_hs(t):
            sl, sl4 = t % 2, t % 4
            c = 4 * sl4
            hs_tok[t] = P.op("act", lambda E: E.activation(out=hb[sl][:, :], in_=xs[sl4][:, :], func=AF.Copy, scale=SC(c + 3)), [rcp_tok[t], tra_tok.get(t - 2)])

        def f_T(t):
            sl = t % 2
            tb = banks_bf[sl]
            for k in range(8):
                tk = P.op("pe", lambda E, k=k: E.transpose(out=tb[:, k * 128:(k + 1) * 128], in_=hb[sl][:, k * 128:(k + 1) * 128], identity=identb[:, :]),
                          [hs_tok[t], hTe_tok.get(t - 2), tc["ident"]])
            tra_tok[t] = tk

        def f_hTe(t):
            G, sub = divmod(t, 4)
            sl = t % 2
            tb = banks_bf[sl]
            dfree = []
            if G >= 2 and sub == 0:
                dfree = [u_tok[4 * (G - 2) + 3], z_tok[G - 2]]
            if t < 8:
                dfree = [dfree, wi_tok]
            hTe_tok[t] = P.op("dve", lambda E: E.tensor_copy(out=hT[G % 2][:, :, sub * 128:(sub + 1) * 128],
                                                             in_=tb[:, :].rearrange("p (k t) -> p k t", k=8)), [tra_tok[t], dfree])

        def f_U(t):
            G, sub = divmod(t, 4)
            sl = t % 2
            U0 = banks[2 + sl]
            U1 = banks[4 + sl]
            for k in range(8):
                tk = P.op("pe", lambda E, k=k: E.matmul(U0[:, :], lhsT=hT[G % 2][:, k, sub * 128:(sub + 1) * 128], rhs=Wi[:, k, 0:512], start=(k == 0), stop=(k == 7)),
                          [hTe_tok[t], wi_tok, ue_tok.get(t - 2)])
            for k in range(8):
                tk = P.op("pe", lambda E, k=k: E.matmul(U1[:, 0:256], lhsT=hT[G % 2][:, k, sub * 128:(sub + 1) * 128], rhs=Wi[:, k, 512:768], start=(k == 0), stop=(k == 7)),
                          [vl_tok.get(t - 2)])
            u_tok[t] = tk
            if sub == 3:
                f_z(t, 0)

        def f_z(t, zi):
            G = t // 4
            Z = banks[6]
            for k in range(8):
                tk = P.op("pe", lambda E, k=k: E.matmul(Z[:, :], lhsT=Wi[:, k, 768 + zi * 128:896 + zi * 128], rhs=hT[G % 2][:, k, :], start=(k == 0), stop=(k == 7)),
                          [silu_tok[1].get(G - 1) if zi == 0 else silu_tok[0][G], hTe_tok[t]])
            z_tok[(G, zi)] = tk
            if zi == 1:
                z_tok[G] = tk

        def f_silu(t, zi):
            G = t // 4
            Z = banks[6]
            silu_tok[zi][G] = P.op("act", lambda E: E.activation(out=zT[:, zi, G * 512:(G + 1) * 512], in_=Z[:, :], func=AF.Silu), [z_tok[(G, zi)]])

        def f_z1(t):
            if t % 4 == 3:
                f_z(t, 1)
                f_silu(t, 1)

        def f_ue(t):
            G, sub = divmod(t, 4)
            sl, s3 = t % 2, t % 3
            U0 = banks[2 + sl]
            U1 = banks[4 + sl]
            ue_tok[t] = P.op("act", lambda E: E.activation(out=usb[s3][:, :], in_=U0[:, :], func=AF.Copy), [u_tok[t], tmp_tok.get(t - 3)])
            vd_tok[t] = P.op("act", lambda E: E.activation(out=Vd[:, t, :], in_=U1[:, 0:128], func=AF.Copy), [u_tok[t]])
            vl_tok[t] = P.op("act", lambda E: E.activation(out=Vl[:, t, :], in_=U1[:, 128:256], func=AF.Copy), [u_tok[t]])
            if sub == 3:
                f_silu(t, 0)

        def f_sq(t):
            s3 = t % 3
            c = 16 + 4 * (t % 4)
            t1 = P.op("dve", lambda E: E.tensor_tensor(out=sqb[:, :], in0=usb[s3][:, :], in1=usb[s3][:, :], op=ALU.mult), [ue_tok[t]])
            t2 = P.op("dve", lambda E: E.tensor_reduce(out=SC(c, 8), in_=sqb[:, :].rearrange("p (g d) -> p g d", g=8), axis=AX.X, op=ALU.add), [t1])
            a8_tok[t] = P.op("dve", lambda E: E.tensor_scalar(out=SC(c + 1, 8), in0=SC(c, 8), scalar1=float(64 * EPS), scalar2=None, op0=ALU.add), [t2])

        def f_sqrt8(t):
            c = 16 + 4 * (t % 4)
            s8_tok[t] = P.op("act", lambda E: E.activation(out=SC(c + 2, 8), in_=SC(c + 1, 8), func=AF.Sqrt), [a8_tok[t]])

        def f_b2(t):
            sl, s3 = t % 2, t % 3
            c = 16 + 4 * (t % 4)
            t5 = P.op("dve", lambda E: E.reciprocal(out=SC(c + 3, 8), in_=SC(c + 2, 8)), [s8_tok[t]])
            tmp_tok[t] = P.op("dve", lambda E: E.tensor_tensor(out=tmpb[:, :].rearrange("p (g d) -> p g d", g=8), in0=usb[s3][:, :].rearrange("p (g d) -> p g d", g=8),
                                                               in1=SC(c + 3, 8).unsqueeze(2).broadcast_to([128, 8, 64]), op=ALU.mult), [t5])
            qn_tok[t] = P.op("dve", lambda E: E.tensor_tensor(out=qn[sl][:, :], in0=tmpb[:, :], in1=gqk8[:, :], op=ALU.mult), [tmp_tok[t], trq_tok.get(t - 2), t_gqk])

        def f_Tq(t):
            sl = t % 2
            qb = banks_bf[7]
            for cidx in range(4):
                tk = P.op("pe", lambda E, cidx=cidx: E.transpose(out=qb[:, cidx * 128:(cidx + 1) * 128], in_=qn[sl][:, cidx * 128:(cidx + 1) * 128], identity=identb[:, :]),
                          [qn_tok[t], qke_tok.get(t - 1)])
            trq_tok[t] = tk

        def f_qke(t):
            qb = banks_bf[7]
            qke_tok[t] = P.op("dve", lambda E: E.tensor_copy(out=QK[:, :, t * 128:(t + 1) * 128], in_=qb[:, 0:512].rearrange("p (c t) -> p c t", c=4)), [trq_tok[t]])

        def emit_step(s):
            ok = lambda t: 0 <= t < nt1
            for f, t in ((f_U, s - 3), (f_Tq, s - 6), (f_a1, s), (f_hs, s - 1), (f_T, s - 1), (f_sqrt8, s - 5), (f_hTe, s - 2), (f_sq, s - 4),
                         (f_a1_add, s), (f_a1_sqrt, s), (f_b2, s - 5), (f_a1_rcp, s), (f_ue, s - 3), (f_z1, s - 3), (f_qke, s - 6), (f_load, s + 1)):
                if ok(t):
                    f(t)

        f_load(0)
        for step in range(nt1 + 7):
            emit_step(step)

        p1_done = [qke_tok[nt1 - 1], qke_tok[nt1 - 2], vd_tok[nt1 - 1], vl_tok[nt1 - 1], silu_tok[0][(nt1 - 1) // 4], silu_tok[1][(nt1 - 1) // 4], z_tok[(nt1 - 1) // 4], u_tok[nt1 - 1], trq_tok[nt1 - 1]]

        fin = []
        stq = P.dsem("st_out")
        if stop == "p1":
            P.wait("sp", p1_done)
            for a in range(4):
                fin.append(P.dma("sp", DMA(dbg_qk[:, a * S_LEN:(a + 1) * S_LEN], QK[:, a, :]), stq, p1_done))
            for a in range(4):
                fin.append(P.dma("sp", DMA(dbg_vd[:, a * 2048:(a + 1) * 2048], Vd[:, a * 16:(a + 1) * 16, :].rearrange("p a t -> p (a t)")), stq))
                fin.append(P.dma("sp", DMA(dbg_vl[:, a * 2048:(a + 1) * 2048], Vl[:, a * 16:(a + 1) * 16, :].rearrange("p a t -> p (a t)")), stq))
            for a in range(2):
                fin.append(P.dma("sp", DMA(dbg_zt[:, a * S_LEN:(a + 1) * S_LEN], zT[:, a, :]), stq))

        ysem = [[P.dsem(f"ybin{j}_{p}") for p in range(2)] for j in range(4)]
        ccsem = P.dsem("cc")
        if stop != "p1":
            tsem = P.dsem("tabs")
            t_e = P.dma("sp", DMA(Etab[:, :], tabDd_d[:, :]), tsem, p1_done)
            t_g0 = P.dma("sp", DMA(Gtab[0][:, :], tabDl_d[:, :]), tsem)
            t_g1 = P.dma("sp", DMA(Gtab[1][:, :], tabDl_d[:, :]), tsem)
            t_m = P.dma("sp", DMA(Mtmp[:, :], tabMl_d[:, :]), tsem)
            tabs_ld = [t_e, t_g0, t_g1, t_m]
            tG = []
            for i in range(2):
                ta = P.op("act", lambda E, i=i: E.activation(out=Gtab[i][:, :], in_=Gtab[i][:, :], func=AF.Exp, scale=nsl[:, 1 + i:2 + i]), [tabs_ld, tc["nsl"]])
                tG.append(P.op("dve", lambda E, i=i: E.tensor_tensor(out=Gtab[i][:, :], in0=Gtab[i][:, :], in1=Mtmp[:, :], op=ALU.mult), [ta]))
            tE = P.op("act", lambda E: E.activation(out=Etab[:, :], in_=Etab[:, :], func=AF.Exp, scale=nsl[:, 0:1]), [tabs_ld])
            tGb = [P.op("dve", lambda E, i=i: E.tensor_copy(out=GtabB[i][:, :], in_=Gtab[i][:, :]), [tG]) for i in range(2)]
            tEb = P.op("dve", lambda E: E.tensor_copy(out=EtabB[:, :], in_=Etab[:, :]), [tE, tG])
            tabs_ready = [tGb, tEb, t_cb]

            SB = [[banks[0], banks[1]], [banks[2], banks[3]]]

            def attention(kind, extra=None):
                diff = kind == "diff"
                steps = []
                for q in range(nqt_dbg):
                    kb0 = q * 4
                    kbs = list(range(64)) if diff else [kb for kb in range(kb0 - 8, kb0 + 12) if 0 <= kb < 64]
                    for i, kb in enumerate(kbs):
                        steps.append((q, kb, i == 0, i == len(kbs) - 1))
                N = len(steps)
                exp_tok = {}
                mul_tok = {}
                av_tok = {}
                st8 = {"epi_free": None, "ss_free": None, "bk7_free": None}
                BK6, BK7 = banks[6], banks[7]
                if diff:
                    OB = [banks[4], banks[5]]
                    LB = [banks[6], banks[6]]
                else:
                    OB = [banks[4], banks[4]]
                    LB = [banks[5], banks[5]]
                qi = 0 if diff else 2
                ki = 1 if diff else 3

                def front(n):
                    q, kb, first, last = steps[n]
                    i0 = q * 512
                    j0 = kb * 128
                    par = n % 2
                    if diff:
                        if j0 + 128 <= i0:
                            off = 640
                            bias = cb[:, (i0 - j0 - 128) // 128:(i0 - j0 - 128) // 128 + 1]
                        elif j0 >= i0 + 512:
                            off = 0
                            bias = cb[:, (j0 - i0 - 512) // 128:(j0 - i0 - 512) // 128 + 1]
                        else:
                            off = 512 - (j0 - i0)
                            bias = cb[:, 0:1]
                    else:
                        off = 1408 - (j0 - i0)
                        bias = None
                    exp_tok[n] = []
                    mul_tok[n] = []
                    for s in range(2):
                        rows = slice(64 * s, 64 * s + 64)
                        Sb = SB[par][s]
                        prev = exp_tok[n - 2][s] if n >= 2 else None
                        ts = P.op("pe", lambda E, Sb=Sb, rows=rows: E.matmul(Sb[:, :], lhsT=QK[rows, ki, j0:j0 + 128], rhs=QK[rows, qi, i0:i0 + 512], start=True, stop=True),
                                  [prev, p1_done if n < 2 else None])
                        pm = mul_tok[n - 2][s] if n >= 2 else None
                        p32 = P32[par][s]
                        if bias is not None:
                            te = P.op("act", lambda E, Sb=Sb, p32=p32, bias=bias: E.activation(out=p32[:, :], in_=Sb[:, :], func=AF.Exp, bias=bias, scale=0.125), [ts, pm, tabs_ready if n < 2 else None])
                        else:
                            te = P.op("act", lambda E, Sb=Sb, p32=p32: E.activation(out=p32[:, :], in_=Sb[:, :], func=AF.Exp, scale=0.125), [ts, pm, tabs_ready if n < 2 else None])
                        exp_tok[n].append(te)
                        tab = EtabB if diff else GtabB[s]
                        pb = Pb[n % 3][s]
                        tm = P.op("dve", lambda E, p32=p32, pb=pb, tab=tab, off=off: E.tensor_tensor(out=pb[:, :], in0=p32[:, :], in1=tab[:, off:off + 512], op=ALU.mult),
                                  [te, av_tok.get(n - 3), tabs_ready if n < 3 else None])
                        mul_tok[n].append(tm)

                pending = []

                def run_pending():
                    for stages in list(pending):
                        stages.pop(0)()
                        if not stages:
                            pending.remove(stages)

                def back(n):
                    q, kb, first, last = steps[n]
                    i0 = q * 512
                    run_pending()
                    deps0 = [st8["epi_free"]] if first else []
                    for s in range(2):
                        pb = Pb[n % 3][s]
                        if diff:
                            P.op("pe", lambda E, pb=pb, s=s: E.matmul(OB[s][:, :], lhsT=Vd[:, kb, :], rhs=pb[:, :], start=first, stop=last), [mul_tok[n][s], deps0],
                                 pre=(lambda E: E.ldweights(Vd[:, kb, :])) if s == 0 else None)
                        else:
                            rows = slice(64 * s, 64 * s + 64)
                            P.op("pe", lambda E, pb=pb, s=s, rows=rows: E.matmul(OB[s][rows, :], lhsT=Vl[:, kb, 64 * s:64 * s + 64], rhs=pb[:, :], start=first, stop=last, tile_position=(0, 64 * s)),
                                 [mul_tok[n][s], deps0])
                    for s in range(2):
                        pb = Pb[n % 3][s]
                        if diff:
                            tk = P.op("pe", lambda E, pb=pb, s=s: E.matmul(BK6[32 * s:32 * s + 32, :], lhsT=onesb[:, 0:32], rhs=pb[:, :], start=first, stop=last, tile_position=(0, 32 * s)),
                                      [st8["ss_free"] if first else None, t_onesb])
                        else:
                            rows = slice(64 * s, 64 * s + 64)
                            tk = P.op("pe", lambda E, pb=pb, s=s, rows=rows: E.matmul(LB[s][rows, :], lhsT=onesb[:, 0:64], rhs=pb[:, :], start=first, stop=last, tile_position=(0, 64 * s)), [t_onesb])
                    av_tok[n] = tk
                    if not last:
                        return
                    j, qq = divmod(q, 4)
                    y = yo[q % 2]

                    def finish_store(ey, rows):
                        st8[("ydma", q % 2)] = P.dma("sp", DMA(ybin[j, rows, qq * 512:(qq + 1) * 512], y[:, :]), ysem[j][q % 2], [ey])
                        if diff and qq == 3:
                            P.wait("pool", [(ysem[j][0][0], ysem[j][0][1]), (ysem[j][1][0], ysem[j][1][1])])
                            ccsem[1] += 1
                            P.q["pool"].append(lambda E, j=j: E.collective_compute(
                                "AllGather", ALU.bypass, replica_groups=[[0, 1, 2, 3], [4, 5, 6, 7]],
                                ins=[ybin[j].opt()], outs=[ygat[j * 1024:(j + 1) * 1024, :].opt()]).then_inc(ccsem[0], 1))

                    if diff:
                        Rs, T0, R0s, R1s, T1, A = ET
                        ycur = y
                        e1p = []

                        def rec1(c):
                            e1p.append(P.op("dve", lambda E: E.reciprocal(out=Rs[0:64, c * 128:(c + 1) * 128], in_=BK6[0:64, c * 128:(c + 1) * 128]), [tk, st8.get("epi_done")]))

                        e1 = P.op("dve", lambda E: E.tensor_copy(out=A[0:64, :], in_=BK6[0:64, :]), [tk, st8.get("epi_done")])
                        st8["ss_free"] = e1
                        o0 = P.op("act", lambda E: E.activation(out=T0[:, :], in_=OB[0][:, :], func=AF.Copy), [tk, st8.get("epi_done")])
                        o1 = P.op("act", lambda E: E.activation(out=T1[:, :], in_=OB[1][:, :], func=AF.Copy), [tk])
                        st8["epi_free"] = [o0, o1]
                        ctx = {}

                        def r1(c):
                            def f():
                                e1p.append(P.op("dve", lambda E: E.reciprocal(out=Rs[0:64, c * 128:(c + 1) * 128], in_=A[0:64, c * 128:(c + 1) * 128]), [e1]))
                            return f

                        def s1():
                            ctx["b0"] = P.op("pe", lambda E: E.matmul(BK7[:, :], lhsT=sel[0:64, 0, :], rhs=Rs[0:64, :], start=True, stop=True), [e1p, st8["bk7_free"], t_sel])

                        def s2():
                            ctx["c0"] = P.op("act", lambda E: E.activation(out=R0s[:, :], in_=BK7[:, :], func=AF.Copy), [ctx["b0"]])

                        def s3():
                            ctx["b1"] = P.op("pe", lambda E: E.matmul(BK7[:, :], lhsT=sel[0:64, 1, :], rhs=Rs[0:64, :], start=True, stop=True), [ctx["c0"]])

                        def s4():
                            ctx["c1"] = P.op("act", lambda E: E.activation(out=R1s[:, :], in_=BK7[:, :], func=AF.Copy), [ctx["b1"]])

                        def s5():
                            e2 = P.op("dve", lambda E: E.tensor_tensor(out=T0[:, :], in0=T0[:, :], in1=R0s[:, :], op=ALU.mult), [ctx["c0"], o0])
                            e4 = P.op("dve", lambda E: E.tensor_tensor(out=T1[:, :], in0=T1[:, :], in1=R1s[:, :], op=ALU.mult), [ctx["c1"], o1])
                            ctx["e5"] = P.op("dve", lambda E: E.scalar_tensor_tensor(out=A[:, :], in0=T1[:, :], scalar=nlam[:, 0:1], in1=T0[:, :], op0=ALU.mult, op1=ALU.add), [e2, e4, t_nlam])

                        def s6():
                            ctx["e6"] = P.op("act", lambda E: E.activation(out=R0s[:, :], in_=A[:, :], func=AF.Square), [ctx["e5"]])

                        def s7():
                            ctx["e7"] = P.op("pe", lambda E: E.matmul(BK7[:, :], lhsT=ones32[:, :], rhs=R0s[:, :], start=True, stop=True), [ctx["e6"], ctx["c1"], t_ones32])

                        def s8():
                            ctx["e8"] = P.op("act", lambda E: E.activation(out=Rs[:, :], in_=BK7[:, :], func=AF.Ln, bias=epsc[:, 0:1], scale=1.0), [ctx["e7"], t_epsc])
                            st8["bk7_free"] = ctx["e8"]

                        def s9():
                            ctx["e9"] = P.op("act", lambda E: E.activation(out=T0[:, :], in_=Rs[:, :], func=AF.Exp, scale=-0.5), [ctx["e8"]])

                        def s10():
                            e11 = P.op("dve", lambda E: E.tensor_tensor(out=T1[:, :], in0=A[:, :], in1=T0[:, :], op=ALU.mult), [ctx["e9"]])
                            ey = P.op("dve", lambda E: E.scalar_tensor_tensor(out=ycur[:, :], in0=T1[:, :], scalar=gsubs[:, 0:1], in1=zT[:, 0, i0:i0 + 512], op0=ALU.mult, op1=ALU.mult),
                                      [e11, t_gs, st8.get(("ydma", q % 2))])
                            st8["epi_done"] = ey
                            finish_store(ey, slice(0, 128))

                        nop = lambda: None
                        pending.append([r1(0), r1(1), r1(2), r1(3), nop, nop, s1, nop, s2, nop, s3, nop, s4, nop, s5, nop, s6, nop, s7, nop, s8, nop, s9, nop, s10])
                        return
                        rows = slice(0, 128)
                    else:
                        Rs, T0, A = ET[0], ET[1], ET[5]
                        ycur = y
                        o0 = P.op("dve", lambda E: E.tensor_copy(out=T0[:, :], in_=OB[0][:, :]), [tk, st8.get("epi_done")])
                        o1 = P.op("dve", lambda E: E.tensor_copy(out=A[:, :], in_=LB[0][:, :]), [tk])
                        st8["epi_free"] = [o0, o1]
                        rp = []

                        def rr(c):
                            def f():
                                rp.append(P.op("dve", lambda E: E.reciprocal(out=Rs[:, c * 128:(c + 1) * 128], in_=A[:, c * 128:(c + 1) * 128]), [o1]))
                            return f

                        ctx = {}

                        def d1():
                            ctx["e2"] = P.op("dve", lambda E: E.tensor_tensor(out=T0[:, :], in0=T0[:, :], in1=Rs[:, :], op=ALU.mult), [rp, o0])

                        def d2():
                            ey = P.op("dve", lambda E: E.tensor_tensor(out=ycur[:, :], in0=T0[:, :], in1=zT[:, 1, i0:i0 + 512], op=ALU.mult), [ctx["e2"], st8.get(("ydma", q % 2))])
                            st8["epi_done"] = ey
                            finish_store(ey, slice(128, 256))

                        nop = lambda: None
                        pending.append([rr(0), rr(1), rr(2), rr(3), nop, d1, nop, d2])
                        return
                    finish_store(ey, rows)

                for n in range(N + 2):
                    if n < N:
                        front(n)
                        if extra is not None:
                            extra(n)
                    if n >= 2:
                        back(n - 2)
                while pending:
                    run_pending()
                return [av_tok[N - 1], mul_tok[N - 1], exp_tok[N - 1], st8["epi_free"], st8.get(("ydma", 0)), st8.get(("ydma", 1)), st8["ss_free"], st8["bk7_free"], st8.get("epi_done")]

            dil_done = attention("dil")
            if stop == "dil":
                P.wait("sp", dil_done)
            else:
                wsem3 = [P.dsem("w3a"), P.dsem("w3b")]
                w3_cast = [None, None]
                w3_tok = []
                jobs = [(Wo, k, w_out, None) for k in range(8)] + [(Wg, k, w_pg, k) for k in range(8)] + [(Wp, k, w_pp, None) for k in range(2)]

                def w3_job(ji):
                    dst, k, src, gk = jobs[ji]
                    sl = ji % 2
                    t_ld = P.dma("sp", DMA(wst[sl][:, :], src[k * 128:(k + 1) * 128, :]), wsem3[sl], [w3_cast[sl], dil_done if ji < 2 else None])
                    if gk is None:
                        w3_cast[sl] = P.op("act", lambda E: E.activation(out=dst[:, k, :], in_=wst[sl][:, :], func=AF.Copy), [t_ld])
                    else:
                        w3_cast[sl] = P.op("act", lambda E: E.activation(out=dst[:, k, :], in_=wst[sl][:, :], func=AF.Copy, scale=gple32[:, k:k + 1]), [t_ld, t_gple])
                    w3_tok.append(w3_cast[sl])

                def diff_extra(n):
                    if n % 6 == 3 and n // 6 < len(jobs):
                        w3_job(n // 6)

                diff_done = attention("diff", diff_extra)

            if stop in ("dil", "diff"):
                last = dil_done if stop == "dil" else diff_done
                P.wait("sp", last)
                for j in range(4):
                    P.wait("sp", [(ysem[j][0][0], ysem[j][0][1]), (ysem[j][1][0], ysem[j][1][1])])
                for j in range(4):
                    for r in range(2):
                        t1 = P.dma("sp", DMA(ET[0][:, :].bitcast(BF16)[:, 0:1024], ybin[j, r * 128:(r + 1) * 128, 0:1024]), stq, [fin[-1]] if fin else [])
                        P.wait("sp", [t1])
                        fin.append(P.dma("sp", DMA(dbg_y[j * 256 + r * 128:j * 256 + (r + 1) * 128, 0:1024], ET[0][:, :].bitcast(BF16)[:, 0:1024]), stq))
                        P.wait("sp", [fin[-1]])
                        t1 = P.dma("sp", DMA(ET[0][:, :].bitcast(BF16)[:, 0:1024], ybin[j, r * 128:(r + 1) * 128, 1024:2048]), stq)
                        P.wait("sp", [t1])
                        fin.append(P.dma("sp", DMA(dbg_y[j * 256 + r * 128:j * 256 + (r + 1) * 128, 1024:2048], ET[0][:, :].bitcast(BF16)[:, 0:1024]), stq))
                        P.wait("sp", [fin[-1]])

        if stop is None:
            P.wait("pool", [(ccsem[0], ccsem[1])])
            gsem = P.dsem("gath")
            gtok = []
            for c in range(8):
                gtok.append(P.dma("pool", lambda E, c=c: E.indirect_dma_start(
                    out=yTa[:, c, :], out_offset=None, in_=ygat[:, :],
                    in_offset=bass.IndirectOffsetOnAxis(ap=idxs[:, c:c + 1], axis=0)), gsem, [diff_done, tc["idx"]]))
            psem = [P.dsem("p3a"), P.dsem("p3b")]
            pc_tok = [None, None]
            p_tok = []
            for ji in range(4):
                c2, hf = divmod(ji, 2)
                sl = ji % 2
                t_ld = P.dma("sp", DMA(pst[sl][:, :], pT_d[c2 * 128:(c2 + 1) * 128, hf * 1024:(hf + 1) * 1024]), psem[sl], [pc_tok[sl], diff_done if ji < 2 else None])
                pc_tok[sl] = P.op("act", lambda E, c2=c2, hf=hf, sl=sl: E.activation(out=pTb[:, c2, hf * 1024:(hf + 1) * 1024], in_=pst[sl][:, :], func=AF.Copy), [t_ld])
                p_tok.append(pc_tok[sl])
            xosem = [P.dsem("xo0"), P.dsem("xo1")]
            osem = [P.dsem("o0"), P.dsem("o1")]
            x1e_tok, hs3_tok, tr3_tok, h2e_tok, gm_tok, pm_tok, sg_tok, fo_tok, od_tok, am_tok = ({} for _ in range(10))
            A0, A1, TB, G0, G1, PP0, PP1 = banks[0], banks[1], banks_bf[2], banks[3], banks[4], banks[5], banks[6]
            xl_tok = {}
            dx_tok = {}
            dxsem = P.dsem("dx")

            def load3(tt):
                if tt >= 16:
                    return
                sl = tt % 2
                xl_tok[tt] = P.dma("sp", DMA(xo[sl][:, :], xo_d[tt * 128:(tt + 1) * 128, :]), xosem[sl], [x1e_tok.get(tt - 2), diff_done if tt < 2 else None])

            rc3_tok = {}

            def stX(tt):
                sl, s3 = tt % 2, tt % 3
                c = 32 + 4 * (tt % 4)
                tsl = slice(tt * 128, (tt + 1) * 128)
                t_x = xl_tok[tt]
                for hf, AB in ((0, A0), (1, A1)):
                    for cc in range(8):
                        tk = P.op("pe", lambda E, hf=hf, AB=AB, cc=cc: E.matmul(AB[:, :], lhsT=yTa[:, cc, tsl], rhs=Wo[:, cc, hf * 512:(hf + 1) * 512], start=(cc == 0), stop=(cc == 7)),
                                  [gtok, w3_tok, x1e_tok.get(tt - 1)])
                am_tok[tt] = tk
                ta = P.op("dve", lambda E: E.tensor_tensor(out=x1[s3][:, 0:512], in0=A0[:, :], in1=xo[sl][:, 0:512], op=ALU.add), [tk, t_x, fo_tok.get(tt - 3), dx_tok.get(tt - 3)])
                x1e_tok[tt] = P.op("dve", lambda E: E.tensor_tensor(out=x1[s3][:, 512:1024], in0=A1[:, :], in1=xo[sl][:, 512:1024], op=ALU.add), [tk])
                load3(tt + 2)
                if dbg.get("dump_x1"):
                    dx_tok[tt] = P.dma("sp", DMA(dbg_x1[tsl, :], x1[s3][:, :]), dxsem, [ta, x1e_tok[tt]])
                    fin.append(dx_tok[tt])
                hs3_tok[tt] = P.op("act", lambda E: E.activation(out=h2[sl][:, :], in_=x1[s3][:, :], func=AF.Copy), [ta, x1e_tok[tt], tr3_tok.get(tt - 2)])
                t_sq = P.op("act", lambda E: E.activation(out=junk3[:, :], in_=x1[s3][:, :], func=AF.Square, accum_out=SC(c)), [ta, x1e_tok[tt]])
                t_a = P.op("dve", lambda E: E.tensor_scalar(out=SC(c + 1), in0=SC(c), scalar1=float(DM * EPS), scalar2=None, op0=ALU.add), [t_sq])
                t_b = P.op("act", lambda E: E.activation(out=SC(c + 2), in_=SC(c + 1), func=AF.Sqrt), [t_a])
                rc3_tok[tt] = P.op("dve", lambda E: E.reciprocal(out=SC(c + 3), in_=SC(c + 2)), [t_b])

            def stY1(tt):
                sl, s3 = tt % 2, tt % 3
                c = 32 + 4 * (tt % 4)
                for k in range(8):
                    tk = P.op("pe", lambda E, k=k: E.transpose(out=TB[:, k * 128:(k + 1) * 128], in_=h2[sl][:, k * 128:(k + 1) * 128], identity=identb[:, :]), [hs3_tok[tt], h2e_tok.get(tt - 1)])
                tr3_tok[tt] = tk
                h2e_tok[tt] = P.op("dve", lambda E: E.tensor_copy(out=h2T[sl][:, :, :], in_=TB[:, :].rearrange("p (k t) -> p k t", k=8)), [tk, gm_tok.get(tt - 2)])

            def stY2(tt):
                sl, s3 = tt % 2, tt % 3
                tsl = slice(tt * 128, (tt + 1) * 128)
                for hf, GB in ((0, G0), (1, G1)):
                    for k in range(8):
                        tk = P.op("pe", lambda E, hf=hf, GB=GB, k=k: E.matmul(GB[:, :], lhsT=h2T[sl][:, k, :], rhs=Wg[:, k, hf * 512:(hf + 1) * 512], start=(k == 0), stop=(k == 7)),
                                  [h2e_tok[tt], sg_tok.get(tt - 1)])
                gm_tok[tt] = tk
                for hf, PB in ((0, PP0), (1, PP1)):
                    for c2 in range(2):
                        tk = P.op("pe", lambda E, hf=hf, PB=PB, c2=c2: E.matmul(PB[:, :], lhsT=pTb[:, c2, tsl], rhs=Wp[:, c2, hf * 512:(hf + 1) * 512], start=(c2 == 0), stop=(c2 == 1)),
                                  [p_tok, fo_tok.get(tt - 1)])
                pm_tok[tt] = tk
                c = 32 + 4 * (tt % 4)
                s0 = P.op("act", lambda E: E.activation(out=gate[:, 0:512], in_=G0[:, :], func=AF.Sigmoid, scale=SC(c + 3)), [gm_tok[tt], fo_tok.get(tt - 1), rc3_tok[tt]])
                sg_tok[tt] = P.op("act", lambda E: E.activation(out=gate[:, 512:1024], in_=G1[:, :], func=AF.Sigmoid, scale=SC(c + 3)), [gm_tok[tt]])
                f0 = P.op("dve", lambda E: E.tensor_tensor(out=gate[:, 0:512], in0=gate[:, 0:512], in1=PP0[:, :], op=ALU.mult), [s0, pm_tok[tt]])
                f1 = P.op("dve", lambda E: E.tensor_tensor(out=gate[:, 512:1024], in0=gate[:, 512:1024], in1=PP1[:, :], op=ALU.mult), [sg_tok[tt], pm_tok[tt]])
                fo_tok[tt] = P.op("dve", lambda E: E.tensor_tensor(out=osb[sl][:, :], in0=gate[:, :], in1=x1[s3][:, :], op=ALU.add), [f0, f1, od_tok.get(tt - 2), hs3_tok[tt]])
                od_tok[tt] = P.dma("sp", DMA(out_d[tsl, :], osb[sl][:, :]), osem[sl], [fo_tok[tt]])
                fin.append(od_tok[tt])

            load3(0)
            load3(1)
            for step in range(16 + 2):
                if step < 16:
                    stX(step)
                if 1 <= step <= 16:
                    stY1(step - 1)
                if step >= 2:
                    stY2(step - 2)

        P.wait("sp", fin)
        with nc.Block() as block:
            P.replay(block)
    return nc


def _alibi(n):
    return np.exp2(-8.0 * np.arange(1, n + 1, dtype=np.float64) / n).astype(np.float32)


def _const_tables():
    p = np.arange(128)[:, None]
    n = np.arange(1152)[None, :]
    tabDd = np.abs(n - p - 512).astype(np.float32)
    n = np.arange(2944)[None, :]
    dl = n - p - 1408
    ad = np.abs(dl)
    tabDl = ad.astype(np.float32)
    mult = (ad <= 64).astype(np.float32) + ((dl % 4 == 0) & (ad <= 256)).astype(np.float32) + ((dl % 16 == 0) & (ad <= 1024)).astype(np.float32)
    mcol = np.broadcast_to((128.0 * np.arange(64, dtype=np.float32))[None, :], (128, 64)).copy()
    ident = np.eye(128, dtype=np.float32).astype(ml_dtypes.bfloat16)
    return tabDd, tabDl, mult.astype(np.float32), mcol, ident


def make_in_maps(x, p, mix_norm_g, w_in, diff_q_norm_g, diff_k_norm_g, lambda_q1, lambda_k1, lambda_q2, lambda_k2,
                 diff_sub_norm_g, dil_q_norm_g, dil_k_norm_g, w_out, ple_norm_g, w_ple_gate, w_ple_proj):
    f = lambda a: np.ascontiguousarray(np.asarray(a, dtype=np.float32))
    x, p = f(x), f(p)
    w_in0, w_out0, w_pg0, w_pp0 = f(w_in)[0], f(w_out)[0], f(w_ple_gate)[0], f(w_ple_proj)[0]
    tabDd, tabDl, tabMl, mcol, ident = _const_tables()
    sl_diff, sl_dil = _alibi(4), _alibi(8)
    bc = lambda v, n=128: np.ascontiguousarray(np.broadcast_to(np.asarray(v, np.float32)[None, :], (n, len(v))))
    gmix = np.ascontiguousarray(f(mix_norm_g)[0].reshape(8, 128).T)
    gple = np.ascontiguousarray(f(ple_norm_g)[0].reshape(8, 128).T)
    dq, dk, bq, bk = f(diff_q_norm_g)[0], f(diff_k_norm_g)[0], f(dil_q_norm_g)[0], f(dil_k_norm_g)[0]
    gqk = bc(np.concatenate([dq, dq, dk, dk, bq, bq, bk, bk]))
    lamv = bc(np.concatenate([f(lambda_q1)[0], f(lambda_k1)[0], f(lambda_q2)[0], f(lambda_k2)[0]]))
    gsub = np.ascontiguousarray(f(diff_sub_norm_g)[0][:, None])
    perm = np.concatenate([np.arange(pt * 512 + r * 128, pt * 512 + (r + 1) * 128) for r in range(4) for pt in range(2)])
    w_out_p = np.ascontiguousarray(w_out0[perm])
    maps = []
    for c in range(8):
        b, g = divmod(c, 4)
        cols = np.concatenate([np.arange(o + g * 128, o + (g + 1) * 128) for o in (0, 512, 1536, 2048, 1024, 2560, 3072, 3584)])
        nsl = bc(np.array([-sl_diff[g], -sl_dil[2 * g], -sl_dil[2 * g + 1]], np.float32))
        idx = (g * 1024 + np.arange(8)[None, :] * 128 + np.arange(128)[:, None]).astype(np.int32)
        maps.append({
            "xb": x[b], "xo": np.ascontiguousarray(x[b, g * 2048:(g + 1) * 2048]),
            "pT": np.ascontiguousarray(p[0, b, g * 2048:(g + 1) * 2048].T),
            "w_in": np.ascontiguousarray(w_in0[:, cols]), "w_out": w_out_p, "w_pg": w_pg0, "w_pp": w_pp0,
            "gmix": gmix, "gple": gple, "gqk": gqk, "lamv": lamv, "gsub": gsub, "nsl": nsl,
            "tabDd": tabDd, "tabDl": tabDl, "tabMl": tabMl, "mcol": mcol, "ident": ident, "idx": idx,
        })
    return maps


def kernel(**inputs):
    maps = make_in_maps(**inputs)
    nc = build_program(_DEBUG)
    res = run_bass_kernel_spmd(nc, maps, core_ids=list(range(8)))
    if _DEBUG.get("stop") or _DEBUG.get("dump_x1"):
        return res
    out = np.empty((2, S_LEN, DM), np.float32)
    for c in range(8):
        b, g = divmod(c, 4)
        out[b, g * 2048:(g + 1) * 2048] = np.asarray(res.results[c]["out"], dtype=np.float32)
    return out
```

```python
import numpy as np
import ml_dtypes
from contextlib import ExitStack
import concourse.bass as bass
import concourse.mybir as mybir
from concourse.bass_utils import run_bass_kernel_spmd

F32 = mybir.dt.float32
BF16 = mybir.dt.bfloat16
I32 = mybir.dt.int32
AF = mybir.ActivationFunctionType
ALU = mybir.AluOpType
AX = mybir.AxisListType

S_LEN = 8192
DM = 1024
NTILE = 64
NQT = 16
EPS = 1e-6
LAM_INIT = 0.8 - 0.6 * 1.0
ENGS = ("pe", "act", "dve", "pool", "sp")
SB_BASE = 16384

_DEBUG = {}


class Prog:
    def __init__(self, nc, stack):
        self.nc = nc
        self.stack = stack
        self.q = {e: [] for e in ENGS}
        self.esem = {e: stack.enter_context(nc.semaphore(f"es_{e}")) for e in ENGS}
        self.ecnt = {e: 0 for e in ENGS}
        self.waited = {e: {} for e in ENGS}
        self.nsem = 0
        self.ntens = 0

    def dsem(self, name=None):
        self.nsem += 1
        s = self.stack.enter_context(self.nc.semaphore(name or f"ds{self.nsem}"))
        return [s, 0]

    def sb(self, shape, dtype, off):
        self.ntens += 1
        h = self.nc.alloc_sbuf_tensor_at(f"sb{self.ntens}", list(shape), dtype, offset=SB_BASE + off)
        return h.ap()

    def _flat(self, deps):
        out = []
        for d in deps:
            if d is None:
                continue
            if isinstance(d, tuple) and len(d) == 2 and not isinstance(d[0], (tuple, list)):
                out.append(d)
            else:
                out.extend(self._flat(d))
        return out

    def _wait(self, eng, tok):
        sem, val = tok
        key = id(sem)
        if self.waited[eng].get(key, 0) >= val:
            return
        self.waited[eng][key] = val
        self.q[eng].append(lambda E, sem=sem, val=val: E.wait_ge(sem, val))

    def op(self, eng, fn, deps=(), pre=None):
        if pre is not None:
            self.q[eng].append(pre)
        for d in self._flat(deps):
            self._wait(eng, d)
        self.ecnt[eng] += 1
        n = self.ecnt[eng]
        sem = self.esem[eng]
        self.q[eng].append(lambda E, fn=fn, sem=sem: fn(E).then_inc(sem, 1))
        return (sem, n)

    def dma(self, eng, fn, ds, deps=()):
        for d in self._flat(deps):
            self._wait(eng, d)
        ds[1] += 16
        self.q[eng].append(lambda E, fn=fn, s=ds[0]: fn(E).then_inc(s, 16))
        return (ds[0], ds[1])

    def wait(self, eng, deps):
        for d in self._flat(deps):
            self._wait(eng, d)

    def replay(self, block):
        q = self.q

        @block.tensor
        def _(E):
            for f in q["pe"]:
                f(E)

        @block.scalar
        def _(E):
            for f in q["act"]:
                f(E)

        @block.vector
        def _(E):
            for f in q["dve"]:
                f(E)

        @block.gpsimd
        def _(E):
            for f in q["pool"]:
                f(E)

        @block.sync
        def _(E):
            for f in q["sp"]:
                f(E)


def DMA(out, in_):
    return lambda E: E.dma_start(out=out, in_=in_)


def build_program(dbg=None):
    dbg = dbg or {}
    stop = dbg.get("stop")
    nqt_dbg = dbg.get("nqt", NQT)
    nc = bass.Bass("TRN2", target_bir_lowering=False)

    def din(name, shape, dt=F32):
        return nc.dram_tensor(name, list(shape), dt, kind="ExternalInput").ap()

    def dout(name, shape, dt=F32):
        return nc.dram_tensor(name, list(shape), dt, kind="ExternalOutput").ap()

    xb = din("xb", [S_LEN, DM])
    xo_d = din("xo", [2048, DM])
    pT_d = din("pT", [256, 2048])
    w_in = din("w_in", [DM, 1024])
    w_out = din("w_out", [1024, DM])
    w_pg = din("w_pg", [DM, DM])
    w_pp = din("w_pp", [256, DM])
    gmix_d = din("gmix", [128, 8])
    gple_d = din("gple", [128, 8])
    gqk_d = din("gqk", [128, 512])
    lamv_d = din("lamv", [128, 256])
    gsub_d = din("gsub", [128, 1])
    nsl_d = din("nsl", [128, 3])
    tabDd_d = din("tabDd", [128, 1152])
    tabDl_d = din("tabDl", [128, 2944])
    tabMl_d = din("tabMl", [128, 2944])
    mcol_d = din("mcol", [128, 64])
    ident_d = din("ident", [128, 128], BF16)
    idx_d = din("idx", [128, 8], I32)
    out_d = dout("out", [2048, DM])
    ybin = nc.dram_tensor("ybin", [4, 256, 2048], BF16).ap()
    ygat = nc.dram_tensor("ygat", [4096, 2048], BF16).ap()
    if stop == "p1":
        dbg_qk = dout("dbg_qk", [128, 4 * S_LEN], BF16)
        dbg_vd = dout("dbg_vd", [128, 64 * 128], BF16)
        dbg_vl = dout("dbg_vl", [128, 64 * 128], BF16)
        dbg_zt = dout("dbg_zt", [128, 2 * S_LEN], BF16)
    if stop in ("dil", "diff"):
        dbg_y = dout("dbg_y", [4 * 256, 2048], BF16)
    if dbg.get("dump_x1"):
        dbg_x1 = dout("dbg_x1", [2048, DM])
        dbg_h2t = dout("dbg_h2t", [16 * 128, DM], BF16)

    with ExitStack() as st:
        P = Prog(nc, st)
        QK = P.sb([128, 4, S_LEN], BF16, 0)
        Vd = P.sb([128, 64, 128], BF16, 65536)
        Vl = P.sb([128, 64, 128], BF16, 81920)
        zT = P.sb([128, 2, S_LEN], BF16, 98304)
        C0 = 131072
        identb = P.sb([128, 128], BF16, C0)
        ones32 = P.sb([128, 128], F32, C0 + 256)
        onesb = P.sb([128, 128], BF16, C0 + 768)
        cb = P.sb([128, 64], F32, C0 + 1024)
        gmix32 = P.sb([128, 8], F32, C0 + 1280)
        gple32 = P.sb([128, 8], F32, C0 + 1408)
        nsl = P.sb([128, 3], F32, C0 + 1536)
        nlam = P.sb([128, 1], F32, C0 + 1664)
        gsubs = P.sb([128, 1], F32, C0 + 1792)
        idxs = P.sb([128, 8], I32, C0 + 1920)
        sc = P.sb([128, 1536], F32, 204800)

        def SC(slot, w=1):
            return sc[:, 32 * slot:32 * slot + w]
        gqk8 = P.sb([128, 512], F32, C0 + 2048)
        epsc = P.sb([128, 8], F32, 211968)
        sel = P.sb([64, 2, 128], F32, 210944)
        W0 = C0 + 4096
        Wi = P.sb([128, 8, 1024], BF16, W0)
        xs = [P.sb([128, 1024], F32, W0 + 16384 + 4096 * i) for i in range(4)]
        junk = P.sb([128, 1024], BF16, W0 + 32768)
        hb = [P.sb([128, 1024], BF16, W0 + 34816 + 2048 * i) for i in range(2)]
        hT = [P.sb([128, 8, 512], BF16, W0 + 38912 + 8192 * i) for i in range(2)]
        usb = [P.sb([128, 512], F32, W0 + 55296 + 2048 * i) for i in range(3)]
        sqb = P.sb([128, 512], F32, W0 + 61440)
        tmpb = P.sb([128, 512], F32, W0 + 63488)
        qn = [P.sb([128, 512], BF16, W0 + 65536 + 1024 * i) for i in range(2)]
        lamtmp = P.sb([128, 128], F32, W0 + 67584)
        wstg = [P.sb([128, 1024], F32, W0 + 38912 + 4096 * i) for i in range(4)]
        Etab = P.sb([128, 1152], F32, W0)
        Gtab = [P.sb([128, 2944], F32, W0 + 4608 + 11776 * i) for i in range(2)]
        Mtmp = P.sb([128, 2944], F32, W0 + 28160)
        P32 = [[P.sb([128, 512], BF16, W0 + 44032 + 1024 * (2 * i + s)) for s in range(2)] for i in range(2)]
        EtabB = P.sb([128, 1152], BF16, W0 + 28160)
        GtabB = [P.sb([128, 2944], BF16, W0 + 30464 + 5888 * i) for i in range(2)]
        Pb = [[P.sb([128, 512], BF16, W0 + 48128 + 1024 * (2 * i + s)) for s in range(2)] for i in range(3)]
        ET = [P.sb([128, 512], F32, W0 + 54272 + 2048 * i) for i in range(6)]
        yo = [P.sb([128, 512], BF16, W0 + 66560 + 1024 * i) for i in range(2)]
        Wo = P.sb([128, 8, 1024], BF16, 32768)
        Wg = P.sb([128, 8, 1024], BF16, 49152)
        Wp = P.sb([128, 2, 1024], BF16, 81920)
        wst = [P.sb([128, 1024], F32, 81920 + 4096 + 4096 * i) for i in range(2)]
        yTa = P.sb([128, 8, 2048], BF16, 0)
        pTb = P.sb([128, 2, 2048], BF16, 65536)
        xo = [P.sb([128, 1024], F32, 98304 + 4096 * i) for i in range(2)]
        x1 = [P.sb([128, 1024], F32, 98304 + 8192 + 4096 * i) for i in range(3)]
        osb = [P.sb([128, 1024], F32, 98304 + 20480 + 4096 * i) for i in range(2)]
        gate = P.sb([128, 1024], F32, 98304 + 28672)
        h2 = [P.sb([128, 1024], BF16, W0 + 2048 * i) for i in range(2)]
        h2T = [P.sb([128, 8, 128], BF16, W0 + 4096 + 2048 * i) for i in range(2)]
        junk3 = P.sb([128, 1024], BF16, W0 + 8192)
        pst = [P.sb([128, 1024], F32, W0 + 10240 + 4096 * i) for i in range(2)]

        banks = [nc.alloc_psum_tensor(f"bank{i}", [128, 512], F32).ap() for i in range(8)]
        banks_bf = [b.bitcast(BF16) for b in banks]

        ld = P.dsem("ld_const")
        tc = {}
        for name, dst, src in (("ident", identb, ident_d), ("gmix", gmix32, gmix_d), ("gple", gple32, gple_d),
                               ("gqk", gqk8, gqk_d), ("lamv", lamtmp[:, 0:128], lamv_d[:, 0:128]),
                               ("lamv2", tmpb[:, 0:128], lamv_d[:, 128:256]),
                               ("gsub", gsubs, gsub_d), ("nsl", nsl, nsl_d), ("mcol", cb, mcol_d), ("idx", idxs, idx_d)):
            tc[name] = P.dma("sp", DMA(dst, src), ld)
        for name in list(tc):
            tc[name] = (ld[0], ld[1])
        t_ones32 = P.op("pool", lambda E: E.memset(ones32[:, :], 1.0))
        t_epsc = P.op("pool", lambda E: E.memset(epsc[:, :], float(128 * EPS)))
        t_sel0 = P.op("pool", lambda E: E.memset(sel[:, :, :], 0.0))
        t_sel1 = P.op("pool", lambda E: E.memset(sel[0:1, 0, :], 1.0), [t_sel0])
        t_sel = P.op("pool", lambda E: E.memset(sel[32:33, 1, :], 1.0), [t_sel0, t_sel1])
        t_onesb = P.op("pool", lambda E: E.memset(onesb[:, :], 1.0))
        t_gmix = P.op("dve", lambda E: E.tensor_scalar(out=gmix32[:, :], in0=gmix32[:, :], scalar1=32.0, scalar2=None, op0=ALU.mult), [tc["gmix"]])
        t_gple = P.op("dve", lambda E: E.tensor_scalar(out=gple32[:, :], in0=gple32[:, :], scalar1=32.0, scalar2=None, op0=ALU.mult), [tc["gple"]])
        t_gqk = P.op("dve", lambda E: E.tensor_scalar(out=gqk8[:, :], in0=gqk8[:, :], scalar1=8.0, scalar2=None, op0=ALU.mult), [tc["gqk"]])
        t_cb = P.op("dve", lambda E: E.tensor_scalar(out=cb[:, :], in0=cb[:, :], scalar1=nsl[:, 0:1], scalar2=None, op0=ALU.mult), [tc["mcol"], tc["nsl"]])
        t_gs = P.op("dve", lambda E: E.tensor_scalar(out=gsubs[:, :], in0=gsubs[:, :], scalar1=float((1.0 - LAM_INIT) * np.sqrt(128.0)), scalar2=None, op0=ALU.mult), [tc["gsub"]])
        t_l1 = P.op("dve", lambda E: E.tensor_tensor(out=lamtmp[:, 0:64], in0=lamtmp[:, 0:64], in1=lamtmp[:, 64:128], op=ALU.mult), [tc["lamv"]])
        t_l2 = P.op("dve", lambda E: E.tensor_tensor(out=tmpb[:, 0:64], in0=tmpb[:, 0:64], in1=tmpb[:, 64:128], op=ALU.mult), [tc["lamv2"]])
        t_l3 = P.op("dve", lambda E: E.tensor_reduce(out=SC(40), in_=lamtmp[:, 0:64], axis=AX.X, op=ALU.add), [t_l1])
        t_l4 = P.op("dve", lambda E: E.tensor_reduce(out=SC(41), in_=tmpb[:, 0:64], axis=AX.X, op=ALU.add), [t_l2])
        t_l5a = P.op("act", lambda E: E.activation(out=SC(42), in_=SC(40), func=AF.Exp), [t_l3])
        t_l5 = P.op("act", lambda E: E.activation(out=SC(43), in_=SC(41), func=AF.Exp), [t_l4])
        t_l6 = P.op("dve", lambda E: E.tensor_tensor(out=SC(44), in0=SC(43), in1=SC(42), op=ALU.subtract), [t_l5, t_l5a])
        t_nlam = P.op("dve", lambda E: E.tensor_scalar(out=nlam[:, :], in0=SC(44), scalar1=float(-LAM_INIT), scalar2=None, op0=ALU.add), [t_l6])

        wsem = [P.dsem(f"wst{i}") for i in range(4)]
        wi_tok = []
        cast_tok = [None] * 4
        for k in range(8):
            sl = k % 4
            t_ld = P.dma("sp", DMA(wstg[sl][:, :], w_in[k * 128:(k + 1) * 128, :]), wsem[sl], [cast_tok[sl]])
            cast_tok[sl] = P.op("act", lambda E, k=k, sl=sl: E.activation(out=Wi[:, k, :], in_=wstg[sl][:, :], func=AF.Copy, scale=gmix32[:, k:k + 1]), [t_ld, t_gmix])
            wi_tok.append(cast_tok[sl])

        xsem = [P.dsem(f"xs{i}") for i in range(4)]
        hs_tok, tra_tok, hTe_tok, u_tok, ue_tok, vl_tok, tmp_tok, qn_tok, trq_tok, qke_tok = ({} for _ in range(10))
        rcp_tok, a8_tok = {}, {}
        z_tok = {}
        silu_tok = {0: {}, 1: {}}
        vd_tok = {}
        nt1 = dbg.get("ntile", NTILE)

        xl1_tok, sq1_tok, add1_tok, sqt1_tok, s8_tok = {}, {}, {}, {}, {}

        def f_load(t):
            sl4 = t % 4
            d0 = [hs_tok.get(t - 4)]
            xl1_tok[t] = P.dma("sp", DMA(xs[sl4][:, :], xb[t * 128:(t + 1) * 128, :]), xsem[sl4], d0)

        def f_a1(t):
            sl4 = t % 4
            c = 4 * sl4
            sq1_tok[t] = P.op("act", lambda E: E.activation(out=junk[:, :], in_=xs[sl4][:, :], func=AF.Square, accum_out=SC(c)), [xl1_tok[t]])

        def f_a1_add(t):
            c = 4 * (t % 4)
            add1_tok[t] = P.op("dve", lambda E: E.tensor_scalar(out=SC(c + 1), in0=SC(c), scalar1=float(DM * EPS), scalar2=None, op0=ALU.add), [sq1_tok[t]])

        def f_a1_sqrt(t):
            c = 4 * (t % 4)
            sqt1_tok[t] = P.op("act", lambda E: E.activation(out=SC(c + 2), in_=SC(c + 1), func=AF.Sqrt), [add1_tok[t]])

        def f_a1_rcp(t):
            c = 4 * (t % 4)
            rcp_tok[t] = P.op("dve", lambda E: E.reciprocal(out=SC(c + 3), in_=SC(c + 2)), [sqt1_tok[t]])

        def f_hs(t):
            sl, sl4 = t % 2, t % 4
            c = 4 * sl4
            hs_tok[t] = P.op("act", lambda E: E.activation(out=hb[sl][:, :], in_=xs[sl4][:, :], func=AF.Copy, scale=SC(c + 3)), [rcp_tok[t], tra_tok.get(t - 2)])

        def f_T(t):
            sl = t % 2
            tb = banks_bf[sl]
            for k in range(8):
                tk = P.op("pe", lambda E, k=k: E.transpose(out=tb[:, k * 128:(k + 1) * 128], in_=hb[sl][:, k * 128:(k + 1) * 128], identity=identb[:, :]),
                          [hs_tok[t], hTe_tok.get(t - 2), tc["ident"]])
            tra_tok[t] = tk

        def f_hTe(t):
            G, sub = divmod(t, 4)
            sl = t % 2
            tb = banks_bf[sl]
            dfree = []
            if G >= 2 and sub == 0:
                dfree = [u_tok[4 * (G - 2) + 3], z_tok[G - 2]]
            if t < 8:
                dfree = [dfree, wi_tok]
            hTe_tok[t] = P.op("dve", lambda E: E.tensor_copy(out=hT[G % 2][:, :, sub * 128:(sub + 1) * 128],
                                                             in_=tb[:, :].rearrange("p (k t) -> p k t", k=8)), [tra_tok[t], dfree])

        def f_U(t):
            G, sub = divmod(t, 4)
            sl = t % 2
            U0 = banks[2 + sl]
            U1 = banks[4 + sl]
            for k in range(8):
                tk = P.op("pe", lambda E, k=k: E.matmul(U0[:, :], lhsT=hT[G % 2][:, k, sub * 128:(sub + 1) * 128], rhs=Wi[:, k, 0:512], start=(k == 0), stop=(k == 7)),
                          [hTe_tok[t], wi_tok, ue_tok.get(t - 2)])
            for k in range(8):
                tk = P.op("pe", lambda E, k=k: E.matmul(U1[:, 0:256], lhsT=hT[G % 2][:, k, sub * 128:(sub + 1) * 128], rhs=Wi[:, k, 512:768], start=(k == 0), stop=(k == 7)),
                          [vl_tok.get(t - 2)])
            u_tok[t] = tk
            if sub == 3:
                f_z(t, 0)

        def f_z(t, zi):
            G = t // 4
            Z = banks[6]
            for k in range(8):
                tk = P.op("pe", lambda E, k=k: E.matmul(Z[:, :], lhsT=Wi[:, k, 768 + zi * 128:896 + zi * 128], rhs=hT[G % 2][:, k, :], start=(k == 0), stop=(k == 7)),
                          [silu_tok[1].get(G - 1) if zi == 0 else silu_tok[0][G], hTe_tok[t]])
            z_tok[(G, zi)] = tk
            if zi == 1:
                z_tok[G] = tk

        def f_silu(t, zi):
            G = t // 4
            Z = banks[6]
            silu_tok[zi][G] = P.op("act", lambda E: E.activation(out=zT[:, zi, G * 512:(G + 1) * 512], in_=Z[:, :], func=AF.Silu), [z_tok[(G, zi)]])

        def f_z1(t):
            if t % 4 == 3:
                f_z(t, 1)
                f_silu(t, 1)

        def f_ue(t):
            G, sub = divmod(t, 4)
            sl, s3 = t % 2, t % 3
            U0 = banks[2 + sl]
            U1 = banks[4 + sl]
            ue_tok[t] = P.op("act", lambda E: E.activation(out=usb[s3][:, :], in_=U0[:, :], func=AF.Copy), [u_tok[t], tmp_tok.get(t - 3)])
            vd_tok[t] = P.op("act", lambda E: E.activation(out=Vd[:, t, :], in_=U1[:, 0:128], func=AF.Copy), [u_tok[t]])
            vl_tok[t] = P.op("act", lambda E: E.activation(out=Vl[:, t, :], in_=U1[:, 128:256], func=AF.Copy), [u_tok[t]])
            if sub == 3:
                f_silu(t, 0)

        def f_sq(t):
            s3 = t % 3
            c = 16 + 4 * (t % 4)
            t1 = P.op("dve", lambda E: E.tensor_tensor(out=sqb[:, :], in0=usb[s3][:, :], in1=usb[s3][:, :], op=ALU.mult), [ue_tok[t]])
            t2 = P.op("dve", lambda E: E.tensor_reduce(out=SC(c, 8), in_=sqb[:, :].rearrange("p (g d) -> p g d", g=8), axis=AX.X, op=ALU.add), [t1])
            a8_tok[t] = P.op("dve", lambda E: E.tensor_scalar(out=SC(c + 1, 8), in0=SC(c, 8), scalar1=float(64 * EPS), scalar2=None, op0=ALU.add), [t2])

        def f_sqrt8(t):
            c = 16 + 4 * (t % 4)
            s8_tok[t] = P.op("act", lambda E: E.activation(out=SC(c + 2, 8), in_=SC(c + 1, 8), func=AF.Sqrt), [a8_tok[t]])

        def f_b2(t):
            sl, s3 = t % 2, t % 3
            c = 16 + 4 * (t % 4)
            t5 = P.op("dve", lambda E: E.reciprocal(out=SC(c + 3, 8), in_=SC(c + 2, 8)), [s8_tok[t]])
            tmp_tok[t] = P.op("dve", lambda E: E.tensor_tensor(out=tmpb[:, :].rearrange("p (g d) -> p g d", g=8), in0=usb[s3][:, :].rearrange("p (g d) -> p g d", g=8),
                                                               in1=SC(c + 3, 8).unsqueeze(2).broadcast_to([128, 8, 64]), op=ALU.mult), [t5])
            qn_tok[t] = P.op("dve", lambda E: E.tensor_tensor(out=qn[sl][:, :], in0=tmpb[:, :], in1=gqk8[:, :], op=ALU.mult), [tmp_tok[t], trq_tok.get(t - 2), t_gqk])

        def f_Tq(t):
            sl = t % 2
            qb = banks_bf[7]
            for cidx in range(4):
                tk = P.op("pe", lambda E, cidx=cidx: E.transpose(out=qb[:, cidx * 128:(cidx + 1) * 128], in_=qn[sl][:, cidx * 128:(cidx + 1) * 128], identity=identb[:, :]),
                          [qn_tok[t], qke_tok.get(t - 1)])
            trq_tok[t] = tk

        def f_qke(t):
            qb = banks_bf[7]
            qke_tok[t] = P.op("dve", lambda E: E.tensor_copy(out=QK[:, :, t * 128:(t + 1) * 128], in_=qb[:, 0:512].rearrange("p (c t) -> p c t", c=4)), [trq_tok[t]])

        def emit_step(s):
            ok = lambda t: 0 <= t < nt1
            for f, t in ((f_U, s - 3), (f_Tq, s - 6), (f_a1, s), (f_hs, s - 1), (f_T, s - 1), (f_sqrt8, s - 5), (f_hTe, s - 2), (f_sq, s - 4),
                         (f_a1_add, s), (f_a1_sqrt, s), (f_b2, s - 5), (f_a1_rcp, s), (f_ue, s - 3), (f_z1, s - 3), (f_qke, s - 6), (f_load, s + 1)):
                if ok(t):
                    f(t)

        f_load(0)
        for step in range(nt1 + 7):
            emit_step(step)

        p1_done = [qke_tok[nt1 - 1], qke_tok[nt1 - 2], vd_tok[nt1 - 1], vl_tok[nt1 - 1], silu_tok[0][(nt1 - 1) // 4], silu_tok[1][(nt1 - 1) // 4], z_tok[(nt1 - 1) // 4], u_tok[nt1 - 1], trq_tok[nt1 - 1]]

        fin = []
        stq = P.dsem("st_out")
        if stop == "p1":
            P.wait("sp", p1_done)
            for a in range(4):
                fin.append(P.dma("sp", DMA(dbg_qk[:, a * S_LEN:(a + 1) * S_LEN], QK[:, a, :]), stq, p1_done))
            for a in range(4):
                fin.append(P.dma("sp", DMA(dbg_vd[:, a * 2048:(a + 1) * 2048], Vd[:, a * 16:(a + 1) * 16, :].rearrange("p a t -> p (a t)")), stq))
                fin.append(P.dma("sp", DMA(dbg_vl[:, a * 2048:(a + 1) * 2048], Vl[:, a * 16:(a + 1) * 16, :].rearrange("p a t -> p (a t)")), stq))
            for a in range(2):
                fin.append(P.dma("sp", DMA(dbg_zt[:, a * S_LEN:(a + 1) * S_LEN], zT[:, a, :]), stq))

        ysem = [[P.dsem(f"ybin{j}_{p}") for p in range(2)] for j in range(4)]
        ccsem = P.dsem("cc")
        if stop != "p1":
            tsem = P.dsem("tabs")
            t_e = P.dma("sp", DMA(Etab[:, :], tabDd_d[:, :]), tsem, p1_done)
            t_g0 = P.dma("sp", DMA(Gtab[0][:, :], tabDl_d[:, :]), tsem)
            t_g1 = P.dma("sp", DMA(Gtab[1][:, :], tabDl_d[:, :]), tsem)
            t_m = P.dma("sp", DMA(Mtmp[:, :], tabMl_d[:, :]), tsem)
            tabs_ld = [t_e, t_g0, t_g1, t_m]
            tG = []
            for i in range(2):
                ta = P.op("act", lambda E, i=i: E.activation(out=Gtab[i][:, :], in_=Gtab[i][:, :], func=AF.Exp, scale=nsl[:, 1 + i:2 + i]), [tabs_ld, tc["nsl"]])
                tG.append(P.op("dve", lambda E, i=i: E.tensor_tensor(out=Gtab[i][:, :], in0=Gtab[i][:, :], in1=Mtmp[:, :], op=ALU.mult), [ta]))
            tE = P.op("act", lambda E: E.activation(out=Etab[:, :], in_=Etab[:, :], func=AF.Exp, scale=nsl[:, 0:1]), [tabs_ld])
            tGb = [P.op("dve", lambda E, i=i: E.tensor_copy(out=GtabB[i][:, :], in_=Gtab[i][:, :]), [tG]) for i in range(2)]
            tEb = P.op("dve", lambda E: E.tensor_copy(out=EtabB[:, :], in_=Etab[:, :]), [tE, tG])
            tabs_ready = [tGb, tEb, t_cb]

            SB = [[banks[0], banks[1]], [banks[2], banks[3]]]

            def attention(kind, extra=None):
                diff = kind == "diff"
                steps = []
                for q in range(nqt_dbg):
                    kb0 = q * 4
                    kbs = list(range(64)) if diff else [kb for kb in range(kb0 - 8, kb0 + 12) if 0 <= kb < 64]
                    for i, kb in enumerate(kbs):
                        steps.append((q, kb, i == 0, i == len(kbs) - 1))
                N = len(steps)
                exp_tok = {}
                mul_tok = {}
                av_tok = {}
                st8 = {"epi_free": None, "ss_free": None, "bk7_free": None}
                BK6, BK7 = banks[6], banks[7]
                if diff:
                    OB = [banks[4], banks[5]]
                    LB = [banks[6], banks[6]]
                else:
                    OB = [banks[4], banks[4]]
                    LB = [banks[5], banks[5]]
                qi = 0 if diff else 2
                ki = 1 if diff else 3

                def front(n):
                    q, kb, first, last = steps[n]
                    i0 = q * 512
                    j0 = kb * 128
                    par = n % 2
                    if diff:
                        if j0 + 128 <= i0:
                            off = 640
                            bias = cb[:, (i0 - j0 - 128) // 128:(i0 - j0 - 128) // 128 + 1]
                        elif j0 >= i0 + 512:
                            off = 0
                            bias = cb[:, (j0 - i0 - 512) // 128:(j0 - i0 - 512) // 128 + 1]
                        else:
                            off = 512 - (j0 - i0)
                            bias = cb[:, 0:1]
                    else:
                        off = 1408 - (j0 - i0)
                        bias = None
                    exp_tok[n] = []
                    mul_tok[n] = []
                    for s in range(2):
                        rows = slice(64 * s, 64 * s + 64)
                        Sb = SB[par][s]
                        prev = exp_tok[n - 2][s] if n >= 2 else None
                        ts = P.op("pe", lambda E, Sb=Sb, rows=rows: E.matmul(Sb[:, :], lhsT=QK[rows, ki, j0:j0 + 128], rhs=QK[rows, qi, i0:i0 + 512], start=True, stop=True),
                                  [prev, p1_done if n < 2 else None])
                        pm = mul_tok[n - 2][s] if n >= 2 else None
                        p32 = P32[par][s]
                        if bias is not None:
                            te = P.op("act", lambda E, Sb=Sb, p32=p32, bias=bias: E.activation(out=p32[:, :], in_=Sb[:, :], func=AF.Exp, bias=bias, scale=0.125), [ts, pm, tabs_ready if n < 2 else None])
                        else:
                            te = P.op("act", lambda E, Sb=Sb, p32=p32: E.activation(out=p32[:, :], in_=Sb[:, :], func=AF.Exp, scale=0.125), [ts, pm, tabs_ready if n < 2 else None])
                        exp_tok[n].append(te)
                        tab = EtabB if diff else GtabB[s]
                        pb = Pb[n % 3][s]
                        tm = P.op("dve", lambda E, p32=p32, pb=pb, tab=tab, off=off: E.tensor_tensor(out=pb[:, :], in0=p32[:, :], in1=tab[:, off:off + 512], op=ALU.mult),
                                  [te, av_tok.get(n - 3), tabs_ready if n < 3 else None])
                        mul_tok[n].append(tm)

                pending = []

                def run_pending():
                    for stages in list(pending):
                        stages.pop(0)()
                        if not stages:
                            pending.remove(stages)

                def back(n):
                    q, kb, first, last = steps[n]
                    i0 = q * 512
                    run_pending()
                    deps0 = [st8["epi_free"]] if first else []
                    for s in range(2):
                        pb = Pb[n % 3][s]
                        if diff:
                            P.op("pe", lambda E, pb=pb, s=s: E.matmul(OB[s][:, :], lhsT=Vd[:, kb, :], rhs=pb[:, :], start=first, stop=last), [mul_tok[n][s], deps0],
                                 pre=(lambda E: E.ldweights(Vd[:, kb, :])) if s == 0 else None)
                        else:
                            rows = slice(64 * s, 64 * s + 64)
                            P.op("pe", lambda E, pb=pb, s=s, rows=rows: E.matmul(OB[s][rows, :], lhsT=Vl[:, kb, 64 * s:64 * s + 64], rhs=pb[:, :], start=first, stop=last, tile_position=(0, 64 * s)),
                                 [mul_tok[n][s], deps0])
                    for s in range(2):
                        pb = Pb[n % 3][s]
                        if diff:
                            tk = P.op("pe", lambda E, pb=pb, s=s: E.matmul(BK6[32 * s:32 * s + 32, :], lhsT=onesb[:, 0:32], rhs=pb[:, :], start=first, stop=last, tile_position=(0, 32 * s)),
                                      [st8["ss_free"] if first else None, t_onesb])
                        else:
                            rows = slice(64 * s, 64 * s + 64)
                            tk = P.op("pe", lambda E, pb=pb, s=s, rows=rows: E.matmul(LB[s][rows, :], lhsT=onesb[:, 0:64], rhs=pb[:, :], start=first, stop=last, tile_position=(0, 64 * s)), [t_onesb])
                    av_tok[n] = tk
                    if not last:
                        return
                    j, qq = divmod(q, 4)
                    y = yo[q % 2]

                    def finish_store(ey, rows):
                        st8[("ydma", q % 2)] = P.dma("sp", DMA(ybin[j, rows, qq * 512:(qq + 1) * 512], y[:, :]), ysem[j][q % 2], [ey])
                        if diff and qq == 3:
                            P.wait("pool", [(ysem[j][0][0], ysem[j][0][1]), (ysem[j][1][0], ysem[j][1][1])])
                            ccsem[1] += 1
                            P.q["pool"].append(lambda E, j=j: E.collective_compute(
                                "AllGather", ALU.bypass, replica_groups=[[0, 1, 2, 3], [4, 5, 6, 7]],
                                ins=[ybin[j].opt()], outs=[ygat[j * 1024:(j + 1) * 1024, :].opt()]).then_inc(ccsem[0], 1))

                    if diff:
                        Rs, T0, R0s, R1s, T1, A = ET
                        ycur = y
                        e1p = []

                        def rec1(c):
                            e1p.append(P.op("dve", lambda E: E.reciprocal(out=Rs[0:64, c * 128:(c + 1) * 128], in_=BK6[0:64, c * 128:(c + 1) * 128]), [tk, st8.get("epi_done")]))

                        e1 = P.op("dve", lambda E: E.tensor_copy(out=A[0:64, :], in_=BK6[0:64, :]), [tk, st8.get("epi_done")])
                        st8["ss_free"] = e1
                        o0 = P.op("act", lambda E: E.activation(out=T0[:, :], in_=OB[0][:, :], func=AF.Copy), [tk, st8.get("epi_done")])
                        o1 = P.op("act", lambda E: E.activation(out=T1[:, :], in_=OB[1][:, :], func=AF.Copy), [tk])
                        st8["epi_free"] = [o0, o1]
                        ctx = {}

                        def r1(c):
                            def f():
                                e1p.append(P.op("dve", lambda E: E.reciprocal(out=Rs[0:64, c * 128:(c + 1) * 128], in_=A[0:64, c * 128:(c + 1) * 128]), [e1]))
                            return f

                        def s1():
                            ctx["b0"] = P.op("pe", lambda E: E.matmul(BK7[:, :], lhsT=sel[0:64, 0, :], rhs=Rs[0:64, :], start=True, stop=True), [e1p, st8["bk7_free"], t_sel])

                        def s2():
                            ctx["c0"] = P.op("act", lambda E: E.activation(out=R0s[:, :], in_=BK7[:, :], func=AF.Copy), [ctx["b0"]])

                        def s3():
                            ctx["b1"] = P.op("pe", lambda E: E.matmul(BK7[:, :], lhsT=sel[0:64, 1, :], rhs=Rs[0:64, :], start=True, stop=True), [ctx["c0"]])

                        def s4():
                            ctx["c1"] = P.op("act", lambda E: E.activation(out=R1s[:, :], in_=BK7[:, :], func=AF.Copy), [ctx["b1"]])

                        def s5():
                            e2 = P.op("dve", lambda E: E.tensor_tensor(out=T0[:, :], in0=T0[:, :], in1=R0s[:, :], op=ALU.mult), [ctx["c0"], o0])
                            e4 = P.op("dve", lambda E: E.tensor_tensor(out=T1[:, :], in0=T1[:, :], in1=R1s[:, :], op=ALU.mult), [ctx["c1"], o1])
                            ctx["e5"] = P.op("dve", lambda E: E.scalar_tensor_tensor(out=A[:, :], in0=T1[:, :], scalar=nlam[:, 0:1], in1=T0[:, :], op0=ALU.mult, op1=ALU.add), [e2, e4, t_nlam])

                        def s6():
                            ctx["e6"] = P.op("act", lambda E: E.activation(out=R0s[:, :], in_=A[:, :], func=AF.Square), [ctx["e5"]])

                        def s7():
                            ctx["e7"] = P.op("pe", lambda E: E.matmul(BK7[:, :], lhsT=ones32[:, :], rhs=R0s[:, :], start=True, stop=True), [ctx["e6"], ctx["c1"], t_ones32])

                        def s8():
                            ctx["e8"] = P.op("act", lambda E: E.activation(out=Rs[:, :], in_=BK7[:, :], func=AF.Ln, bias=epsc[:, 0:1], scale=1.0), [ctx["e7"], t_epsc])
                            st8["bk7_free"] = ctx["e8"]

                        def s9():
                            ctx["e9"] = P.op("act", lambda E: E.activation(out=T0[:, :], in_=Rs[:, :], func=AF.Exp, scale=-0.5), [ctx["e8"]])

                        def s10():
                            e11 = P.op("dve", lambda E: E.tensor_tensor(out=T1[:, :], in0=A[:, :], in1=T0[:, :], op=ALU.mult), [ctx["e9"]])
                            ey = P.op("dve", lambda E: E.scalar_tensor_tensor(out=ycur[:, :], in0=T1[:, :], scalar=gsubs[:, 0:1], in1=zT[:, 0, i0:i0 + 512], op0=ALU.mult, op1=ALU.mult),
                                      [e11, t_gs, st8.get(("ydma", q % 2))])
                            st8["epi_done"] = ey
                            finish_store(ey, slice(0, 128))

                        nop = lambda: None
                        pending.append([r1(0), r1(1), r1(2), r1(3), nop, nop, s1, nop, s2, nop, s3, nop, s4, nop, s5, nop, s6, nop, s7, nop, s8, nop, s9, nop, s10])
                        return
                        rows = slice(0, 128)
                    else:
                        Rs, T0, A = ET[0], ET[1], ET[5]
                        ycur = y
                        o0 = P.op("dve", lambda E: E.tensor_copy(out=T0[:, :], in_=OB[0][:, :]), [tk, st8.get("epi_done")])
                        o1 = P.op("dve", lambda E: E.tensor_copy(out=A[:, :], in_=LB[0][:, :]), [tk])
                        st8["epi_free"] = [o0, o1]
                        rp = []

                        def rr(c):
                            def f():
                                rp.append(P.op("dve", lambda E: E.reciprocal(out=Rs[:, c * 128:(c + 1) * 128], in_=A[:, c * 128:(c + 1) * 128]), [o1]))
                            return f

                        ctx = {}

                        def d1():
                            ctx["e2"] = P.op("dve", lambda E: E.tensor_tensor(out=T0[:, :], in0=T0[:, :], in1=Rs[:, :], op=ALU.mult), [rp, o0])

                        def d2():
                            ey = P.op("dve", lambda E: E.tensor_tensor(out=ycur[:, :], in0=T0[:, :], in1=zT[:, 1, i0:i0 + 512], op=ALU.mult), [ctx["e2"], st8.get(("ydma", q % 2))])
                            st8["epi_done"] = ey
                            finish_store(ey, slice(128, 256))

                        nop = lambda: None
                        pending.append([rr(0), rr(1), rr(2), rr(3), nop, d1, nop, d2])
                        return
                    finish_store(ey, rows)

                for n in range(N + 2):
                    if n < N:
                        front(n)
                        if extra is not None:
                            extra(n)
                    if n >= 2:
                        back(n - 2)
                while pending:
                    run_pending()
                return [av_tok[N - 1], mul_tok[N - 1], exp_tok[N - 1], st8["epi_free"], st8.get(("ydma", 0)), st8.get(("ydma", 1)), st8["ss_free"], st8["bk7_free"], st8.get("epi_done")]

            dil_done = attention("dil")
            if stop == "dil":
                P.wait("sp", dil_done)
            else:
                wsem3 = [P.dsem("w3a"), P.dsem("w3b")]
                w3_cast = [None, None]
                w3_tok = []
                jobs = [(Wo, k, w_out, None) for k in range(8)] + [(Wg, k, w_pg, k) for k in range(8)] + [(Wp, k, w_pp, None) for k in range(2)]

                def w3_job(ji):
                    dst, k, src, gk = jobs[ji]
                    sl = ji % 2
                    t_ld = P.dma("sp", DMA(wst[sl][:, :], src[k * 128:(k + 1) * 128, :]), wsem3[sl], [w3_cast[sl], dil_done if ji < 2 else None])
                    if gk is None:
                        w3_cast[sl] = P.op("act", lambda E: E.activation(out=dst[:, k, :], in_=wst[sl][:, :], func=AF.Copy), [t_ld])
                    else:
                        w3_cast[sl] = P.op("act", lambda E: E.activation(out=dst[:, k, :], in_=wst[sl][:, :], func=AF.Copy, scale=gple32[:, k:k + 1]), [t_ld, t_gple])
                    w3_tok.append(w3_cast[sl])

                def diff_extra(n):
                    if n % 6 == 3 and n // 6 < len(jobs):
                        w3_job(n // 6)

                diff_done = attention("diff", diff_extra)

            if stop in ("dil", "diff"):
                last = dil_done if stop == "dil" else diff_done
                P.wait("sp", last)
                for j in range(4):
                    P.wait("sp", [(ysem[j][0][0], ysem[j][0][1]), (ysem[j][1][0], ysem[j][1][1])])
                for j in range(4):
                    for r in range(2):
                        t1 = P.dma("sp", DMA(ET[0][:, :].bitcast(BF16)[:, 0:1024], ybin[j, r * 128:(r + 1) * 128, 0:1024]), stq, [fin[-1]] if fin else [])
                        P.wait("sp", [t1])
                        fin.append(P.dma("sp", DMA(dbg_y[j * 256 + r * 128:j * 256 + (r + 1) * 128, 0:1024], ET[0][:, :].bitcast(BF16)[:, 0:1024]), stq))
                        P.wait("sp", [fin[-1]])
                        t1 = P.dma("sp", DMA(ET[0][:, :].bitcast(BF16)[:, 0:1024], ybin[j, r * 128:(r + 1) * 128, 1024:2048]), stq)
                        P.wait("sp", [t1])
                        fin.append(P.dma("sp", DMA(dbg_y[j * 256 + r * 128:j * 256 + (r + 1) * 128, 1024:2048], ET[0][:, :].bitcast(BF16)[:, 0:1024]), stq))
                        P.wait("sp", [fin[-1]])

        if stop is None:
            P.wait("pool", [(ccsem[0], ccsem[1])])
            gsem = P.dsem("gath")
            gtok = []
            for c in range(8):
                gtok.append(P.dma("pool", lambda E, c=c: E.indirect_dma_start(
                    out=yTa[:, c, :], out_offset=None, in_=ygat[:, :],
                    in_offset=bass.IndirectOffsetOnAxis(ap=idxs[:, c:c + 1], axis=0)), gsem, [diff_done, tc["idx"]]))
            psem = [P.dsem("p3a"), P.dsem("p3b")]
            pc_tok = [None, None]
            p_tok = []
            for ji in range(4):
                c2, hf = divmod(ji, 2)
                sl = ji % 2
                t_ld = P.dma("sp", DMA(pst[sl][:, :], pT_d[c2 * 128:(c2 + 1) * 128, hf * 1024:(hf + 1) * 1024]), psem[sl], [pc_tok[sl], diff_done if ji < 2 else None])
                pc_tok[sl] = P.op("act", lambda E, c2=c2, hf=hf, sl=sl: E.activation(out=pTb[:, c2, hf * 1024:(hf + 1) * 1024], in_=pst[sl][:, :], func=AF.Copy), [t_ld])
                p_tok.append(pc_tok[sl])
            xosem = [P.dsem("xo0"), P.dsem("xo1")]
            osem = [P.dsem("o0"), P.dsem("o1")]
            x1e_tok, hs3_tok, tr3_tok, h2e_tok, gm_tok, pm_tok, sg_tok, fo_tok, od_tok, am_tok = ({} for _ in range(10))
            A0, A1, TB, G0, G1, PP0, PP1 = banks[0], banks[1], banks_bf[2], banks[3], banks[4], banks[5], banks[6]
            xl_tok = {}
            dx_tok = {}
            dxsem = P.dsem("dx")

            def load3(tt):
                if tt >= 16:
                    return
                sl = tt % 2
                xl_tok[tt] = P.dma("sp", DMA(xo[sl][:, :], xo_d[tt * 128:(tt + 1) * 128, :]), xosem[sl], [x1e_tok.get(tt - 2), diff_done if tt < 2 else None])

            rc3_tok = {}

            def stX(tt):
                sl, s3 = tt % 2, tt % 3
                c = 32 + 4 * (tt % 4)
                tsl = slice(tt * 128, (tt + 1) * 128)
                t_x = xl_tok[tt]
                for hf, AB in ((0, A0), (1, A1)):
                    for cc in range(8):
                        tk = P.op("pe", lambda E, hf=hf, AB=AB, cc=cc: E.matmul(AB[:, :], lhsT=yTa[:, cc, tsl], rhs=Wo[:, cc, hf * 512:(hf + 1) * 512], start=(cc == 0), stop=(cc == 7)),
                                  [gtok, w3_tok, x1e_tok.get(tt - 1)])
                am_tok[tt] = tk
                ta = P.op("dve", lambda E: E.tensor_tensor(out=x1[s3][:, 0:512], in0=A0[:, :], in1=xo[sl][:, 0:512], op=ALU.add), [tk, t_x, fo_tok.get(tt - 3), dx_tok.get(tt - 3)])
                x1e_tok[tt] = P.op("dve", lambda E: E.tensor_tensor(out=x1[s3][:, 512:1024], in0=A1[:, :], in1=xo[sl][:, 512:1024], op=ALU.add), [tk])
                load3(tt + 2)
                if dbg.get("dump_x1"):
                    dx_tok[tt] = P.dma("sp", DMA(dbg_x1[tsl, :], x1[s3][:, :]), dxsem, [ta, x1e_tok[tt]])
                    fin.append(dx_tok[tt])
                hs3_tok[tt] = P.op("act", lambda E: E.activation(out=h2[sl][:, :], in_=x1[s3][:, :], func=AF.Copy), [ta, x1e_tok[tt], tr3_tok.get(tt - 2)])
                t_sq = P.op("act", lambda E: E.activation(out=junk3[:, :], in_=x1[s3][:, :], func=AF.Square, accum_out=SC(c)), [ta, x1e_tok[tt]])
                t_a = P.op("dve", lambda E: E.tensor_scalar(out=SC(c + 1), in0=SC(c), scalar1=float(DM * EPS), scalar2=None, op0=ALU.add), [t_sq])
                t_b = P.op("act", lambda E: E.activation(out=SC(c + 2), in_=SC(c + 1), func=AF.Sqrt), [t_a])
                rc3_tok[tt] = P.op("dve", lambda E: E.reciprocal(out=SC(c + 3), in_=SC(c + 2)), [t_b])

            def stY1(tt):
                sl, s3 = tt % 2, tt % 3
                c = 32 + 4 * (tt % 4)
                for k in range(8):
                    tk = P.op("pe", lambda E, k=k: E.transpose(out=TB[:, k * 128:(k + 1) * 128], in_=h2[sl][:, k * 128:(k + 1) * 128], identity=identb[:, :]), [hs3_tok[tt], h2e_tok.get(tt - 1)])
                tr3_tok[tt] = tk
                h2e_tok[tt] = P.op("dve", lambda E: E.tensor_copy(out=h2T[sl][:, :, :], in_=TB[:, :].rearrange("p (k t) -> p k t", k=8)), [tk, gm_tok.get(tt - 2)])

            def stY2(tt):
                sl, s3 = tt % 2, tt % 3
                tsl = slice(tt * 128, (tt + 1) * 128)
                for hf, GB in ((0, G0), (1, G1)):
                    for k in range(8):
                        tk = P.op("pe", lambda E, hf=hf, GB=GB, k=k: E.matmul(GB[:, :], lhsT=h2T[sl][:, k, :], rhs=Wg[:, k, hf * 512:(hf + 1) * 512], start=(k == 0), stop=(k == 7)),
                                  [h2e_tok[tt], sg_tok.get(tt - 1)])
                gm_tok[tt] = tk
                for hf, PB in ((0, PP0), (1, PP1)):
                    for c2 in range(2):
                        tk = P.op("pe", lambda E, hf=hf, PB=PB, c2=c2: E.matmul(PB[:, :], lhsT=pTb[:, c2, tsl], rhs=Wp[:, c2, hf * 512:(hf + 1) * 512], start=(c2 == 0), stop=(c2 == 1)),
                                  [p_tok, fo_tok.get(tt - 1)])
                pm_tok[tt] = tk
                c = 32 + 4 * (tt % 4)
                s0 = P.op("act", lambda E: E.activation(out=gate[:, 0:512], in_=G0[:, :], func=AF.Sigmoid, scale=SC(c + 3)), [gm_tok[tt], fo_tok.get(tt - 1), rc3_tok[tt]])
                sg_tok[tt] = P.op("act", lambda E: E.activation(out=gate[:, 512:1024], in_=G1[:, :], func=AF.Sigmoid, scale=SC(c + 3)), [gm_tok[tt]])
                f0 = P.op("dve", lambda E: E.tensor_tensor(out=gate[:, 0:512], in0=gate[:, 0:512], in1=PP0[:, :], op=ALU.mult), [s0, pm_tok[tt]])
                f1 = P.op("dve", lambda E: E.tensor_tensor(out=gate[:, 512:1024], in0=gate[:, 512:1024], in1=PP1[:, :], op=ALU.mult), [sg_tok[tt], pm_tok[tt]])
                fo_tok[tt] = P.op("dve", lambda E: E.tensor_tensor(out=osb[sl][:, :], in0=gate[:, :], in1=x1[s3][:, :], op=ALU.add), [f0, f1, od_tok.get(tt - 2), hs3_tok[tt]])
                od_tok[tt] = P.dma("sp", DMA(out_d[tsl, :], osb[sl][:, :]), osem[sl], [fo_tok[tt]])
                fin.append(od_tok[tt])

            load3(0)
            load3(1)
            for step in range(16 + 2):
                if step < 16:
                    stX(step)
                if 1 <= step <= 16:
                    stY1(step - 1)
                if step >= 2:
                    stY2(step - 2)

        P.wait("sp", fin)
        with nc.Block() as block:
            P.replay(block)
    return nc


def _alibi(n):
    return np.exp2(-8.0 * np.arange(1, n + 1, dtype=np.float64) / n).astype(np.float32)


def _const_tables():
    p = np.arange(128)[:, None]
    n = np.arange(1152)[None, :]
    tabDd = np.abs(n - p - 512).astype(np.float32)
    n = np.arange(2944)[None, :]
    dl = n - p - 1408
    ad = np.abs(dl)
    tabDl = ad.astype(np.float32)
    mult = (ad <= 64).astype(np.float32) + ((dl % 4 == 0) & (ad <= 256)).astype(np.float32) + ((dl % 16 == 0) & (ad <= 1024)).astype(np.float32)
    mcol = np.broadcast_to((128.0 * np.arange(64, dtype=np.float32))[None, :], (128, 64)).copy()
    ident = np.eye(128, dtype=np.float32).astype(ml_dtypes.bfloat16)
    return tabDd, tabDl, mult.astype(np.float32), mcol, ident


def make_in_maps(x, p, mix_norm_g, w_in, diff_q_norm_g, diff_k_norm_g, lambda_q1, lambda_k1, lambda_q2, lambda_k2,
                 diff_sub_norm_g, dil_q_norm_g, dil_k_norm_g, w_out, ple_norm_g, w_ple_gate, w_ple_proj):
    f = lambda a: np.ascontiguousarray(np.asarray(a, dtype=np.float32))
    x, p = f(x), f(p)
    w_in0, w_out0, w_pg0, w_pp0 = f(w_in)[0], f(w_out)[0], f(w_ple_gate)[0], f(w_ple_proj)[0]
    tabDd, tabDl, tabMl, mcol, ident = _const_tables()
    sl_diff, sl_dil = _alibi(4), _alibi(8)
    bc = lambda v, n=128: np.ascontiguousarray(np.broadcast_to(np.asarray(v, np.float32)[None, :], (n, len(v))))
    gmix = np.ascontiguousarray(f(mix_norm_g)[0].reshape(8, 128).T)
    gple = np.ascontiguousarray(f(ple_norm_g)[0].reshape(8, 128).T)
    dq, dk, bq, bk = f(diff_q_norm_g)[0], f(diff_k_norm_g)[0], f(dil_q_norm_g)[0], f(dil_k_norm_g)[0]
    gqk = bc(np.concatenate([dq, dq, dk, dk, bq, bq, bk, bk]))
    lamv = bc(np.concatenate([f(lambda_q1)[0], f(lambda_k1)[0], f(lambda_q2)[0], f(lambda_k2)[0]]))
    gsub = np.ascontiguousarray(f(diff_sub_norm_g)[0][:, None])
    perm = np.concatenate([np.arange(pt * 512 + r * 128, pt * 512 + (r + 1) * 128) for r in range(4) for pt in range(2)])
    w_out_p = np.ascontiguousarray(w_out0[perm])
    maps = []
    for c in range(8):
        b, g = divmod(c, 4)
        cols = np.concatenate([np.arange(o + g * 128, o + (g + 1) * 128) for o in (0, 512, 1536, 2048, 1024, 2560, 3072, 3584)])
        nsl = bc(np.array([-sl_diff[g], -sl_dil[2 * g], -sl_dil[2 * g + 1]], np.float32))
        idx = (g * 1024 + np.arange(8)[None, :] * 128 + np.arange(128)[:, None]).astype(np.int32)
        maps.append({
            "xb": x[b], "xo": np.ascontiguousarray(x[b, g * 2048:(g + 1) * 2048]),
            "pT": np.ascontiguousarray(p[0, b, g * 2048:(g + 1) * 2048].T),
            "w_in": np.ascontiguousarray(w_in0[:, cols]), "w_out": w_out_p, "w_pg": w_pg0, "w_pp": w_pp0,
            "gmix": gmix, "gple": gple, "gqk": gqk, "lamv": lamv, "gsub": gsub, "nsl": nsl,
            "tabDd": tabDd, "tabDl": tabDl, "tabMl": tabMl, "mcol": mcol, "ident": ident, "idx": idx,
        })
    return maps


def kernel(**inputs):
    maps = make_in_maps(**inputs)
    nc = build_program(_DEBUG)
    res = run_bass_kernel_spmd(nc, maps, core_ids=list(range(8)))
    if _DEBUG.get("stop") or _DEBUG.get("dump_x1"):
        return res
    out = np.empty((2, S_LEN, DM), np.float32)
    for c in range(8):
        b, g = divmod(c, 4)
        out[b, g * 2048:(g + 1) * 2048] = np.asarray(res.results[c]["out"], dtype=np.float32)
    return out
```

```python
import numpy as np
import ml_dtypes
from contextlib import ExitStack
import concourse.bass as bass
import concourse.mybir as mybir
from concourse.bass_utils import run_bass_kernel_spmd

F32 = mybir.dt.float32
BF16 = mybir.dt.bfloat16
I32 = mybir.dt.int32
AF = mybir.ActivationFunctionType
ALU = mybir.AluOpType
AX = mybir.AxisListType

S_LEN = 8192
DM = 1024
NTILE = 64
NQT = 16
EPS = 1e-6
LAM_INIT = 0.8 - 0.6 * 1.0
ENGS = ("pe", "act", "dve", "pool", "sp")
SB_BASE = 16384

_DEBUG = {}


class Prog:
    def __init__(self, nc, stack):
        self.nc = nc
        self.stack = stack
        self.q = {e: [] for e in ENGS}
        self.esem = {e: stack.enter_context(nc.semaphore(f"es_{e}")) for e in ENGS}
        self.ecnt = {e: 0 for e in ENGS}
        self.waited = {e: {} for e in ENGS}
        self.nsem = 0
        self.ntens = 0

    def dsem(self, name=None):
        self.nsem += 1
        s = self.stack.enter_context(self.nc.semaphore(name or f"ds{self.nsem}"))
        return [s, 0]

    def sb(self, shape, dtype, off):
        self.ntens += 1
        h = self.nc.alloc_sbuf_tensor_at(f"sb{self.ntens}", list(shape), dtype, offset=SB_BASE + off)
        return h.ap()

    def _flat(self, deps):
        out = []
        for d in deps:
            if d is None:
                continue
            if isinstance(d, tuple) and len(d) == 2 and not isinstance(d[0], (tuple, list)):
                out.append(d)
            else:
                out.extend(self._flat(d))
        return out

    def _wait(self, eng, tok):
        sem, val = tok
        key = id(sem)
        if self.waited[eng].get(key, 0) >= val:
            return
        self.waited[eng][key] = val
        self.q[eng].append(lambda E, sem=sem, val=val: E.wait_ge(sem, val))

    def op(self, eng, fn, deps=(), pre=None):
        if pre is not None:
            self.q[eng].append(pre)
        for d in self._flat(deps):
            self._wait(eng, d)
        self.ecnt[eng] += 1
        n = self.ecnt[eng]
        sem = self.esem[eng]
        self.q[eng].append(lambda E, fn=fn, sem=sem: fn(E).then_inc(sem, 1))
        return (sem, n)

    def dma(self, eng, fn, ds, deps=()):
        for d in self._flat(deps):
            self._wait(eng, d)
        ds[1] += 16
        self.q[eng].append(lambda E, fn=fn, s=ds[0]: fn(E).then_inc(s, 16))
        return (ds[0], ds[1])

    def wait(self, eng, deps):
        for d in self._flat(deps):
            self._wait(eng, d)

    def replay(self, block):
        q = self.q

        @block.tensor
        def _(E):
            for f in q["pe"]:
                f(E)

        @block.scalar
        def _(E):
            for f in q["act"]:
                f(E)

        @block.vector
        def _(E):
            for f in q["dve"]:
                f(E)

        @block.gpsimd
        def _(E):
            for f in q["pool"]:
                f(E)

        @block.sync
        def _(E):
            for f in q["sp"]:
                f(E)


def DMA(out, in_):
    return lambda E: E.dma_start(out=out, in_=in_)


def build_program(dbg=None):
    dbg = dbg or {}
    stop = dbg.get("stop")
    nqt_dbg = dbg.get("nqt", NQT)
    nc = bass.Bass("TRN2", target_bir_lowering=False)

    def din(name, shape, dt=F32):
        return nc.dram_tensor(name, list(shape), dt, kind="ExternalInput").ap()

    def dout(name, shape, dt=F32):
        return nc.dram_tensor(name, list(shape), dt, kind="ExternalOutput").ap()

    xb = din("xb", [S_LEN, DM])
    xo_d = din("xo", [2048, DM])
    pT_d = din("pT", [256, 2048])
    w_in = din("w_in", [DM, 1024])
    w_out = din("w_out", [1024, DM])
    w_pg = din("w_pg", [DM, DM])
    w_pp = din("w_pp", [256, DM])
    gmix_d = din("gmix", [128, 8])
    gple_d = din("gple", [128, 8])
    gqk_d = din("gqk", [128, 512])
    lamv_d = din("lamv", [128, 256])
    gsub_d = din("gsub", [128, 1])
    nsl_d = din("nsl", [128, 3])
    tabDd_d = din("tabDd", [128, 1152])
    tabDl_d = din("tabDl", [128, 2944])
    tabMl_d = din("tabMl", [128, 2944])
    mcol_d = din("mcol", [128, 64])
    ident_d = din("ident", [128, 128], BF16)
    idx_d = din("idx", [128, 8], I32)
    out_d = dout("out", [2048, DM])
    ybin = nc.dram_tensor("ybin", [4, 256, 2048], BF16).ap()
    ygat = nc.dram_tensor("ygat", [4096, 2048], BF16).ap()
    if stop == "p1":
        dbg_qk = dout("dbg_qk", [128, 4 * S_LEN], BF16)
        dbg_vd = dout("dbg_vd", [128, 64 * 128], BF16)
        dbg_vl = dout("dbg_vl", [128, 64 * 128], BF16)
        dbg_zt = dout("dbg_zt", [128, 2 * S_LEN], BF16)
    if stop in ("dil", "diff"):
        dbg_y = dout("dbg_y", [4 * 256, 2048], BF16)
    if dbg.get("dump_x1"):
        dbg_x1 = dout("dbg_x1", [2048, DM])
        dbg_h2t = dout("dbg_h2t", [16 * 128, DM], BF16)

    with ExitStack() as st:
        P = Prog(nc, st)
        QK = P.sb([128, 4, S_LEN], BF16, 0)
        Vd = P.sb([128, 64, 128], BF16, 65536)
        Vl = P.sb([128, 64, 128], BF16, 81920)
        zT = P.sb([128, 2, S_LEN], BF16, 98304)
        C0 = 131072
        identb = P.sb([128, 128], BF16, C0)
        ones32 = P.sb([128, 128], F32, C0 + 256)
        onesb = P.sb([128, 128], BF16, C0 + 768)
        cb = P.sb([128, 64], F32, C0 + 1024)
        gmix32 = P.sb([128, 8], F32, C0 + 1280)
        gple32 = P.sb([128, 8], F32, C0 + 1408)
        nsl = P.sb([128, 3], F32, C0 + 1536)
        nlam = P.sb([128, 1], F32, C0 + 1664)
        gsubs = P.sb([128, 1], F32, C0 + 1792)
        idxs = P.sb([128, 8], I32, C0 + 1920)
        sc = P.sb([128, 1536], F32, 204800)

        def SC(slot, w=1):
            return sc[:, 32 * slot:32 * slot + w]
        gqk8 = P.sb([128, 512], F32, C0 + 2048)
        epsc = P.sb([128, 8], F32, 211968)
        sel = P.sb([64, 2, 128], F32, 210944)
        W0 = C0 + 4096
        Wi = P.sb([128, 8, 1024], BF16, W0)
        xs = [P.sb([128, 1024], F32, W0 + 16384 + 4096 * i) for i in range(4)]
        junk = P.sb([128, 1024], BF16, W0 + 32768)
        hb = [P.sb([128, 1024], BF16, W0 + 34816 + 2048 * i) for i in range(2)]
        hT = [P.sb([128, 8, 512], BF16, W0 + 38912 + 8192 * i) for i in range(2)]
        usb = [P.sb([128, 512], F32, W0 + 55296 + 2048 * i) for i in range(3)]
        sqb = P.sb([128, 512], F32, W0 + 61440)
        tmpb = P.sb([128, 512], F32, W0 + 63488)
        qn = [P.sb([128, 512], BF16, W0 + 65536 + 1024 * i) for i in range(2)]
        lamtmp = P.sb([128, 128], F32, W0 + 67584)
        wstg = [P.sb([128, 1024], F32, W0 + 38912 + 4096 * i) for i in range(4)]
        Etab = P.sb([128, 1152], F32, W0)
        Gtab = [P.sb([128, 2944], F32, W0 + 4608 + 11776 * i) for i in range(2)]
        Mtmp = P.sb([128, 2944], F32, W0 + 28160)
        P32 = [[P.sb([128, 512], BF16, W0 + 44032 + 1024 * (2 * i + s)) for s in range(2)] for i in range(2)]
        EtabB = P.sb([128, 1152], BF16, W0 + 28160)
        GtabB = [P.sb([128, 2944], BF16, W0 + 30464 + 5888 * i) for i in range(2)]
        Pb = [[P.sb([128, 512], BF16, W0 + 48128 + 1024 * (2 * i + s)) for s in range(2)] for i in range(3)]
        ET = [P.sb([128, 512], F32, W0 + 54272 + 2048 * i) for i in range(6)]
        yo = [P.sb([128, 512], BF16, W0 + 66560 + 1024 * i) for i in range(2)]
        Wo = P.sb([128, 8, 1024], BF16, 32768)
        Wg = P.sb([128, 8, 1024], BF16, 49152)
        Wp = P.sb([128, 2, 1024], BF16, 81920)
        wst = [P.sb([128, 1024], F32, 81920 + 4096 + 4096 * i) for i in range(2)]
        yTa = P.sb([128, 8, 2048], BF16, 0)
        pTb = P.sb([128, 2, 2048], BF16, 65536)
        xo = [P.sb([128, 1024], F32, 98304 + 4096 * i) for i in range(2)]
        x1 = [P.sb([128, 1024], F32, 98304 + 8192 + 4096 * i) for i in range(3)]
        osb = [P.sb([128, 1024], F32, 98304 + 20480 + 4096 * i) for i in range(2)]
        gate = P.sb([128, 1024], F32, 98304 + 28672)
        h2 = [P.sb([128, 1024], BF16, W0 + 2048 * i) for i in range(2)]
        h2T = [P.sb([128, 8, 128], BF16, W0 + 4096 + 2048 * i) for i in range(2)]
        junk3 = P.sb([128, 1024], BF16, W0 + 8192)
        pst = [P.sb([128, 1024], F32, W0 + 10240 + 4096 * i) for i in range(2)]

        banks = [nc.alloc_psum_tensor(f"bank{i}", [128, 512], F32).ap() for i in range(8)]
        banks_bf = [b.bitcast(BF16) for b in banks]

        ld = P.dsem("ld_const")
        tc = {}
        for name, dst, src in (("ident", identb, ident_d), ("gmix", gmix32, gmix_d), ("gple", gple32, gple_d),
                               ("gqk", gqk8, gqk_d), ("lamv", lamtmp[:, 0:128], lamv_d[:, 0:128]),
                               ("lamv2", tmpb[:, 0:128], lamv_d[:, 128:256]),
                               ("gsub", gsubs, gsub_d), ("nsl", nsl, nsl_d), ("mcol", cb, mcol_d), ("idx", idxs, idx_d)):
            tc[name] = P.dma("sp", DMA(dst, src), ld)
        for name in list(tc):
            tc[name] = (ld[0], ld[1])
        t_ones32 = P.op("pool", lambda E: E.memset(ones32[:, :], 1.0))
        t_epsc = P.op("pool", lambda E: E.memset(epsc[:, :], float(128 * EPS)))
        t_sel0 = P.op("pool", lambda E: E.memset(sel[:, :, :], 0.0))
        t_sel1 = P.op("pool", lambda E: E.memset(sel[0:1, 0, :], 1.0), [t_sel0])
        t_sel = P.op("pool", lambda E: E.memset(sel[32:33, 1, :], 1.0), [t_sel0, t_sel1])
        t_onesb = P.op("pool", lambda E: E.memset(onesb[:, :], 1.0))
        t_gmix = P.op("dve", lambda E: E.tensor_scalar(out=gmix32[:, :], in0=gmix32[:, :], scalar1=32.0, scalar2=None, op0=ALU.mult), [tc["gmix"]])
        t_gple = P.op("dve", lambda E: E.tensor_scalar(out=gple32[:, :], in0=gple32[:, :], scalar1=32.0, scalar2=None, op0=ALU.mult), [tc["gple"]])
        t_gqk = P.op("dve", lambda E: E.tensor_scalar(out=gqk8[:, :], in0=gqk8[:, :], scalar1=8.0, scalar2=None, op0=ALU.mult), [tc["gqk"]])
        t_cb = P.op("dve", lambda E: E.tensor_scalar(out=cb[:, :], in0=cb[:, :], scalar1=nsl[:, 0:1], scalar2=None, op0=ALU.mult), [tc["mcol"], tc["nsl"]])
        t_gs = P.op("dve", lambda E: E.tensor_scalar(out=gsubs[:, :], in0=gsubs[:, :], scalar1=float((1.0 - LAM_INIT) * np.sqrt(128.0)), scalar2=None, op0=ALU.mult), [tc["gsub"]])
        t_l1 = P.op("dve", lambda E: E.tensor_tensor(out=lamtmp[:, 0:64], in0=lamtmp[:, 0:64], in1=lamtmp[:, 64:128], op=ALU.mult), [tc["lamv"]])
        t_l2 = P.op("dve", lambda E: E.tensor_tensor(out=tmpb[:, 0:64], in0=tmpb[:, 0:64], in1=tmpb[:, 64:128], op=ALU.mult), [tc["lamv2"]])
        t_l3 = P.op("dve", lambda E: E.tensor_reduce(out=SC(40), in_=lamtmp[:, 0:64], axis=AX.X, op=ALU.add), [t_l1])
        t_l4 = P.op("dve", lambda E: E.tensor_reduce(out=SC(41), in_=tmpb[:, 0:64], axis=AX.X, op=ALU.add), [t_l2])
        t_l5a = P.op("act", lambda E: E.activation(out=SC(42), in_=SC(40), func=AF.Exp), [t_l3])
        t_l5 = P.op("act", lambda E: E.activation(out=SC(43), in_=SC(41), func=AF.Exp), [t_l4])
        t_l6 = P.op("dve", lambda E: E.tensor_tensor(out=SC(44), in0=SC(43), in1=SC(42), op=ALU.subtract), [t_l5, t_l5a])
        t_nlam = P.op("dve", lambda E: E.tensor_scalar(out=nlam[:, :], in0=SC(44), scalar1=float(-LAM_INIT), scalar2=None, op0=ALU.add), [t_l6])

        wsem = [P.dsem(f"wst{i}") for i in range(4)]
        wi_tok = []
        cast_tok = [None] * 4
        for k in range(8):
            sl = k % 4
            t_ld = P.dma("sp", DMA(wstg[sl][:, :], w_in[k * 128:(k + 1) * 128, :]), wsem[sl], [cast_tok[sl]])
            cast_tok[sl] = P.op("act", lambda E, k=k, sl=sl: E.activation(out=Wi[:, k, :], in_=wstg[sl][:, :], func=AF.Copy, scale=gmix32[:, k:k + 1]), [t_ld, t_gmix])
            wi_tok.append(cast_tok[sl])

        xsem = [P.dsem(f"xs{i}") for i in range(4)]
        hs_tok, tra_tok, hTe_tok, u_tok, ue_tok, vl_tok, tmp_tok, qn_tok, trq_tok, qke_tok = ({} for _ in range(10))
        rcp_tok, a8_tok = {}, {}
        z_tok = {}
        silu_tok = {0: {}, 1: {}}
        vd_tok = {}
        nt1 = dbg.get("ntile", NTILE)

        xl1_tok, sq1_tok, add1_tok, sqt1_tok, s8_tok = {}, {}, {}, {}, {}

        def f_load(t):
            sl4 = t % 4
            d0 = [hs_tok.get(t - 4)]
            xl1_tok[t] = P.dma("sp", DMA(xs[sl4][:, :], xb[t * 128:(t + 1) * 128, :]), xsem[sl4], d0)

        def f_a1(t):
            sl4 = t % 4
            c = 4 * sl4
            sq1_tok[t] = P.op("act", lambda E: E.activation(out=junk[:, :], in_=xs[sl4][:, :], func=AF.Square, accum_out=SC(c)), [xl1_tok[t]])

        def f_a1_add(t):
            c = 4 * (t % 4)
            add1_tok[t] = P.op("dve", lambda E: E.tensor_scalar(out=SC(c + 1), in0=SC(c), scalar1=float(DM * EPS), scalar2=None, op0=ALU.add), [sq1_tok[t]])

        def f_a1_sqrt(t):
            c = 4 * (t % 4)
            sqt1_tok[t] = P.op("act", lambda E: E.activation(out=SC(c + 2), in_=SC(c + 1), func=AF.Sqrt), [add1_tok[t]])

        def f_a1_rcp(t):
            c = 4 * (t % 4)
            rcp_tok[t] = P.op("dve", lambda E: E.reciprocal(out=SC(c + 3), in_=SC(c + 2)), [sqt1_tok[t]])

        def f_hs(t):
            sl, sl4 = t % 2, t % 4
            c = 4 * sl4
            hs_tok[t] = P.op("act", lambda E: E.activation(out=hb[sl][:, :], in_=xs[sl4][:, :], func=AF.Copy, scale=SC(c + 3)), [rcp_tok[t], tra_tok.get(t - 2)])

        def f_T(t):
            sl = t % 2
            tb = banks_bf[sl]
            for k in range(8):
                tk = P.op("pe", lambda E, k=k: E.transpose(out=tb[:, k * 128:(k + 1) * 128], in_=hb[sl][:, k * 128:(k + 1) * 128], identity=identb[:, :]),
                          [hs_tok[t], hTe_tok.get(t - 2), tc["ident"]])
            tra_tok[t] = tk

        def f_hTe(t):
            G, sub = divmod(t, 4)
            sl = t % 2
            tb = banks_bf[sl]
            dfree = []
            if G >= 2 and sub == 0:
                dfree = [u_tok[4 * (G - 2) + 3], z_tok[G - 2]]
            if t < 8:
                dfree = [dfree, wi_tok]
            hTe_tok[t] = P.op("dve", lambda E: E.tensor_copy(out=hT[G % 2][:, :, sub * 128:(sub + 1) * 128],
                                                             in_=tb[:, :].rearrange("p (k t) -> p k t", k=8)), [tra_tok[t], dfree])

        def f_U(t):
            G, sub = divmod(t, 4)
            sl = t % 2
            U0 = banks[2 + sl]
            U1 = banks[4 + sl]
            for k in range(8):
                tk = P.op("pe", lambda E, k=k: E.matmul(U0[:, :], lhsT=hT[G % 2][:, k, sub * 128:(sub + 1) * 128], rhs=Wi[:, k, 0:512], start=(k == 0), stop=(k == 7)),
                          [hTe_tok[t], wi_tok, ue_tok.get(t - 2)])
            for k in range(8):
                tk = P.op("pe", lambda E, k=k: E.matmul(U1[:, 0:256], lhsT=hT[G % 2][:, k, sub * 128:(sub + 1) * 128], rhs=Wi[:, k, 512:768], start=(k == 0), stop=(k == 7)),
                          [vl_tok.get(t - 2)])
            u_tok[t] = tk
            if sub == 3:
                f_z(t, 0)

        def f_z(t, zi):
            G = t // 4
            Z = banks[6]
            for k in range(8):
                tk = P.op("pe", lambda E, k=k: E.matmul(Z[:, :], lhsT=Wi[:, k, 768 + zi * 128:896 + zi * 128], rhs=hT[G % 2][:, k, :], start=(k == 0), stop=(k == 7)),
                          [silu_tok[1].get(G - 1) if zi == 0 else silu_tok[0][G], hTe_tok[t]])
            z_tok[(G, zi)] = tk
            if zi == 1:
                z_tok[G] = tk

        def f_silu(t, zi):
            G = t // 4
            Z = banks[6]
            silu_tok[zi][G] = P.op("act", lambda E: E.activation(out=zT[:, zi, G * 512:(G + 1) * 512], in_=Z[:, :], func=AF.Silu), [z_tok[(G, zi)]])

        def f_z1(t):
            if t % 4 == 3:
                f_z(t, 1)
                f_silu(t, 1)

        def f_ue(t):
            G, sub = divmod(t, 4)
            sl, s3 = t % 2, t % 3
            U0 = banks[2 + sl]
            U1 = banks[4 + sl]
            ue_tok[t] = P.op("act", lambda E: E.activation(out=usb[s3][:, :], in_=U0[:, :], func=AF.Copy), [u_tok[t], tmp_tok.get(t - 3)])
            vd_tok[t] = P.op("act", lambda E: E.activation(out=Vd[:, t, :], in_=U1[:, 0:128], func=AF.Copy), [u_tok[t]])
            vl_tok[t] = P.op("act", lambda E: E.activation(out=Vl[:, t, :], in_=U1[:, 128:256], func=AF.Copy), [u_tok[t]])
            if sub == 3:
                f_silu(t, 0)

        def f_sq(t):
            s3 = t % 3
            c = 16 + 4 * (t % 4)
            t1 = P.op("dve", lambda E: E.tensor_tensor(out=sqb[:, :], in0=usb[s3][:, :], in1=usb[s3][:, :], op=ALU.mult), [ue_tok[t]])
            t2 = P.op("dve", lambda E: E.tensor_reduce(out=SC(c, 8), in_=sqb[:, :].rearrange("p (g d) -> p g d", g=8), axis=AX.X, op=ALU.add), [t1])
            a8_tok[t] = P.op("dve", lambda E: E.tensor_scalar(out=SC(c + 1, 8), in0=SC(c, 8), scalar1=float(64 * EPS), scalar2=None, op0=ALU.add), [t2])

        def f_sqrt8(t):
            c = 16 + 4 * (t % 4)
            s8_tok[t] = P.op("act", lambda E: E.activation(out=SC(c + 2, 8), in_=SC(c + 1, 8), func=AF.Sqrt), [a8_tok[t]])

        def f_b2(t):
            sl, s3 = t % 2, t % 3
            c = 16 + 4 * (t % 4)
            t5 = P.op("dve", lambda E: E.reciprocal(out=SC(c + 3, 8), in_=SC(c + 2, 8)), [s8_tok[t]])
            tmp_tok[t] = P.op("dve", lambda E: E.tensor_tensor(out=tmpb[:, :].rearrange("p (g d) -> p g d", g=8), in0=usb[s3][:, :].rearrange("p (g d) -> p g d", g=8),
                                                               in1=SC(c + 3, 8).unsqueeze(2).broadcast_to([128, 8, 64]), op=ALU.mult), [t5])
            qn_tok[t] = P.op("dve", lambda E: E.tensor_tensor(out=qn[sl][:, :], in0=tmpb[:, :], in1=gqk8[:, :], op=ALU.mult), [tmp_tok[t], trq_tok.get(t - 2), t_gqk])

        def f_Tq(t):
            sl = t % 2
            qb = banks_bf[7]
            for cidx in range(4):
                tk = P.op("pe", lambda E, cidx=cidx: E.transpose(out=qb[:, cidx * 128:(cidx + 1) * 128], in_=qn[sl][:, cidx * 128:(cidx + 1) * 128], identity=identb[:, :]),
                          [qn_tok[t], qke_tok.get(t - 1)])
            trq_tok[t] = tk

        def f_qke(t):
            qb = banks_bf[7]
            qke_tok[t] = P.op("dve", lambda E: E.tensor_copy(out=QK[:, :, t * 128:(t + 1) * 128], in_=qb[:, 0:512].rearrange("p (c t) -> p c t", c=4)), [trq_tok[t]])

        def emit_step(s):
            ok = lambda t: 0 <= t < nt1
            for f, t in ((f_U, s - 3), (f_Tq, s - 6), (f_a1, s), (f_hs, s - 1), (f_T, s - 1), (f_sqrt8, s - 5), (f_hTe, s - 2), (f_sq, s - 4),
                         (f_a1_add, s), (f_a1_sqrt, s), (f_b2, s - 5), (f_a1_rcp, s), (f_ue, s - 3), (f_z1, s - 3), (f_qke, s - 6), (f_load, s + 1)):
                if ok(t):
                    f(t)

        f_load(0)
        for step in range(nt1 + 7):
            emit_step(step)

        p1_done = [qke_tok[nt1 - 1], qke_tok[nt1 - 2], vd_tok[nt1 - 1], vl_tok[nt1 - 1], silu_tok[0][(nt1 - 1) // 4], silu_tok[1][(nt1 - 1) // 4], z_tok[(nt1 - 1) // 4], u_tok[nt1 - 1], trq_tok[nt1 - 1]]

        fin = []
        stq = P.dsem("st_out")
        if stop == "p1":
            P.wait("sp", p1_done)
            for a in range(4):
                fin.append(P.dma("sp", DMA(dbg_qk[:, a * S_LEN:(a + 1) * S_LEN], QK[:, a, :]), stq, p1_done))
            for a in range(4):
                fin.append(P.dma("sp", DMA(dbg_vd[:, a * 2048:(a + 1) * 2048], Vd[:, a * 16:(a + 1) * 16, :].rearrange("p a t -> p (a t)")), stq))
                fin.append(P.dma("sp", DMA(dbg_vl[:, a * 2048:(a + 1) * 2048], Vl[:, a * 16:(a + 1) * 16, :].rearrange("p a t -> p (a t)")), stq))
            for a in range(2):
                fin.append(P.dma("sp", DMA(dbg_zt[:, a * S_LEN:(a + 1) * S_LEN], zT[:, a, :]), stq))

        ysem = [[P.dsem(f"ybin{j}_{p}") for p in range(2)] for j in range(4)]
        ccsem = P.dsem("cc")
        if stop != "p1":
            tsem = P.dsem("tabs")
            t_e = P.dma("sp", DMA(Etab[:, :], tabDd_d[:, :]), tsem, p1_done)
            t_g0 = P.dma("sp", DMA(Gtab[0][:, :], tabDl_d[:, :]), tsem)
            t_g1 = P.dma("sp", DMA(Gtab[1][:, :], tabDl_d[:, :]), tsem)
            t_m = P.dma("sp", DMA(Mtmp[:, :], tabMl_d[:, :]), tsem)
            tabs_ld = [(tsem[0], tsem[1])]
            tG = []
            for i in range(2):
                ta = P.op("act", lambda E, i=i: E.activation(out=Gtab[i][:, :], in_=Gtab[i][:, :], func=AF.Exp, scale=nsl[:, 1 + i:2 + i]), [tabs_ld, tc["nsl"]])
                tG.append(P.op("dve", lambda E, i=i: E.tensor_tensor(out=Gtab[i][:, :], in0=Gtab[i][:, :], in1=Mtmp[:, :], op=ALU.mult), [ta]))
            tE = P.op("act", lambda E: E.activation(out=Etab[:, :], in_=Etab[:, :], func=AF.Exp, scale=nsl[:, 0:1]), [tabs_ld])
            tGb = [P.op("dve", lambda E, i=i: E.tensor_copy(out=GtabB[i][:, :], in_=Gtab[i][:, :]), [tG]) for i in range(2)]
            tEb = P.op("dve", lambda E: E.tensor_copy(out=EtabB[:, :], in_=Etab[:, :]), [tE, tG])
            tabs_ready = [tGb, tEb, t_cb]

            SB = [[banks[0], banks[1]], [banks[2], banks[3]]]

            def attention(kind, extra=None):
                diff = kind == "diff"
                steps = []
                for q in range(nqt_dbg):
                    kb0 = q * 4
                    kbs = list(range(64)) if diff else [kb for kb in range(kb0 - 8, kb0 + 12) if 0 <= kb < 64]
                    for i, kb in enumerate(kbs):
                        steps.append((q, kb, i == 0, i == len(kbs) - 1))
                N = len(steps)
                exp_tok = {}
                mul_tok = {}
                av_tok = {}
                st8 = {"epi_free": None, "ss_free": None, "bk7_free": None}
                BK6, BK7 = banks[6], banks[7]
                if diff:
                    OB = [banks[4], banks[5]]
                    LB = [banks[6], banks[6]]
                else:
                    OB = [banks[4], banks[4]]
                    LB = [banks[5], banks[5]]
                qi = 0 if diff else 2
                ki = 1 if diff else 3

                def front(n):
                    q, kb, first, last = steps[n]
                    i0 = q * 512
                    j0 = kb * 128
                    par = n % 2
                    if diff:
                        if j0 + 128 <= i0:
                            off = 640
                            bias = cb[:, (i0 - j0 - 128) // 128:(i0 - j0 - 128) // 128 + 1]
                        elif j0 >= i0 + 512:
                            off = 0
                            bias = cb[:, (j0 - i0 - 512) // 128:(j0 - i0 - 512) // 128 + 1]
                        else:
                            off = 512 - (j0 - i0)
                            bias = cb[:, 0:1]
                    else:
                        off = 1408 - (j0 - i0)
                        bias = None
                    exp_tok[n] = []
                    mul_tok[n] = []
                    for s in range(2):
                        rows = slice(64 * s, 64 * s + 64)
                        Sb = SB[par][s]
                        prev = exp_tok[n - 2][s] if n >= 2 else None
                        ts = P.op("pe", lambda E, Sb=Sb, rows=rows: E.matmul(Sb[:, :], lhsT=QK[rows, ki, j0:j0 + 128], rhs=QK[rows, qi, i0:i0 + 512], start=True, stop=True),
                                  [prev, p1_done if n < 2 else None])
                        pm = mul_tok[n - 2][s] if n >= 2 else None
                        p32 = P32[par][s]
                        if bias is not None:
                            te = P.op("act", lambda E, Sb=Sb, p32=p32, bias=bias: E.activation(out=p32[:, :], in_=Sb[:, :], func=AF.Exp, bias=bias, scale=0.125), [ts, pm, tabs_ready if n < 2 else None])
                        else:
                            te = P.op("act", lambda E, Sb=Sb, p32=p32: E.activation(out=p32[:, :], in_=Sb[:, :], func=AF.Exp, scale=0.125), [ts, pm, tabs_ready if n < 2 else None])
                        exp_tok[n].append(te)
                        tab = EtabB if diff else GtabB[s]
                        pb = Pb[n % 3][s]
                        tm = P.op("dve", lambda E, p32=p32, pb=pb, tab=tab, off=off: E.tensor_tensor(out=pb[:, :], in0=p32[:, :], in1=tab[:, off:off + 512], op=ALU.mult),
                                  [te, av_tok.get(n - 3), tabs_ready if n < 3 else None])
                        mul_tok[n].append(tm)

                pending = []

                def run_pending():
                    for stages in list(pending):
                        stages.pop(0)()
                        if not stages:
                            pending.remove(stages)

                def back(n):
                    q, kb, first, last = steps[n]
                    i0 = q * 512
                    run_pending()
                    deps0 = [st8["epi_free"]] if first else []
                    for s in range(2):
                        pb = Pb[n % 3][s]
                        if diff:
                            P.op("pe", lambda E, pb=pb, s=s: E.matmul(OB[s][:, :], lhsT=Vd[:, kb, :], rhs=pb[:, :], start=first, stop=last), [mul_tok[n][s], deps0],
                                 pre=(lambda E: E.ldweights(Vd[:, kb, :])) if s == 0 else None)
                        else:
                            rows = slice(64 * s, 64 * s + 64)
                            P.op("pe", lambda E, pb=pb, s=s, rows=rows: E.matmul(OB[s][rows, :], lhsT=Vl[:, kb, 64 * s:64 * s + 64], rhs=pb[:, :], start=first, stop=last, tile_position=(0, 64 * s)),
                                 [mul_tok[n][s], deps0])
                    for s in range(2):
                        pb = Pb[n % 3][s]
                        if diff:
                            tk = P.op("pe", lambda E, pb=pb, s=s: E.matmul(BK6[32 * s:32 * s + 32, :], lhsT=onesb[:, 0:32], rhs=pb[:, :], start=first, stop=last, tile_position=(0, 32 * s)),
                                      [st8["ss_free"] if first else None, t_onesb])
                        else:
                            rows = slice(64 * s, 64 * s + 64)
                            tk = P.op("pe", lambda E, pb=pb, s=s, rows=rows: E.matmul(LB[s][rows, :], lhsT=onesb[:, 0:64], rhs=pb[:, :], start=first, stop=last, tile_position=(0, 64 * s)), [t_onesb])
                    av_tok[n] = tk
                    if not last:
                        return
                    j, qq = divmod(q, 4)
                    y = yo[q % 2]

                    def finish_store(ey, rows):
                        st8[("ydma", q % 2)] = P.dma("sp", DMA(ybin[j, rows, qq * 512:(qq + 1) * 512], y[:, :]), ysem[j][q % 2], [ey])
                        if diff and qq == 3:
                            P.wait("pool", [(ysem[j][0][0], ysem[j][0][1]), (ysem[j][1][0], ysem[j][1][1])])
                            ccsem[1] += 1
                            P.q["pool"].append(lambda E, j=j: E.collective_compute(
                                "AllGather", ALU.bypass, replica_groups=[[0, 1, 2, 3], [4, 5, 6, 7]],
                                ins=[ybin[j].opt()], outs=[ygat[j * 1024:(j + 1) * 1024, :].opt()]).then_inc(ccsem[0], 1))

                    if diff:
                        Rs, T0, R0s, R1s, T1, A = ET
                        ycur = y
                        e1p = []

                        def rec1(c):
                            e1p.append(P.op("dve", lambda E: E.reciprocal(out=Rs[0:64, c * 128:(c + 1) * 128], in_=BK6[0:64, c * 128:(c + 1) * 128]), [tk, st8.get("epi_done")]))

                        e1 = P.op("dve", lambda E: E.tensor_copy(out=A[0:64, :], in_=BK6[0:64, :]), [tk, st8.get("epi_done")])
                        st8["ss_free"] = e1
                        o0 = P.op("act", lambda E: E.activation(out=T0[:, :], in_=OB[0][:, :], func=AF.Copy), [tk, st8.get("epi_done")])
                        o1 = P.op("act", lambda E: E.activation(out=T1[:, :], in_=OB[1][:, :], func=AF.Copy), [tk])
                        st8["epi_free"] = [o0, o1]
                        ctx = {}

                        def r1(c):
                            def f():
                                e1p.append(P.op("dve", lambda E: E.reciprocal(out=Rs[0:64, c * 128:(c + 1) * 128], in_=A[0:64, c * 128:(c + 1) * 128]), [e1]))
                            return f

                        def s1():
                            ctx["b0"] = P.op("pe", lambda E: E.matmul(BK7[:, :], lhsT=sel[0:64, 0, :], rhs=Rs[0:64, :], start=True, stop=True), [e1p, st8["bk7_free"], t_sel])

                        def s2():
                            ctx["c0"] = P.op("act", lambda E: E.activation(out=R0s[:, :], in_=BK7[:, :], func=AF.Copy), [ctx["b0"]])

                        def s3():
                            ctx["b1"] = P.op("pe", lambda E: E.matmul(BK7[:, :], lhsT=sel[0:64, 1, :], rhs=Rs[0:64, :], start=True, stop=True), [ctx["c0"]])

                        def s4():
                            ctx["c1"] = P.op("act", lambda E: E.activation(out=R1s[:, :], in_=BK7[:, :], func=AF.Copy), [ctx["b1"]])

                        def s5():
                            e2 = P.op("dve", lambda E: E.tensor_tensor(out=T0[:, :], in0=T0[:, :], in1=R0s[:, :], op=ALU.mult), [ctx["c0"], o0])
                            e4 = P.op("dve", lambda E: E.tensor_tensor(out=T1[:, :], in0=T1[:, :], in1=R1s[:, :], op=ALU.mult), [ctx["c1"], o1])
                            ctx["e5"] = P.op("dve", lambda E: E.scalar_tensor_tensor(out=A[:, :], in0=T1[:, :], scalar=nlam[:, 0:1], in1=T0[:, :], op0=ALU.mult, op1=ALU.add), [e2, e4, t_nlam])

                        def s6():
                            ctx["e6"] = P.op("act", lambda E: E.activation(out=R0s[:, :], in_=A[:, :], func=AF.Square), [ctx["e5"]])

                        def s7():
                            ctx["e7"] = P.op("pe", lambda E: E.matmul(BK7[:, :], lhsT=ones32[:, :], rhs=R0s[:, :], start=True, stop=True), [ctx["e6"], ctx["c1"], t_ones32])

                        def s8():
                            ctx["e8"] = P.op("act", lambda E: E.activation(out=Rs[:, :], in_=BK7[:, :], func=AF.Ln, bias=epsc[:, 0:1], scale=1.0), [ctx["e7"], t_epsc])
                            st8["bk7_free"] = ctx["e8"]

                        def s9():
                            ctx["e9"] = P.op("act", lambda E: E.activation(out=T0[:, :], in_=Rs[:, :], func=AF.Exp, scale=-0.5), [ctx["e8"]])

                        def s10():
                            e11 = P.op("dve", lambda E: E.tensor_tensor(out=T1[:, :], in0=A[:, :], in1=T0[:, :], op=ALU.mult), [ctx["e9"]])
                            ey = P.op("dve", lambda E: E.scalar_tensor_tensor(out=ycur[:, :], in0=T1[:, :], scalar=gsubs[:, 0:1], in1=zT[:, 0, i0:i0 + 512], op0=ALU.mult, op1=ALU.mult),
                                      [e11, t_gs, st8.get(("ydma", q % 2))])
                            st8["epi_done"] = ey
                            finish_store(ey, slice(0, 128))

                        nop = lambda: None
                        pending.append([r1(0), r1(1), r1(2), r1(3), nop, nop, s1, nop, s2, nop, s3, nop, s4, nop, s5, nop, s6, nop, s7, nop, s8, nop, s9, nop, s10])
                        return
                        rows = slice(0, 128)
                    else:
                        Rs, T0, A = ET[0], ET[1], ET[5]
                        ycur = y
                        o0 = P.op("dve", lambda E: E.tensor_copy(out=T0[:, :], in_=OB[0][:, :]), [tk, st8.get("epi_done")])
                        o1 = P.op("dve", lambda E: E.tensor_copy(out=A[:, :], in_=LB[0][:, :]), [tk])
                        st8["epi_free"] = [o0, o1]
                        rp = []

                        def rr(c):
                            def f():
                                rp.append(P.op("dve", lambda E: E.reciprocal(out=Rs[:, c * 128:(c + 1) * 128], in_=A[:, c * 128:(c + 1) * 128]), [o1]))
                            return f

                        ctx = {}

                        def d1():
                            ctx["e2"] = P.op("dve", lambda E: E.tensor_tensor(out=T0[:, :], in0=T0[:, :], in1=Rs[:, :], op=ALU.mult), [rp, o0])

                        def d2():
                            ey = P.op("dve", lambda E: E.tensor_tensor(out=ycur[:, :], in0=T0[:, :], in1=zT[:, 1, i0:i0 + 512], op=ALU.mult), [ctx["e2"], st8.get(("ydma", q % 2))])
                            st8["epi_done"] = ey
                            finish_store(ey, slice(128, 256))

                        nop = lambda: None
                        pending.append([rr(0), rr(1), rr(2), rr(3), nop, d1, nop, d2])
                        return
                    finish_store(ey, rows)

                for n in range(N + 2):
                    if n < N:
                        front(n)
                        if extra is not None:
                            extra(n)
                    if n >= 2:
                        back(n - 2)
                while pending:
                    run_pending()
                return [av_tok[N - 1], mul_tok[N - 1], exp_tok[N - 1], st8["epi_free"], st8.get(("ydma", 0)), st8.get(("ydma", 1)), st8["ss_free"], st8["bk7_free"], st8.get("epi_done")]

            dil_done = attention("dil")
            if stop == "dil":
                P.wait("sp", dil_done)
            else:
                wsem3 = [P.dsem("w3a"), P.dsem("w3b")]
                w3_cast = [None, None]
                w3_tok = []
                jobs = [(Wo, k, w_out, None) for k in range(8)] + [(Wg, k, w_pg, k) for k in range(8)] + [(Wp, k, w_pp, None) for k in range(2)]

                def w3_job(ji):
                    dst, k, src, gk = jobs[ji]
                    sl = ji % 2
                    t_ld = P.dma("sp", DMA(wst[sl][:, :], src[k * 128:(k + 1) * 128, :]), wsem3[sl], [w3_cast[sl], dil_done if ji < 2 else None])
                    if gk is None:
                        w3_cast[sl] = P.op("act", lambda E: E.activation(out=dst[:, k, :], in_=wst[sl][:, :], func=AF.Copy), [t_ld])
                    else:
                        w3_cast[sl] = P.op("act", lambda E: E.activation(out=dst[:, k, :], in_=wst[sl][:, :], func=AF.Copy, scale=gple32[:, k:k + 1]), [t_ld, t_gple])
                    w3_tok.append(w3_cast[sl])

                def diff_extra(n):
                    if n % 6 == 3 and n // 6 < len(jobs):
                        w3_job(n // 6)

                diff_done = attention("diff", diff_extra)

            if stop in ("dil", "diff"):
                last = dil_done if stop == "dil" else diff_done
                P.wait("sp", last)
                for j in range(4):
                    P.wait("sp", [(ysem[j][0][0], ysem[j][0][1]), (ysem[j][1][0], ysem[j][1][1])])
                for j in range(4):
                    for r in range(2):
                        t1 = P.dma("sp", DMA(ET[0][:, :].bitcast(BF16)[:, 0:1024], ybin[j, r * 128:(r + 1) * 128, 0:1024]), stq, [fin[-1]] if fin else [])
                        P.wait("sp", [t1])
                        fin.append(P.dma("sp", DMA(dbg_y[j * 256 + r * 128:j * 256 + (r + 1) * 128, 0:1024], ET[0][:, :].bitcast(BF16)[:, 0:1024]), stq))
                        P.wait("sp", [fin[-1]])
                        t1 = P.dma("sp", DMA(ET[0][:, :].bitcast(BF16)[:, 0:1024], ybin[j, r * 128:(r + 1) * 128, 1024:2048]), stq)
                        P.wait("sp", [t1])
                        fin.append(P.dma("sp", DMA(dbg_y[j * 256 + r * 128:j * 256 + (r + 1) * 128, 1024:2048], ET[0][:, :].bitcast(BF16)[:, 0:1024]), stq))
                        P.wait("sp", [fin[-1]])

        if stop is None:
            P.wait("pool", [(ccsem[0], ccsem[1])])
            gsem = P.dsem("gath")
            gtok = []
            for c in range(8):
                gtok.append(P.dma("pool", lambda E, c=c: E.indirect_dma_start(
                    out=yTa[:, c, :], out_offset=None, in_=ygat[:, :],
                    in_offset=bass.IndirectOffsetOnAxis(ap=idxs[:, c:c + 1], axis=0)), gsem, [diff_done, tc["idx"]]))
            gtok = [(gsem[0], gsem[1])]
            psem = [P.dsem("p3a"), P.dsem("p3b")]
            pc_tok = [None, None]
            p_tok = []
            for ji in range(4):
                c2, hf = divmod(ji, 2)
                sl = ji % 2
                t_ld = P.dma("sp", DMA(pst[sl][:, :], pT_d[c2 * 128:(c2 + 1) * 128, hf * 1024:(hf + 1) * 1024]), psem[sl], [pc_tok[sl], diff_done if ji < 2 else None])
                pc_tok[sl] = P.op("act", lambda E, c2=c2, hf=hf, sl=sl: E.activation(out=pTb[:, c2, hf * 1024:(hf + 1) * 1024], in_=pst[sl][:, :], func=AF.Copy), [t_ld])
                p_tok.append(pc_tok[sl])
            xosem = [P.dsem("xo0"), P.dsem("xo1")]
            osem = [P.dsem("o0"), P.dsem("o1")]
            x1e_tok, hs3_tok, tr3_tok, h2e_tok, gm_tok, pm_tok, sg_tok, fo_tok, od_tok, am_tok = ({} for _ in range(10))
            A0, A1, TB, G0, G1, PP0, PP1 = banks[0], banks[1], banks_bf[2], banks[3], banks[4], banks[5], banks[6]
            xl_tok = {}
            dx_tok = {}
            dxsem = P.dsem("dx")

            def load3(tt):
                if tt >= 16:
                    return
                sl = tt % 2
                xl_tok[tt] = P.dma("sp", DMA(xo[sl][:, :], xo_d[tt * 128:(tt + 1) * 128, :]), xosem[sl], [x1e_tok.get(tt - 2), diff_done if tt < 2 else None])

            rc3_tok = {}

            def stX(tt):
                sl, s3 = tt % 2, tt % 3
                c = 32 + 4 * (tt % 4)
                tsl = slice(tt * 128, (tt + 1) * 128)
                t_x = xl_tok[tt]
                for hf, AB in ((0, A0), (1, A1)):
                    for cc in range(8):
                        tk = P.op("pe", lambda E, hf=hf, AB=AB, cc=cc: E.matmul(AB[:, :], lhsT=yTa[:, cc, tsl], rhs=Wo[:, cc, hf * 512:(hf + 1) * 512], start=(cc == 0), stop=(cc == 7)),
                                  [gtok, w3_tok, x1e_tok.get(tt - 1)])
                am_tok[tt] = tk
                ta = P.op("dve", lambda E: E.tensor_tensor(out=x1[s3][:, 0:512], in0=A0[:, :], in1=xo[sl][:, 0:512], op=ALU.add), [tk, t_x, fo_tok.get(tt - 3), dx_tok.get(tt - 3)])
                x1e_tok[tt] = P.op("dve", lambda E: E.tensor_tensor(out=x1[s3][:, 512:1024], in0=A1[:, :], in1=xo[sl][:, 512:1024], op=ALU.add), [tk])
                load3(tt + 2)
                if dbg.get("dump_x1"):
                    dx_tok[tt] = P.dma("sp", DMA(dbg_x1[tsl, :], x1[s3][:, :]), dxsem, [ta, x1e_tok[tt]])
                    fin.append(dx_tok[tt])
                hs3_tok[tt] = P.op("act", lambda E: E.activation(out=h2[sl][:, :], in_=x1[s3][:, :], func=AF.Copy), [ta, x1e_tok[tt], tr3_tok.get(tt - 2)])
                t_sq = P.op("act", lambda E: E.activation(out=junk3[:, :], in_=x1[s3][:, :], func=AF.Square, accum_out=SC(c)), [ta, x1e_tok[tt]])
                t_a = P.op("dve", lambda E: E.tensor_scalar(out=SC(c + 1), in0=SC(c), scalar1=float(DM * EPS), scalar2=None, op0=ALU.add), [t_sq])
                t_b = P.op("act", lambda E: E.activation(out=SC(c + 2), in_=SC(c + 1), func=AF.Sqrt), [t_a])
                rc3_tok[tt] = P.op("dve", lambda E: E.reciprocal(out=SC(c + 3), in_=SC(c + 2)), [t_b])

            def stY1(tt):
                sl, s3 = tt % 2, tt % 3
                c = 32 + 4 * (tt % 4)
                for k in range(8):
                    tk = P.op("pe", lambda E, k=k: E.transpose(out=TB[:, k * 128:(k + 1) * 128], in_=h2[sl][:, k * 128:(k + 1) * 128], identity=identb[:, :]), [hs3_tok[tt], h2e_tok.get(tt - 1)])
                tr3_tok[tt] = tk
                h2e_tok[tt] = P.op("dve", lambda E: E.tensor_copy(out=h2T[sl][:, :, :], in_=TB[:, :].rearrange("p (k t) -> p k t", k=8)), [tk, gm_tok.get(tt - 2)])

            def stY2(tt):
                sl, s3 = tt % 2, tt % 3
                tsl = slice(tt * 128, (tt + 1) * 128)
                for hf, GB in ((0, G0), (1, G1)):
                    for k in range(8):
                        tk = P.op("pe", lambda E, hf=hf, GB=GB, k=k: E.matmul(GB[:, :], lhsT=h2T[sl][:, k, :], rhs=Wg[:, k, hf * 512:(hf + 1) * 512], start=(k == 0), stop=(k == 7)),
                                  [h2e_tok[tt], sg_tok.get(tt - 1)])
                gm_tok[tt] = tk
                for hf, PB in ((0, PP0), (1, PP1)):
                    for c2 in range(2):
                        tk = P.op("pe", lambda E, hf=hf, PB=PB, c2=c2: E.matmul(PB[:, :], lhsT=pTb[:, c2, tsl], rhs=Wp[:, c2, hf * 512:(hf + 1) * 512], start=(c2 == 0), stop=(c2 == 1)),
                                  [p_tok, fo_tok.get(tt - 1)])
                pm_tok[tt] = tk
                c = 32 + 4 * (tt % 4)
                s0 = P.op("act", lambda E: E.activation(out=gate[:, 0:512], in_=G0[:, :], func=AF.Sigmoid, scale=SC(c + 3)), [gm_tok[tt], fo_tok.get(tt - 1), rc3_tok[tt]])
                sg_tok[tt] = P.op("act", lambda E: E.activation(out=gate[:, 512:1024], in_=G1[:, :], func=AF.Sigmoid, scale=SC(c + 3)), [gm_tok[tt]])
                f0 = P.op("dve", lambda E: E.tensor_tensor(out=gate[:, 0:512], in0=gate[:, 0:512], in1=PP0[:, :], op=ALU.mult), [s0, pm_tok[tt]])
                f1 = P.op("dve", lambda E: E.tensor_tensor(out=gate[:, 512:1024], in0=gate[:, 512:1024], in1=PP1[:, :], op=ALU.mult), [sg_tok[tt], pm_tok[tt]])
                fo_tok[tt] = P.op("dve", lambda E: E.tensor_tensor(out=osb[sl][:, :], in0=gate[:, :], in1=x1[s3][:, :], op=ALU.add), [f0, f1, od_tok.get(tt - 2), hs3_tok[tt]])
                od_tok[tt] = P.dma("sp", DMA(out_d[tsl, :], osb[sl][:, :]), osem[sl], [fo_tok[tt]])
                fin.append(od_tok[tt])

            load3(0)
            load3(1)
            for step in range(16 + 2):
                if step < 16:
                    stX(step)
                if 1 <= step <= 16:
                    stY1(step - 1)
                if step >= 2:
                    stY2(step - 2)

        P.wait("sp", fin)
        with nc.Block() as block:
            P.replay(block)
    return nc


def _alibi(n):
    return np.exp2(-8.0 * np.arange(1, n + 1, dtype=np.float64) / n).astype(np.float32)


def _const_tables():
    p = np.arange(128)[:, None]
    n = np.arange(1152)[None, :]
    tabDd = np.abs(n - p - 512).astype(np.float32)
    n = np.arange(2944)[None, :]
    dl = n - p - 1408
    ad = np.abs(dl)
    tabDl = ad.astype(np.float32)
    mult = (ad <= 64).astype(np.float32) + ((dl % 4 == 0) & (ad <= 256)).astype(np.float32) + ((dl % 16 == 0) & (ad <= 1024)).astype(np.float32)
    mcol = np.broadcast_to((128.0 * np.arange(64, dtype=np.float32))[None, :], (128, 64)).copy()
    ident = np.eye(128, dtype=np.float32).astype(ml_dtypes.bfloat16)
    return tabDd, tabDl, mult.astype(np.float32), mcol, ident


def make_in_maps(x, p, mix_norm_g, w_in, diff_q_norm_g, diff_k_norm_g, lambda_q1, lambda_k1, lambda_q2, lambda_k2,
                 diff_sub_norm_g, dil_q_norm_g, dil_k_norm_g, w_out, ple_norm_g, w_ple_gate, w_ple_proj):
    f = lambda a: np.ascontiguousarray(np.asarray(a, dtype=np.float32))
    x, p = f(x), f(p)
    w_in0, w_out0, w_pg0, w_pp0 = f(w_in)[0], f(w_out)[0], f(w_ple_gate)[0], f(w_ple_proj)[0]
    tabDd, tabDl, tabMl, mcol, ident = _const_tables()
    sl_diff, sl_dil = _alibi(4), _alibi(8)
    bc = lambda v, n=128: np.ascontiguousarray(np.broadcast_to(np.asarray(v, np.float32)[None, :], (n, len(v))))
    gmix = np.ascontiguousarray(f(mix_norm_g)[0].reshape(8, 128).T)
    gple = np.ascontiguousarray(f(ple_norm_g)[0].reshape(8, 128).T)
    dq, dk, bq, bk = f(diff_q_norm_g)[0], f(diff_k_norm_g)[0], f(dil_q_norm_g)[0], f(dil_k_norm_g)[0]
    gqk = bc(np.concatenate([dq, dq, dk, dk, bq, bq, bk, bk]))
    lamv = bc(np.concatenate([f(lambda_q1)[0], f(lambda_k1)[0], f(lambda_q2)[0], f(lambda_k2)[0]]))
    gsub = np.ascontiguousarray(f(diff_sub_norm_g)[0][:, None])
    perm = np.concatenate([np.arange(pt * 512 + r * 128, pt * 512 + (r + 1) * 128) for r in range(4) for pt in range(2)])
    w_out_p = np.ascontiguousarray(w_out0[perm])
    maps = []
    for c in range(8):
        b, g = divmod(c, 4)
        cols = np.concatenate([np.arange(o + g * 128, o + (g + 1) * 128) for o in (0, 512, 1536, 2048, 1024, 2560, 3072, 3584)])
        nsl = bc(np.array([-sl_diff[g], -sl_dil[2 * g], -sl_dil[2 * g + 1]], np.float32))
        idx = (g * 1024 + np.arange(8)[None, :] * 128 + np.arange(128)[:, None]).astype(np.int32)
        maps.append({
            "xb": x[b], "xo": np.ascontiguousarray(x[b, g * 2048:(g + 1) * 2048]),
            "pT": np.ascontiguousarray(p[0, b, g * 2048:(g + 1) * 2048].T),
            "w_in": np.ascontiguousarray(w_in0[:, cols]), "w_out": w_out_p, "w_pg": w_pg0, "w_pp": w_pp0,
            "gmix": gmix, "gple": gple, "gqk": gqk, "lamv": lamv, "gsub": gsub, "nsl": nsl,
            "tabDd": tabDd, "tabDl": tabDl, "tabMl": tabMl, "mcol": mcol, "ident": ident, "idx": idx,
        })
    return maps


def kernel(**inputs):
    maps = make_in_maps(**inputs)
    nc = build_program(_DEBUG)
    res = run_bass_kernel_spmd(nc, maps, core_ids=list(range(8)))
    if _DEBUG.get("stop") or _DEBUG.get("dump_x1"):
        return res
    out = np.empty((2, S_LEN, DM), np.float32)
    for c in range(8):
        b, g = divmod(c, 4)
        out[b, g * 2048:(g + 1) * 2048] = np.asarray(res.results[c]["out"], dtype=np.float32)
    return out
```
